# Optimizing a Trainium2 kernel written in Bass

```python
import math
import jax, jax.numpy as jnp
from jax import lax
import numpy as np

D_MODEL = 2048
BATCH = 4
SEQ = 2048
DEPTH = 1
DEC_BATCH = 128
DEC_SEQ = 4
PAST_LEN = 2048
PAGE_SIZE = 128

D_SSM = 1024
SSM_GROUP = 16
N_GROUPS = D_SSM // SSM_GROUP
SSM_STATE = 64
D_ATT = 1024
HEAD_DIM = 64
N_HEADS = D_ATT // (2 * HEAD_DIM)
D_FF = 4 * D_MODEL
N_IN = D_SSM + 3 * D_ATT + 2 * D_MODEL
BLOCK_Q = 128
LN_EPS = 1e-5
DEEPNORM_ALPHA = (2 * DEPTH) ** 0.25
DEEPNORM_BETA = (8 * DEPTH) ** -0.25

kernel_name = 'hybrid_s5_diffattn_alibi_deepnorm_step'


def layer_norm(x, g=None, b=None):
    xf = x.astype(jnp.float32)
    mu = jnp.mean(xf, axis=-1, keepdims=True)
    xc = xf - mu
    var = jnp.mean(xc * xc, axis=-1, keepdims=True)
    y = xc * lax.rsqrt(var + LN_EPS)
    if g is not None:
        y = y * g.astype(jnp.float32) + b.astype(jnp.float32)
    return y.astype(x.dtype)


def rms_norm(x, g):
    xf = x.astype(jnp.float32)
    y = xf * lax.rsqrt(jnp.mean(xf * xf, axis=-1, keepdims=True) + LN_EPS)
    return (y * g.astype(jnp.float32)).astype(x.dtype)


def alibi_slopes():
    return jnp.asarray(np.array([2.0 ** (-8.0 * (h + 1) / N_HEADS) for h in range(N_HEADS)], dtype=np.float32))


def _complex_affine_combine(e1, e2):
    a1r, a1i, b1r, b1i = e1
    a2r, a2i, b2r, b2i = e2
    return (a2r * a1r - a2i * a1i,
            a2r * a1i + a2i * a1r,
            a2r * b1r - a2i * b1i + b2r,
            a2r * b1i + a2i * b1r + b2i)


def s5_branch(u, s0_re, s0_im, a_re, a_im, log_dt, b_re, b_im, c_re, c_im, d_skip, w_glu):
    f32 = jnp.float32
    bsz, seq_len, _ = u.shape
    ug = u.reshape(bsz, seq_len, N_GROUPS, SSM_GROUP).astype(f32)
    a_re = a_re.astype(f32)
    a_im = a_im.astype(f32)
    dt = jnp.exp(log_dt.astype(f32))[:, None]
    mag = jnp.exp(dt * a_re)
    ab_re = mag * jnp.cos(dt * a_im)
    ab_im = mag * jnp.sin(dt * a_im)
    den = a_re * a_re + a_im * a_im
    f_re = ((ab_re - 1.0) * a_re + ab_im * a_im) / den
    f_im = (ab_im * a_re - (ab_re - 1.0) * a_im) / den
    br = b_re.astype(f32)
    bi = b_im.astype(f32)
    bb_re = f_re[..., None] * br - f_im[..., None] * bi
    bb_im = f_re[..., None] * bi + f_im[..., None] * br
    bu_re = jnp.einsum('gpc,blgc->blgp', bb_re, ug)
    bu_im = jnp.einsum('gpc,blgc->blgp', bb_im, ug)
    s0_re = s0_re.astype(f32)
    s0_im = s0_im.astype(f32)
    bu_re = bu_re.at[:, 0].add(ab_re * s0_re - ab_im * s0_im)
    bu_im = bu_im.at[:, 0].add(ab_re * s0_im + ab_im * s0_re)
    shape = bu_re.shape
    _, _, s_re, s_im = lax.associative_scan(
        _complex_affine_combine,
        (jnp.broadcast_to(ab_re, shape), jnp.broadcast_to(ab_im, shape), bu_re, bu_im),
        axis=1)
    y = (jnp.einsum('gcp,blgp->blgc', c_re.astype(f32), s_re)
         - jnp.einsum('gcp,blgp->blgc', c_im.astype(f32), s_im)
         + d_skip.reshape(N_GROUPS, SSM_GROUP).astype(f32) * ug)
    z = jax.nn.gelu(y.reshape(bsz, seq_len, D_SSM)).astype(u.dtype)
    out = z * jax.nn.sigmoid(z @ w_glu)
    return out, s_re[:, -1], s_im[:, -1]


def diff_attn_block(q1, q2, q_pos, k1, k2, v, k_pos, lam):
    scale = HEAD_DIM ** -0.5
    dist = (q_pos[:, None] - k_pos[None, :]).astype(jnp.float32)
    bias = jnp.where(dist[None] >= 0, -alibi_slopes()[:, None, None] * dist[None], -jnp.inf)
    s1 = jnp.einsum('bqhd,bkhd->bhqk', q1, k1).astype(jnp.float32) * scale + bias
    s2 = jnp.einsum('bqhd,bkhd->bhqk', q2, k2).astype(jnp.float32) * scale + bias
    p = jax.nn.softmax(s1, axis=-1) - lam * jax.nn.softmax(s2, axis=-1)
    return jnp.einsum('bhqk,bkhd->bqhd', p.astype(v.dtype), v)


def diff_attention(q1, q2, q_pos, k1, k2, v, k_pos, lam):
    bsz, tq = q1.shape[:2]
    if tq <= BLOCK_Q or tq % BLOCK_Q != 0:
        return diff_attn_block(q1, q2, q_pos, k1, k2, v, k_pos, lam)
    nb = tq // BLOCK_Q

    def blk(t):
        return jnp.moveaxis(t.reshape(bsz, nb, BLOCK_Q, *t.shape[2:]), 1, 0)

    out = lax.map(lambda xs: diff_attn_block(xs[0], xs[1], xs[2], k1, k2, v, k_pos, lam),
                  (blk(q1), blk(q2), q_pos.reshape(nb, BLOCK_Q)))
    return jnp.moveaxis(out, 0, 1).reshape(bsz, tq, N_HEADS, 2 * HEAD_DIM)


def trunk(x, c, s_re, s_im, cache_k, cache_v, page_table, weights):
    (w_ada, b_ada, w_in, ssm_a_re, ssm_a_im, ssm_log_dt, ssm_b_re, ssm_b_im, ssm_c_re, ssm_c_im,
     ssm_d, w_glu, w_up_ssm, lam_q1, lam_k1, lam_q2, lam_k2, subln_g, w_up_att, w_o,
     ln1_g, ln1_b, w_ff1, w_ff2, ln2_g, ln2_b) = weights
    bsz, seq_len, _ = x.shape
    splits = [D_SSM, D_SSM + D_ATT, D_SSM + 2 * D_ATT, D_SSM + 3 * D_ATT, D_SSM + 3 * D_ATT + D_MODEL]
    new_k, new_v, new_sr, new_si = [], [], [], []
    for l in range(DEPTH):
        lambda_init = 0.8 - 0.6 * math.exp(-0.3 * l)
        mod = jax.nn.silu(c) @ w_ada[l] + b_ada[l]
        sh1, sc1, g1, sh2, sc2, g2 = jnp.split(mod[:, None, :], 6, axis=-1)
        h = layer_norm(x) * (1.0 + sc1) + sh1
        proj = h @ w_in[l]
        u, q, k, v, gs, ga = jnp.split(proj, splits, axis=-1)
        if s_re is None:
            s0_re = jnp.zeros((bsz, N_GROUPS, SSM_STATE), jnp.float32)
            s0_im = jnp.zeros((bsz, N_GROUPS, SSM_STATE), jnp.float32)
        else:
            s0_re = s_re[l]
            s0_im = s_im[l]
        y_ssm, sr, si = s5_branch(u, s0_re, s0_im, ssm_a_re[l], ssm_a_im[l], ssm_log_dt[l],
                                  ssm_b_re[l], ssm_b_im[l], ssm_c_re[l], ssm_c_im[l], ssm_d[l], w_glu[l])
        y_ssm = y_ssm @ w_up_ssm[l]
        qh = q.reshape(bsz, seq_len, N_HEADS, 2, HEAD_DIM)
        q1 = qh[..., 0, :]
        q2 = qh[..., 1, :]
        k_rows = k.reshape(bsz, seq_len, N_HEADS, 2 * HEAD_DIM)
        v_rows = v.reshape(bsz, seq_len, N_HEADS, 2 * HEAD_DIM)
        lam = (jnp.exp(jnp.sum(lam_q1[l].astype(jnp.float32) * lam_k1[l].astype(jnp.float32)))
               - jnp.exp(jnp.sum(lam_q2[l].astype(jnp.float32) * lam_k2[l].astype(jnp.float32)))
               + lambda_init)
        if cache_k is None:
            past_len = 0
            k_all = k_rows
            v_all = v_rows
        else:
            kp = cache_k[l, page_table]
            vp = cache_v[l, page_table]
            past_len = kp.shape[1] * kp.shape[2]
            kp = kp.reshape(bsz, past_len, N_HEADS, 2 * HEAD_DIM)
            vp = vp.reshape(bsz, past_len, N_HEADS, 2 * HEAD_DIM)
            k_all = jnp.concatenate([kp, k_rows.astype(kp.dtype)], axis=1)
            v_all = jnp.concatenate([vp, v_rows.astype(vp.dtype)], axis=1)
        q_pos = past_len + jnp.arange(seq_len, dtype=jnp.int32)
        k_pos = jnp.arange(past_len + seq_len, dtype=jnp.int32)
        o = diff_attention(q1, q2, q_pos, k_all[..., :HEAD_DIM], k_all[..., HEAD_DIM:], v_all, k_pos, lam)
        o = rms_norm(o, subln_g[l]) * (1.0 - lambda_init)
        y_att = o.reshape(bsz, seq_len, D_ATT).astype(x.dtype) @ w_up_att[l]
        mix = (jax.nn.sigmoid(gs) * y_ssm + jax.nn.sigmoid(ga) * y_att) @ w_o[l]
        x = layer_norm(DEEPNORM_ALPHA * x + g1 * mix, ln1_g[l], ln1_b[l])
        h2 = layer_norm(x) * (1.0 + sc2) + sh2
        ff = jnp.square(jax.nn.relu(h2 @ w_ff1[l])) @ w_ff2[l]
        x = layer_norm(DEEPNORM_ALPHA * x + g2 * ff, ln2_g[l], ln2_b[l])
        new_k.append(k_rows)
        new_v.append(v_rows)
        new_sr.append(sr)
        new_si.append(si)
    return x, jnp.stack(new_k), jnp.stack(new_v), jnp.stack(new_sr), jnp.stack(new_si)


def setup_inputs(seed: int = 0) -> dict:
    key = jax.random.key(seed)
    ks = jax.random.split(key, 40)
    f32 = jnp.float32
    L = DEPTH

    def nrm(k, shape, scale):
        return jax.random.normal(k, shape, f32) * scale

    n_pages = PAST_LEN // PAGE_SIZE
    n_used = DEC_BATCH * n_pages
    n_pool = n_used + n_used // 4
    page_table = jax.random.permutation(ks[8], n_pool)[:n_used].reshape(DEC_BATCH, n_pages).astype(jnp.int32)
    col_scale = jnp.concatenate([jnp.ones((D_SSM + 2 * D_ATT,), f32),
                                 jnp.full((D_ATT,), DEEPNORM_BETA, f32),
                                 jnp.ones((2 * D_MODEL,), f32)])
    return {
        'x_prompt': nrm(ks[0], (BATCH, SEQ, D_MODEL), 1.0),
        'x_sample': nrm(ks[1], (DEC_BATCH, DEC_SEQ, D_MODEL), 1.0),
        'c_prompt': nrm(ks[2], (BATCH, D_MODEL), 1.0),
        'c_sample': nrm(ks[3], (DEC_BATCH, D_MODEL), 1.0),
        'cache_k': nrm(ks[4], (L, n_pool, PAGE_SIZE, N_HEADS, 2 * HEAD_DIM), 1.0),
        'cache_v': nrm(ks[5], (L, n_pool, PAGE_SIZE, N_HEADS, 2 * HEAD_DIM), 1.0),
        'state_ssm_re': nrm(ks[6], (L, DEC_BATCH, N_GROUPS, SSM_STATE), 0.5),
        'state_ssm_im': nrm(ks[7], (L, DEC_BATCH, N_GROUPS, SSM_STATE), 0.5),
        'page_table': page_table,
        'w_ada': nrm(ks[9], (L, D_MODEL, 6 * D_MODEL), 0.5 * D_MODEL ** -0.5),
        'b_ada': nrm(ks[10], (L, 6 * D_MODEL), 0.02),
        'w_in': nrm(ks[11], (L, D_MODEL, N_IN), D_MODEL ** -0.5) * col_scale,
        'ssm_a_re': -0.5 + nrm(ks[12], (L, N_GROUPS, SSM_STATE), 0.01),
        'ssm_a_im': jnp.pi * jnp.arange(SSM_STATE, dtype=f32)[None, None, :] + nrm(ks[13], (L, N_GROUPS, SSM_STATE), 0.01),
        'ssm_log_dt': jax.random.uniform(ks[14], (L, N_GROUPS), f32, math.log(1e-3), math.log(1e-1)),
        'ssm_b_re': nrm(ks[15], (L, N_GROUPS, SSM_STATE, SSM_GROUP), (2 * SSM_GROUP) ** -0.5),
        'ssm_b_im': nrm(ks[16], (L, N_GROUPS, SSM_STATE, SSM_GROUP), (2 * SSM_GROUP) ** -0.5),
        'ssm_c_re': nrm(ks[17], (L, N_GROUPS, SSM_GROUP, SSM_STATE), SSM_STATE ** -0.5),
        'ssm_c_im': nrm(ks[18], (L, N_GROUPS, SSM_GROUP, SSM_STATE), SSM_STATE ** -0.5),
        'ssm_d': nrm(ks[19], (L, D_SSM), 0.5),
        'w_glu': nrm(ks[20], (L, D_SSM, D_SSM), D_SSM ** -0.5),
        'w_up_ssm': nrm(ks[21], (L, D_SSM, D_MODEL), DEEPNORM_BETA * D_SSM ** -0.5),
        'lam_q1': nrm(ks[22], (L, HEAD_DIM), 0.1),
        'lam_k1': nrm(ks[23], (L, HEAD_DIM), 0.1),
        'lam_q2': nrm(ks[24], (L, HEAD_DIM), 0.1),
        'lam_k2': nrm(ks[25], (L, HEAD_DIM), 0.1),
        'subln_g': 1.0 + nrm(ks[26], (L, 2 * HEAD_DIM), 0.02),
        'w_up_att': nrm(ks[27], (L, D_ATT, D_MODEL), DEEPNORM_BETA * D_ATT ** -0.5),
        'w_o': nrm(ks[28], (L, D_MODEL, D_MODEL), DEEPNORM_BETA * D_MODEL ** -0.5),
        'ln1_g': 1.0 + nrm(ks[29], (L, D_MODEL), 0.02),
        'ln1_b': nrm(ks[30], (L, D_MODEL), 0.02),
        'w_ff1': nrm(ks[31], (L, D_MODEL, D_FF), DEEPNORM_BETA * D_MODEL ** -0.5),
        'w_ff2': nrm(ks[32], (L, D_FF, D_MODEL), DEEPNORM_BETA * D_FF ** -0.5),
        'ln2_g': 1.0 + nrm(ks[33], (L, D_MODEL), 0.02),
        'ln2_b': nrm(ks[34], (L, D_MODEL), 0.02),
    }


def reference(x_prompt, x_sample, c_prompt, c_sample, cache_k, cache_v, state_ssm_re, state_ssm_im,
              page_table, w_ada, b_ada, w_in, ssm_a_re, ssm_a_im, ssm_log_dt, ssm_b_re, ssm_b_im,
              ssm_c_re, ssm_c_im, ssm_d, w_glu, w_up_ssm, lam_q1, lam_k1, lam_q2, lam_k2, subln_g,
              w_up_att, w_o, ln1_g, ln1_b, w_ff1, w_ff2, ln2_g, ln2_b):
    weights = (w_ada, b_ada, w_in, ssm_a_re, ssm_a_im, ssm_log_dt, ssm_b_re, ssm_b_im, ssm_c_re, ssm_c_im,
               ssm_d, w_glu, w_up_ssm, lam_q1, lam_k1, lam_q2, lam_k2, subln_g, w_up_att, w_o,
               ln1_g, ln1_b, w_ff1, w_ff2, ln2_g, ln2_b)
    y_prompt, k_prompt, v_prompt, ssm_re_prompt, ssm_im_prompt = trunk(
        x_prompt, c_prompt, None, None, None, None, None, weights)
    y_sample, k_sample, v_sample, ssm_re_sample, ssm_im_sample = trunk(
        x_sample, c_sample, state_ssm_re, state_ssm_im, cache_k, cache_v, page_table, weights)
    return (y_prompt, y_sample, k_prompt, v_prompt, ssm_re_prompt, ssm_im_prompt,
            k_sample, v_sample, ssm_re_sample, ssm_im_sample)
```

```python
import numpy as np
import ml_dtypes
from contextlib import ExitStack
import concourse.bass as bass
import concourse.mybir as mybir
from concourse.alu_op_type import AluOpType as ALU
from concourse.bass_utils import run_bass_kernel_spmd

F32 = mybir.dt.float32
BF16 = mybir.dt.bfloat16
I32 = mybir.dt.int32
U32 = mybir.dt.uint32
AF = mybir.ActivationFunctionType
AX = mybir.AxisListType
NPBF = ml_dtypes.bfloat16

D = 2048
NTOK = 2112
NOWN = 1088
LN_EPS = 1e-5
ALPHA = 2.0 ** 0.25
ENGS = ("sync", "gpsimd", "scalar", "vector", "tensor")
NDMASEM = 8
SAME_ENG_SYNC = True


class U:
    __slots__ = ("name", "w", "r")

    def __init__(self, name=""):
        self.name = name
        self.w = None
        self.r = []


class Op:
    __slots__ = ("eng", "fn", "deps", "dma", "sig", "sem", "val", "idx", "throttle")

    def __init__(self, eng, fn, dma):
        self.eng = eng
        self.fn = fn
        self.dma = dma
        self.deps = set()
        self.sig = False
        self.sem = None
        self.val = None
        self.throttle = None


class Prog:
    def __init__(self, nc):
        self.nc = nc
        self.ops = []

    def op(self, eng, fn, reads=(), writes=(), dma=False):
        o = Op(eng, fn, dma)
        o.idx = len(self.ops)
        for u in reads:
            if u.w is not None:
                o.deps.add(u.w)
        for u in writes:
            if u.w is not None:
                o.deps.add(u.w)
            for r in u.r:
                o.deps.add(r)
        for u in reads:
            u.r.append(o.idx)
        for u in writes:
            u.w = o.idx
            u.r = []
        o.deps.discard(o.idx)
        self.ops.append(o)
        return o

    def emit(self, stack):
        nc = self.nc
        ops = self.ops
        for o in ops:
            if o.dma:
                o.sig = True
        for o in ops:
            for d in o.deps:
                p = ops[d]
                if p.dma:
                    continue
                if p.eng == o.eng and (p.eng == "tensor" or not SAME_ENG_SYNC):
                    continue
                p.sig = True
        esem = {e: stack.enter_context(nc.semaphore("s_" + e)) for e in ENGS}
        dsem = {e: [stack.enter_context(nc.semaphore("d_%s%d" % (e, i))) for i in range(NDMASEM)]
                for e in ("sync", "gpsimd")}
        ecnt = {e: 0 for e in ENGS}
        dcnt = {e: 0 for e in ("sync", "gpsimd")}
        dhist = {e: [] for e in ("sync", "gpsimd")}
        for o in ops:
            if not o.sig:
                continue
            if o.dma:
                n = dcnt[o.eng]
                dcnt[o.eng] += 1
                o.sem = ("d", o.eng, n % NDMASEM)
                o.val = 16 * (n // NDMASEM + 1)
                if n >= NDMASEM:
                    o.throttle = dhist[o.eng][n - NDMASEM]
                dhist[o.eng].append(o.idx)
            else:
                ecnt[o.eng] += 1
                o.sem = ("e", o.eng, 0)
                o.val = ecnt[o.eng]

        def semh(key):
            return esem[key[1]] if key[0] == "e" else dsem[key[1]][key[2]]

        per_eng = {e: [o for o in ops if o.eng == e] for e in ENGS}
        final_waits = {}
        for e in ("sync", "gpsimd"):
            last = {}
            for o in per_eng[e]:
                if o.dma:
                    last[o.sem] = o.val
            final_waits[e] = last

        def run_engine(ename, h):
            seen = {}
            for o in per_eng[ename]:
                need = {}
                dl = list(o.deps)
                if o.throttle is not None:
                    dl.append(o.throttle)
                for d in dl:
                    p = ops[d]
                    if not p.sig:
                        continue
                    if (not p.dma) and p.eng == ename and (ename == "tensor" or not SAME_ENG_SYNC):
                        continue
                    if need.get(p.sem, 0) < p.val:
                        need[p.sem] = p.val
                for k, v in need.items():
                    if seen.get(k, 0) < v:
                        h.wait_ge(semh(k), v)
                        seen[k] = v
                inst = o.fn(h)
                if o.sig:
                    inst.then_inc(semh(o.sem), 16 if o.dma else 1)
            for k, v in final_waits.get(ename, {}).items():
                if seen.get(k, 0) < v:
                    h.wait_ge(semh(k), v)

        block = stack.enter_context(nc.Block())

        @block.sync
        def _(h):
            run_engine("sync", h)

        @block.gpsimd
        def _(h):
            run_engine("gpsimd", h)

        @block.scalar
        def _(h):
            run_engine("scalar", h)

        @block.vector
        def _(h):
            run_engine("vector", h)

        @block.tensor
        def _(h):
            run_engine("tensor", h)


class Arena:
    def __init__(self, bld, name, nbytes):
        self.t = bld.st.enter_context(bld.nc.sbuf_tensor("ar_" + name, [128, nbytes // 4], F32))
        self.nbytes = nbytes
        self.off = 0
        self.cur = []
        self.prev = []
        self.name = name

    def reset(self, keep=0):
        self.prev = self.cur + self.prev
        self.cur = []
        self.off = (keep + 31) // 32 * 32

    def alloc(self, shape, dt, name=""):
        esz = 4 if dt in (F32, I32, U32) else 2
        n = 1
        for s_ in shape[1:]:
            n *= s_
        nb = (n * esz + 31) // 32 * 32
        assert self.off + nb <= self.nbytes, (self.name, name, self.off, nb, self.nbytes)
        v = self.t[0:shape[0], self.off // 4:(self.off + nb) // 4]
        if esz == 2:
            v = v.bitcast(dt)
        elif dt != F32:
            v = v.bitcast(dt)
        v = v[:, 0:n]
        if len(shape) == 3:
            v = v.rearrange("p (a b) -> p a b", a=shape[1])
        elif len(shape) == 4:
            v = v.rearrange("p (a b c) -> p a b c", a=shape[1], b=shape[2])
        elif len(shape) == 5:
            v = v.rearrange("p (a b c d) -> p a b c d", a=shape[1], b=shape[2], c=shape[3])
        self.off += nb
        u = U(self.name + ":" + name)
        for pu in self.prev:
            if pu.w is not None:
                u.r.append(pu.w)
            u.r.extend(pu.r)
        self.cur.append(u)
        return v, u


class Builder:
    def __init__(self, npool_rows, stage):
        self.stage = stage
        self.nc = bass.Bass("TRN2", target_bir_lowering=False)
        self.P = Prog(self.nc)
        self.st = ExitStack()
        self.npool_rows = npool_rows
        self.din = {}
        self.dout = {}
        self._evk = 0
        self._pb = 0

    def inp(self, name, shape, dt=F32):
        t = self.nc.dram_tensor(name, list(shape), dt, kind="ExternalInput").ap()
        self.din[name] = t
        return t

    def outp(self, name, shape, dt=F32):
        t = self.nc.dram_tensor(name, list(shape), dt, kind="ExternalOutput").ap()
        self.dout[name] = t
        return t

    def scratch(self, name, shape, dt=F32):
        return self.nc.dram_tensor(name, list(shape), dt, kind="Internal").ap()

    def sb(self, name, shape, dt=F32):
        return self.st.enter_context(self.nc.sbuf_tensor("sb_" + name, list(shape), dt))

    def dma(self, out, in_, reads=(), writes=(), eng="sync"):
        return self.P.op(eng, lambda h: h.dma_start(out=out, in_=in_), reads, writes, dma=True)

    def mm(self, out, lhsT, rhs, start, stop, reads, writes, **kw):
        return self.P.op("tensor", lambda h: h.matmul(out, lhsT=lhsT, rhs=rhs, start=start, stop=stop, **kw),
                         reads, writes)

    def tr(self, out, in_, ident, reads, writes):
        return self.P.op("tensor", lambda h: h.transpose(out, in_, ident), reads, writes)

    def act(self, out, in_, func, reads, writes, **kw):
        return self.P.op("scalar", lambda h: h.activation(out=out, in_=in_, func=func, **kw), reads, writes)

    def tt(self, out, in0, in1, op, reads, writes, eng="vector"):
        return self.P.op(eng, lambda h: h.tensor_tensor(out=out, in0=in0, in1=in1, op=op), reads, writes)

    def ts(self, out, in0, s1, s2, op0, op1, reads, writes, eng="vector"):
        if op1 is None:
            return self.P.op(eng, lambda h: h.tensor_scalar(out=out, in0=in0, scalar1=s1, scalar2=None, op0=op0),
                             reads, writes)
        return self.P.op(eng, lambda h: h.tensor_scalar(out=out, in0=in0, scalar1=s1, scalar2=s2, op0=op0, op1=op1),
                         reads, writes)

    def stt(self, out, in0, scalar, in1, op0, op1, reads, writes):
        return self.P.op("vector", lambda h: h.scalar_tensor_tensor(out=out, in0=in0, scalar=scalar, in1=in1,
                                                                    op0=op0, op1=op1), reads, writes)

    def cp(self, out, in_, reads, writes, eng="vector"):
        if eng == "scalar":
            return self.P.op(eng, lambda h: h.copy(out=out, in_=in_), reads, writes)
        return self.P.op(eng, lambda h: h.tensor_copy(out=out, in_=in_), reads, writes)

    def evac_eng(self):
        self._evk += 1
        return "scalar" if (self._evk & 1) else "vector"

    def bank(self, n=6):
        self._pb = (self._pb + 1) % n
        return self._pb

    def memset(self, ap, val, writes, eng="vector"):
        return self.P.op(eng, lambda h: h.memset(ap, val), (), writes)

    def cmul(self, or_, oi, ar, ai, br, bi, t1, t2, rd, wr, eng="vector"):
        M, A, S = ALU.mult, ALU.add, ALU.subtract
        self.tt(t1, ar, br, M, rd, wr, eng)
        self.tt(t2, ai, bi, M, rd, wr, eng)
        self.tt(or_, t1, t2, S, rd + wr, wr, eng)
        self.tt(t1, ar, bi, M, rd + wr, wr, eng)
        self.tt(t2, ai, br, M, rd + wr, wr, eng)
        self.tt(oi, t1, t2, A, rd + wr, wr, eng)

    def build(self):
        self.setup()
        self.phase_ada()
        self.phase_ssm_gen()
        self.phase1()
        if self.stage >= 2:
            self.phase_ssm_y()
        if self.stage >= 3:
            self.phase_attn()
        if self.stage >= 4:
            self.phase_rest()
        self.finish()

    def setup(self):
        nc = self.nc
        inp, outp, sb = self.inp, self.outp, self.sb
        self.xa = inp("xa", [NTOK, D])
        self.call = inp("call", [65, D])
        self.w_ada = inp("w_ada", [D, 6 * D])
        self.b_ada = inp("b_ada", [96, 128])
        self.w_in = inp("w_in", [D, 8192])
        self.w_glu = inp("w_glu", [1024, 1024])
        self.w_up_ssm = inp("w_up_ssm", [1024, D])
        self.w_up_att = inp("w_up_att", [1024, D])
        ident_f_d = inp("ident_f", [128, 128])
        ident_b_d = inp("ident_b", [128, 128], BF16)
        self.k_own = outp("k_own", [1024, 1024])
        self.v_own = outp("v_own", [1024, 1024])
        self.k_s = outp("k_s", [64, 1024])
        self.v_s = outp("v_s", [64, 1024])
        self.KT_s = self.scratch("KT_s", [8, 128, NTOK], BF16)
        self.V_s = self.scratch("V_s", [8, NTOK, 128], BF16)
        self.GS = self.scratch("GS", [2, 192, D])
        self.WB = [sb("WB%d" % i, [128, 16, 512], BF16) for i in range(2)]
        self.uWB = [U("WB0"), U("WB1")]
        self.XS = [sb("XS%d" % i, [128, D], F32) for i in range(2)]
        self.uXS = [U("XS0"), U("XS1")]
        self.ident_f = sb("ident_f", [128, 128], F32)
        self.ident_b = sb("ident_b", [128, 128], BF16)
        self.uID = U("ident")
        self.dma(self.ident_f[:], ident_f_d, writes=[self.uID])
        self.dma(self.ident_b[:], ident_b_d, writes=[self.uID])
        self.PS = [self.st.enter_context(nc.psum_tensor("ps%d" % i, [128, 512], F32)) for i in range(8)]
        self.uPS = [U("ps%d" % i) for i in range(8)]
        self.wn = 0
        self.AR1 = Arena(self, "A1", 34816)
        self.AR2 = Arena(self, "A2", 22 * 1024)
        self.AR3 = Arena(self, "A3", 36 * 1024)
        self.AR4 = Arena(self, "A4", 42 * 1024)
        self.AR5 = Arena(self, "A5", 8704)
        self.AR6 = Arena(self, "A6", 4352)

    def wload(self, wview, k0, nkt, c0, ncols):
        i = self.wn % 2
        self.wn += 1
        src = wview[k0:k0 + 128 * nkt, c0:c0 + ncols].rearrange("(kt p) n -> p kt n", p=128)
        self.dma(self.WB[i][:, 0:nkt, 0:ncols], src, writes=[self.uWB[i]], eng="gpsimd")
        return self.WB[i], self.uWB[i]

    def proj_fm(self, wview, k0, nkt, c0, ncols, actf, uact, chunks, evac):
        wb, uwb = self.wload(wview, k0, nkt, c0, ncols)
        PS, uPS = self.PS, self.uPS
        for ms in range(ncols // 128):
            for (t0, n) in chunks:
                pb = self.bank()
                for kt in range(nkt):
                    self.mm(PS[pb][:, 0:n], wb[:, kt, ms * 128:(ms + 1) * 128], actf(kt, t0, n), kt == 0, kt == nkt - 1,
                            [uwb] + uact, [uPS[pb]])
                evac(ms, t0, n, PS[pb][:, 0:n], uPS[pb])

    def phase_ada(self):
        sb, P = self.sb, self.P
        PS, uPS, XS, uXS = self.PS, self.uPS, self.XS, self.uXS
        ident_f, uID = self.ident_f, self.uID
        cT, ucT = self.AR1.alloc([128, 16, 65], BF16, "cT")
        cTp, _ = self.AR1.alloc([128, 16, 128], BF16, "cTp")
        gb, ugb = self.AR1.alloc([128, D], F32, "gb")
        ba = sb("ba", [128, 96], F32)
        uba = U("ba")
        self.mod, self.umod = self.AR5.alloc([128, 2, 16, 65], F32, "mod")
        mod2, umod2 = self.AR3.alloc([128, 2, 16, 65], F32, "mod2")
        self.MOD2_s = self.scratch("MOD2_s", [128, 2 * 16 * 65])
        self.dma(XS[0][0:65, :], self.call, writes=[uXS[0]])
        self.act(XS[0][0:65, :], XS[0][0:65, :], AF.Silu, [uXS[0]], [uXS[0]])
        for kt in range(16):
            b = kt % 2
            self.tr(PS[b][:, 0:65], XS[0][0:65, kt * 128:(kt + 1) * 128], ident_f[0:65, 0:65],
                    [uXS[0], uID], [uPS[b]])
            self.cp(cT[:, kt, :], PS[b][:, 0:65], [uPS[b]], [ucT], eng=self.evac_eng())
        self.cp(cTp[:], cT[:, :, 0:1].to_broadcast([128, 16, 128]), [ucT], [ucT], eng="vector")
        self.dma(XS[1][0:96, 0:128], self.b_ada, writes=[uXS[1]])
        self.tr(PS[2][:, 0:96], XS[1][0:96, 0:128], ident_f[0:96, 0:96], [uXS[1], uID], [uPS[2]])
        self.cp(ba[:], PS[2][:, 0:96], [uPS[2]], [uba])
        b_ada_flat = self.b_ada.rearrange("a b -> (a b)")
        for cb in range(24):
            kind = cb // 4
            wb, uwb = self.wload(self.w_ada, 0, 16, cb * 512, 512)
            if kind in (0, 1, 3, 4):
                mk = {0: 0, 1: 1, 3: 2, 4: 3}[kind]
                for ms in range(4):
                    tile_i = (cb % 4) * 4 + ms
                    pb = 3 + (ms % 2)
                    for kt in range(16):
                        self.mm(PS[pb][:, 0:65], wb[:, kt, ms * 128:(ms + 1) * 128], cT[:, kt, :], kt == 0, kt == 15,
                                [uwb, ucT], [uPS[pb]])
                    col = cb * 4 + ms
                    md, umd = (self.mod, self.umod) if mk < 2 else (mod2, umod2)
                    self.ts(md[:, mk % 2, tile_i, :], PS[pb][:, 0:65], ba[:, col:col + 1],
                            1.0 if kind in (1, 4) else 0.0, ALU.add, ALU.add, [uPS[pb], uba], [umd])
            else:
                gi = 0 if kind == 2 else 1
                if cb % 4 == 0:
                    src = b_ada_flat[kind * D:(kind + 1) * D].partition_broadcast(128)
                    self.dma(gb[:], src, writes=[ugb])
                c0 = (cb % 4) * 512
                for kt in range(16):
                    self.mm(PS[5][:, :], cTp[:, kt, :], wb[:, kt, :], kt == 0, kt == 15, [uwb, ucT], [uPS[5]])
                self.tt(XS[0][:, c0:c0 + 512], PS[5][:, :], gb[:, c0:c0 + 512], ALU.add, [uPS[5], ugb], [uXS[0]])
                for kt in range(16):
                    self.mm(PS[6][0:64, :], cT[:, kt, 1:65], wb[:, kt, :], kt == 0, kt == 15, [uwb, ucT], [uPS[6]])
                self.tt(XS[1][0:64, c0:c0 + 512], PS[6][0:64, :], gb[0:64, c0:c0 + 512], ALU.add, [uPS[6], ugb], [uXS[1]])
                if cb % 4 == 3:
                    self.dma(self.GS[gi, 0:128, :], XS[0][:, :], reads=[uXS[0]])
                    self.dma(self.GS[gi, 128:192, :], XS[1][0:64, :], reads=[uXS[1]])
        self.dma(self.MOD2_s, mod2[:].rearrange("p a b c -> p (a b c)"), reads=[umod2])
        self.AR1.reset()
        self.AR3.reset()
    def phase_ssm_gen(self):
        sb, P, inp = self.sb, self.P, self.inp
        PS, uPS, XS, uXS = self.PS, self.uPS, self.XS, self.uXS
        ident_f, ident_b, uID = self.ident_f, self.ident_b, self.uID
        M, A, S = ALU.mult, ALU.add, ALU.subtract
        a_re_d = inp("a_re", [64, 64]); a_im_d = inp("a_im", [64, 64]); ldt_d = inp("log_dt", [64, 1])
        b_re_d = inp("b_re", [64, 1024]); b_im_d = inp("b_im", [64, 1024])
        c_re_d = inp("c_re", [64, 1024]); c_im_d = inp("c_im", [64, 1024])
        dvec_d = inp("dvec", [8, 128]); hflag_d = inp("hflag", [64, 1]); esel_d = inp("esel", [64, 256], BF16)
        self.KL_s = self.scratch("KL_s", [128, 8 * 4 * 128], BF16)
        self.CAL_s = self.scratch("CAL_s", [128, 64 * 4 * 32], BF16)
        A1, A2, A3 = self.AR1, self.AR2, self.AR3
        ug = U("gen")
        rd, wr = [ug], [ug]
        A1.cur.append(ug); A2.cur.append(ug); A3.cur.append(ug)

        def sm(name):
            v, _ = A1.alloc([64, 64], F32, name)
            return v
        are, aim, xr, xi, e_, cs, sn, Ar, Ai, t1, t2, t3 = [sm("s%d" % i) for i in range(12)]
        den, rden, m1, Fr, Fi, A128r, A128i = [sm("q%d" % i) for i in range(7)]
        Apr, _ = A1.alloc([64, 5, 64], F32, "Apr")
        Api, _ = A1.alloc([64, 5, 64], F32, "Api")
        cols, _ = A1.alloc([64, 4], F32, "cols")
        CAp, uCAp = A1.alloc([128, 64, 5, 16], BF16, "CAp")
        KL, uKL = A1.alloc([128, 8, 4, 128], BF16, "KL")
        brc, _ = A1.alloc([64, 1024], F32, "brc"); bic, _ = A1.alloc([64, 1024], F32, "bic")
        br = brc.rearrange("g (p c) -> g p c", c=16); bi = bic.rearrange("g (p c) -> g p c", c=16)
        cr = brc.rearrange("g (c p) -> g c p", c=16); ci = bic.rearrange("g (c p) -> g c p", c=16)
        Bbr, _ = A2.alloc([64, 64, 16], F32, "Bbr"); Bbi, _ = A2.alloc([64, 64, 16], F32, "Bbi")
        T1, _ = A2.alloc([64, 1024], F32, "T1"); T2, _ = A2.alloc([64, 1024], F32, "T2")
        Wg, uWg = A3.alloc([64, 4, 16, 2, 64], BF16, "Wg")
        CAg, uCAg = A3.alloc([64, 5, 16, 2, 64], BF16, "CAg")
        self.WT, self.uWT = self.AR4.alloc([128, 16, 4, 128], BF16, "WT")
        self.A4p = sb("A4p", [64, 2, 64], F32); self.A128p = sb("A128p", [64, 2, 64], F32); self.uAp = U("Ap")
        self.dcol = sb("dcol", [128, 8], F32); self.udcol = U("dcol")
        self.hf = sb("hf", [64, 1], F32); self.esel = sb("esel", [64, 2, 128], BF16); self.uhf = U("hf")
        for t_, d_ in ((are, a_re_d), (aim, a_im_d)):
            self.dma(t_, d_, writes=wr)
        self.dma(cols[:, 0:1], ldt_d, writes=wr)
        self.dma(br.rearrange("g p c -> g (p c)"), b_re_d, writes=wr)
        self.dma(bi.rearrange("g p c -> g (p c)"), b_im_d, writes=wr)
        self.dma(self.hf[:], hflag_d, writes=[self.uhf])
        self.dma(self.esel[:].rearrange("p a b -> p (a b)"), esel_d, writes=[self.uhf])
        self.memset(cols[:, 1:2], float(np.pi / 2), wr)
        self.memset(cols[:, 2:3], 0.0, wr)
        self.act(cols[:, 3:4], cols[:, 0:1], AF.Exp, rd, wr)
        self.ts(xr, are, cols[:, 3:4], None, M, None, rd, wr)
        self.ts(xi, aim, cols[:, 3:4], None, M, None, rd, wr)
        self.act(e_, xr, AF.Exp, rd, wr, scale=1.0 / 16)
        self.act(cs, xi, AF.Sin, rd, wr, scale=1.0 / 16, bias=cols[:, 1:2])
        self.act(sn, xi, AF.Sin, rd, wr, scale=1.0 / 16, bias=cols[:, 2:3])
        self.tt(Ar, e_, cs, M, rd, wr)
        self.tt(Ai, e_, sn, M, rd, wr)

        def csq(r_, i_):
            self.tt(t1, r_, r_, M, rd, wr)
            self.tt(t2, i_, i_, M, rd, wr)
            self.tt(t3, r_, i_, M, rd, wr)
            self.tt(r_, t1, t2, S, rd, wr)
            self.ts(i_, t3, 2.0, None, M, None, rd, wr)
        for _ in range(4):
            csq(Ar, Ai)
        self.memset(Apr[:, 0, :], 1.0, wr)
        self.memset(Api[:, 0, :], 0.0, wr)
        self.cp(Apr[:, 1, :], Ar, rd, wr)
        self.cp(Api[:, 1, :], Ai, rd, wr)
        self.cmul(Apr[:, 2, :], Api[:, 2, :], Ar, Ai, Ar, Ai, t1, t2, rd, wr)
        self.cmul(Apr[:, 3, :], Api[:, 3, :], Apr[:, 2, :], Api[:, 2, :], Ar, Ai, t1, t2, rd, wr)
        self.cmul(Apr[:, 4, :], Api[:, 4, :], Apr[:, 2, :], Api[:, 2, :], Apr[:, 2, :], Api[:, 2, :], t1, t2, rd, wr)
        self.cp(A128r, Apr[:, 4, :], rd, wr)
        self.cp(A128i, Api[:, 4, :], rd, wr)
        for _ in range(5):
            csq(A128r, A128i)
        self.tt(t1, are, are, M, rd, wr)
        self.tt(t2, aim, aim, M, rd, wr)
        self.tt(den, t1, t2, A, rd, wr)
        P.op("vector", lambda h: h.reciprocal(out=rden, in_=den), rd, wr)
        self.ts(m1, Ar, -1.0, None, A, None, rd, wr)
        self.tt(t1, m1, are, M, rd, wr)
        self.tt(t2, Ai, aim, M, rd, wr)
        self.tt(t1, t1, t2, A, rd, wr)
        self.tt(Fr, t1, rden, M, rd, wr)
        self.tt(t1, Ai, are, M, rd, wr)
        self.tt(t2, m1, aim, M, rd, wr)
        self.tt(t1, t1, t2, S, rd, wr)
        self.tt(Fi, t1, rden, M, rd, wr)
        T1a = T1.rearrange("g (p c) -> g p c", c=16)
        T2a = T2.rearrange("g (p c) -> g p c", c=16)
        Frb = Fr.unsqueeze(2).to_broadcast([64, 64, 16])
        Fib = Fi.unsqueeze(2).to_broadcast([64, 64, 16])
        self.cmul(Bbr, Bbi, Frb, Fib, br, bi, T1a, T2a, rd, wr)
        for i in range(4):
            k = 3 - i
            pr = Apr[:, k, :].unsqueeze(2).to_broadcast([64, 64, 16])
            pi_ = Api[:, k, :].unsqueeze(2).to_broadcast([64, 64, 16])
            o_r = Wg[:, i, :, 0, :].rearrange("g c p -> g p c")
            o_i = Wg[:, i, :, 1, :].rearrange("g c p -> g p c")
            self.cmul(o_r, o_i, pr, pi_, Bbr, Bbi, T1a, T2a, rd, wr + [uWg])
        self.dma(brc, c_re_d, reads=rd, writes=wr)
        self.dma(bic, c_im_d, reads=rd, writes=wr)
        T1c = T1.rearrange("g (c p) -> g c p", c=16)
        T2c = T2.rearrange("g (c p) -> g c p", c=16)
        for k in range(5):
            pr = Apr[:, k, :].unsqueeze(1).to_broadcast([64, 16, 64])
            pi_ = Api[:, k, :].unsqueeze(1).to_broadcast([64, 16, 64])
            self.tt(T1c, cr, pr, M, rd, wr)
            self.tt(T2c, ci, pi_, M, rd, wr)
            self.tt(CAg[:, k, :, 0, :], T1c, T2c, S, rd, wr + [uCAg])
            self.tt(T1c, cr, pi_, M, rd, wr)
            self.tt(T2c, ci, pr, M, rd, wr)
            self.stt(CAg[:, k, :, 1, :], T1c, -1.0, T2c, M, S, rd, wr + [uCAg])
        for k in range(5):
            pb = 6 + (k % 2)
            psb = PS[pb][:].bitcast(BF16)
            for c in range(16):
                self.tr(psb[:, c * 64:(c + 1) * 64], CAg[:, k, c, :, :].rearrange("g r p -> g (r p)"),
                        ident_b[0:64, 0:64], [uCAg, uID], [uPS[pb]])
            self.cp(CAp[:, :, k, :], psb[:, 0:1024].rearrange("p (c g) -> p g c", c=16), [uPS[pb]], [uCAp],
                    eng=self.evac_eng())
        A2.reset()
        Wpp, uWpp = A2.alloc([128, 4, 64, 32], BF16, "Wpp")
        self.memset(Wpp[:].rearrange("p a b c -> p (a b c)"), 0.0, [uWpp], eng="gpsimd")
        for i in range(4):
            pb = 6 + (i % 2)
            psb = PS[pb][:].bitcast(BF16)
            for c in range(16):
                self.tr(psb[:, c * 64:(c + 1) * 64], Wg[:, i, c, :, :].rearrange("g r p -> g (r p)"),
                        ident_b[0:64, 0:64], [uWg, uID], [uPS[pb]])
            self.cp(Wpp[:, i, :, 0:16], psb[:, 0:1024].rearrange("p (c g) -> p g c", c=16), [uPS[pb]], [uWpp],
                    eng=self.evac_eng())
        A3.reset()
        BpK, uBpK = A3.alloc([128, 64, 128], BF16, "BpK")
        CAL, uCAL = A3.alloc([128, 64, 4, 32], BF16, "CAL")
        self.memset(BpK[:].rearrange("p a b -> p (a b)"), 0.0, [uBpK], eng="gpsimd")
        BpKv = BpK.rearrange("p (t g8) (blk c) -> p t g8 blk c", g8=8, c=16)
        Wp3 = Wpp[:, 3, :, :].rearrange("p (t g8) c -> p t g8 c", g8=8)
        for g8 in range(8):
            self.cp(BpKv[:, :, g8, g8, :], Wp3[:, :, g8, 0:16], [uWpp], [uBpK], eng=("vector" if g8 % 2 else "gpsimd"))
        for kk in range(8):
            pb = 6 + (kk % 2)
            psb = PS[pb][:].bitcast(BF16)
            j = 0
            for t in (2 * kk, 2 * kk + 1):
                for i in range(4):
                    self.tr(psb[:, j * 128:(j + 1) * 128], Wpp[:, i, 4 * t:4 * t + 4, :].rearrange("p a b -> p (a b)"),
                            ident_b[:, :], [uWpp, uID], [uPS[pb]])
                    j += 1
            self.cp(self.WT[:, 2 * kk:2 * kk + 2, :, :].rearrange("p a b c -> p (a b c)"), psb[:, 0:1024],
                    [uPS[pb]], [self.uWT], eng=self.evac_eng())
        for t8 in range(8):
            pb = self.bank()
            pv = PS[pb][:, :].rearrange("p (l c) -> p l c", l=4)
            for g8 in range(8):
                g = 8 * t8 + g8
                self.mm(pv[:, :, 16 * g8:16 * g8 + 16], BpK[:, g, :], CAp[:, g, 0:4, :], True, True,
                        [uBpK, uCAp], [uPS[pb]])
            self.cp(KL[:, t8, :, :].rearrange("p l c -> p (l c)"), PS[pb][:, :], [uPS[pb]], [uKL], eng=self.evac_eng())
        self.dma(self.KL_s, KL[:].rearrange("p a b c -> p (a b c)"), reads=[uKL])
        self.memset(CAL[:].rearrange("p a b c -> p (a b c)"), 0.0, [uCAL], eng="gpsimd")
        CALv = CAL.rearrange("p (gp two) j c -> p gp two j c", two=2)
        CApv = CAp.rearrange("p (gp two) k c -> p gp two k c", two=2)
        self.cp(CALv[:, :, 0, :, 0:16], CApv[:, :, 0, 1:5, :], [uCAp], [uCAL], eng="vector")
        self.cp(CALv[:, :, 1, :, 16:32], CApv[:, :, 1, 1:5, :], [uCAp], [uCAL], eng="gpsimd")
        self.dma(self.CAL_s, CAL[:].rearrange("p a b c -> p (a b c)"), reads=[uCAL])
        for idx, (src_r, src_i, dst) in enumerate(((Apr[:, 4, :], Api[:, 4, :], self.A4p), (A128r, A128i, self.A128p))):
            for ri, src in enumerate((src_r, src_i)):
                pb = self.bank()
                self.tr(PS[pb][0:64, 0:64], src, ident_f[0:64, 0:64], rd + [uID], [uPS[pb]])
                self.cp(dst[:, ri, :], PS[pb][0:64, 0:64], [uPS[pb]], [self.uAp], eng=self.evac_eng())
        self.dma(XS[1][0:8, 0:128], dvec_d, writes=[uXS[1]])
        pb = self.bank()
        self.tr(PS[pb][:, 0:8], XS[1][0:8, 0:128], ident_f[0:8, 0:8], [uXS[1], uID], [uPS[pb]])
        self.cp(self.dcol[:], PS[pb][:, 0:8], [uPS[pb]], [self.udcol])
        A1.reset(); A2.reset(); A3.reset()
    def ln_block(self, src_rows, nrows, par, mk_sh, mk_sc, sample, dst, dst_u, dst_tok0):
        P = self.P
        PS, uPS = self.PS, self.uPS
        xs, ux = self.XS[par], self.uXS[par]
        xn, uxn = self.XN[par], self.uXN[par]
        stt, mv, ust = self.stt_t, self.mv, self.ust
        mod, umod = self.mod, self.umod
        self.dma(xs[0:nrows, :], src_rows, writes=[ux])
        self.ln_stats(xs, ux, nrows, par)
        self.ts(xn[0:nrows, :], xs[0:nrows, :], mv[0:nrows, par, 0:1], mv[0:nrows, par, 3:4], ALU.subtract, ALU.mult,
                [ux, ust[par]], [uxn])
        self.xn_to_fm(xn, uxn, nrows, mk_sh, mk_sc, sample, dst, dst_u, dst_tok0)

    def ln_stats(self, xs, ux, nrows, par):
        P = self.P
        stt, mv, ust = self.stt_t, self.mv, self.ust
        for c in range(4):
            P.op("vector", lambda h, c=c: h.bn_stats(out=stt[0:nrows, par, c, :], in_=xs[0:nrows, c * 512:(c + 1) * 512]),
                 [ux], [ust[par]])
        P.op("vector", lambda h: h.bn_aggr(out=mv[0:nrows, par, 0:2], in_=stt[0:nrows, par, :, :].rearrange("p a b -> p (a b)")),
             [ust[par]], [ust[par]])
        self.act(mv[0:nrows, par, 2:3], mv[0:nrows, par, 1:2], AF.Sqrt, [ust[par], self.ueps], [ust[par]],
                 bias=self.eps_t[0:nrows, :], scale=1.0)
        P.op("vector", lambda h: h.reciprocal(out=mv[0:nrows, par, 3:4], in_=mv[0:nrows, par, 2:3]), [ust[par]], [ust[par]])

    def xn_to_fm(self, xn, uxn, nrows, mk_sh, mk_sc, sample, dst, dst_u, dst_tok0):
        PS, uPS = self.PS, self.uPS
        mod, umod = self.mod, self.umod
        for half in range(2):
            pb = 6 + half
            psb = PS[pb][:].bitcast(BF16)
            for j in range(8):
                kt = half * 8 + j
                self.tr(psb[:, j * 128:j * 128 + nrows], xn[0:nrows, kt * 128:(kt + 1) * 128],
                        self.ident_b[0:nrows, 0:nrows], [uxn, self.uID], [uPS[pb]])
            for j in range(8):
                kt = half * 8 + j
                src = psb[:, j * 128:j * 128 + nrows]
                o = dst[:, kt, dst_tok0:dst_tok0 + nrows]
                if not sample:
                    if j % 2 == 0:
                        self.act(o, src, AF.Identity, [uPS[pb], umod], [dst_u],
                                 scale=mod[:, mk_sc, kt, 0:1], bias=mod[:, mk_sh, kt, 0:1])
                    else:
                        self.ts(o, src, mod[:, mk_sc, kt, 0:1], mod[:, mk_sh, kt, 0:1], ALU.mult, ALU.add,
                                [uPS[pb], umod], [dst_u])
                else:
                    self.tt(o, src, mod[:, mk_sc, kt, 1:65], ALU.mult, [uPS[pb], umod], [dst_u])
                    self.tt(o, o, mod[:, mk_sh, kt, 1:65], ALU.add, [umod, dst_u], [dst_u], eng="gpsimd")

    def phase1(self):
        sb, P = self.sb, self.P
        PS, uPS, XS, uXS = self.PS, self.uPS, self.XS, self.uXS
        ident_f, ident_b, uID = self.ident_f, self.ident_b, self.uID
        xa, w_in = self.xa, self.w_in
        M, A, S = ALU.mult, ALU.add, ALU.subtract
        A4 = self.AR4
        xn0, uxn0 = A4.alloc([128, D], BF16, "XN0"); xn1, uxn1 = A4.alloc([128, D], BF16, "XN1")
        self.XN = [xn0, xn1]; self.uXN = [uxn0, uxn1]
        self.stt_t = sb("stt", [128, 2, 4, 6], F32)
        self.mv = sb("mv", [128, 2, 4], F32)
        self.ust = [U("st0"), U("st1")]
        self.eps_t = sb("eps_t", [128, 1], F32)
        self.ueps = U("eps")
        self.memset(self.eps_t[:], LN_EPS, [self.ueps])
        HG, uHG = self.AR1.alloc([128, 16, NOWN], BF16, "HG")
        self.HG, self.uHG = HG, uHG
        ZB, uZB = self.AR2.alloc([64, 2, 8, 272], F32, "ZB")
        UTb = [self.AR2.alloc([128, 1024], BF16, "UTb%d" % i) for i in range(2)]
        UO, uUO = self.AR3.alloc([128, 8, NOWN], BF16, "UO")
        self.UO, self.uUO = UO, uUO
        UPb = [self.AR3.alloc([128, 2, NOWN], BF16, "UPb%d" % i) for i in range(2)]
        SM, uSM = self.AR3.alloc([64, 2, 8, 272], BF16, "SM")
        SMBb, uSMBb = self.AR6.alloc([128, 8 * 272], BF16, "SMBb")
        KST = [XS[0][:, 0:1024], XS[1][:, 0:1024]]
        uKST = uXS
        KTB, uKTB = A4.alloc([128, 8, 128], BF16, "KTB")
        VB, uVB = A4.alloc([128, 1024], BF16, "VB")
        for i in range(2):
            self.memset(UPb[i][0][:].rearrange("p a b -> p (a b)"), 0.0, [UPb[i][1]], eng="gpsimd")
        ZBLK, uZBLK = A4.alloc([64, 2, 64, 8], F32, "ZBLKo")
        ZOWN, _ = A4.alloc([64, 2, 8, 8], F32, "ZOWN")
        A4.cur.append(uZBLK)
        R, uR = A4.alloc([64, 2, 2, 8, 16], F32, "R")
        TT, uTT = A4.alloc([64, 4, 8, 16], F32, "TT")
        SOWN, uSOWN = A4.alloc([64, 2, 8, 8], F32, "SOWN")
        SFIN, uSFIN = A4.alloc([64, 2, 64], F32, "SFIN")
        S0b, uS0 = A4.alloc([64, 2, 8, 16], F32, "S0b")
        SFSb, uSFS = A4.alloc([64, 2, 8, 16], F32, "SFSb")
        self.SMB_s = self.scratch("SMB_s", [8, 128, 8 * 272], BF16)
        A4p, A128p, uAp = self.A4p, self.A128p, self.uAp
        s0T = self.inp("s0T", [64, 2, 64, 16])
        sfin_o = self.outp("sfin_o", [64, 2, 64])
        sfs_o = self.outp("sfs_o", [64, 2, 64, 16])

        def cstep(ar, ai, sr, si, shape_n):
            g_, l_ = shape_n
            t = [TT[:, k, 0:g_, 0:l_] for k in range(4)]
            rdd = [uR, uAp, uZB, uZBLK, uSOWN, uSFIN, uSFS, uS0]
            self.tt(t[0], ar, sr, M, rdd, [uTT])
            self.tt(t[1], ai, si, M, rdd, [uTT])
            self.tt(t[2], ar, si, M, rdd, [uTT], eng="gpsimd")
            self.tt(t[3], ai, sr, M, rdd, [uTT], eng="gpsimd")
            self.tt(t[0], t[0], t[1], S, [uTT], [uTT])
            self.tt(t[2], t[2], t[3], A, [uTT], [uTT], eng="gpsimd")
            return t[0], t[2]

        def ssm_batch(gb, usrc, uusrc, ntok, is_A):
            nm = ntok // 4
            up, uup = UPb[gb % 2]
            for g8 in range(8):
                self.dma(up[32 * (g8 % 4):32 * (g8 % 4) + 16, g8 // 4, 0:ntok], usrc[16 * g8:16 * g8 + 16, 0:ntok],
                         reads=[uusrc], writes=[uup])
            for g8 in range(8):
                q, tt_ = g8 % 4, g8 // 4
                t = 2 * gb + tt_
                upv = up[32 * q:32 * q + 32, tt_, 0:ntok].rearrange("p (m i) -> p m i", i=4)
                for ri in range(2):
                    pb = self.bank()
                    for i in range(4):
                        self.mm(PS[pb][0:64, 0:nm], self.WT[32 * q:32 * q + 32, t, i, 64 * ri:64 * ri + 64], upv[:, :, i],
                                i == 0, i == 3, [self.uWT, uup], [uPS[pb]], tile_position=(32 * q, 0))
                    self.cp(ZB[:, ri, g8, 0:nm], PS[pb][0:64, 0:nm], [uPS[pb]], [uZB], eng="vector")
            gsl = slice(8 * gb, 8 * gb + 8)
            Zv = ZB[:, :, :, 0:256].rearrange("p r g (b m) -> p r g b m", m=32)
            a4r = A4p[:, 0, gsl].unsqueeze(2).to_broadcast([64, 8, 8])
            a4i = A4p[:, 1, gsl].unsqueeze(2).to_broadcast([64, 8, 8])
            cur = 0
            self.cp(R[:, cur, 0, :, 0:8], Zv[:, 0, :, :, 0], [uZB], [uR])
            self.cp(R[:, cur, 1, :, 0:8], Zv[:, 1, :, :, 0], [uZB], [uR], eng="gpsimd")
            for m in range(1, 32):
                pr, pi_ = cstep(a4r, a4i, R[:, cur, 0, :, 0:8], R[:, cur, 1, :, 0:8], (8, 8))
                nxt = 1 - cur
                self.tt(R[:, nxt, 0, :, 0:8], pr, Zv[:, 0, :, :, m], A, [uTT, uZB], [uR])
                self.tt(R[:, nxt, 1, :, 0:8], pi_, Zv[:, 1, :, :, m], A, [uTT, uZB], [uR], eng="gpsimd")
                cur = nxt
            if not is_A:
                self.cp(ZBLK[:, 0, gsl, :], R[:, cur, 0, :, 0:8], [uR], [uZBLK])
                self.cp(ZBLK[:, 1, gsl, :], R[:, cur, 1, :, 0:8], [uR], [uZBLK], eng="gpsimd")
                return
            self.cp(ZOWN[:, 0, :, :], R[:, cur, 0, :, 0:8], [uR], [uZBLK])
            self.cp(ZOWN[:, 1, :, :], R[:, cur, 1, :, 0:8], [uR], [uZBLK], eng="gpsimd")
            hf = self.hf[:, 0:1]
            a128r = A128p[:, 0, gsl]
            a128i = A128p[:, 1, gsl]
            Sr, Si = SFIN[:, 0, gsl], SFIN[:, 1, gsl]
            self.memset(Sr, 0.0, [uSFIN])
            self.memset(Si, 0.0, [uSFIN])
            W1 = R[:, 0, :, :, 8:9].rearrange("p r g o -> p r (g o)")
            W2 = R[:, 1, :, :, 8:9].rearrange("p r g o -> p r (g o)")
            W3 = R[:, 0, :, :, 9:10].rearrange("p r g o -> p r (g o)")
            t = [TT[:, k, :, 0] for k in range(4)]
            for i in range(8):
                for ri in range(2):
                    X = ZOWN[:, ri, :, i]
                    Y = ZBLK[:, ri, gsl, i]
                    self.tt(W1[:, ri, :], Y, X, S, [uZBLK], [uR])
                    self.stt(W2[:, ri, :], W1[:, ri, :], hf, X, M, A, [uR, self.uhf, uZBLK], [uR])
                    self.tt(W1[:, ri, :], X, Y, A, [uZBLK], [uR])
                    self.tt(W3[:, ri, :], W1[:, ri, :], W2[:, ri, :], S, [uR], [uR])
                rdd = [uAp, uSFIN, uR]
                self.tt(t[0], a128r, Sr, M, rdd, [uTT]); self.tt(t[1], a128i, Si, M, rdd, [uTT])
                self.tt(t[2], a128r, Si, M, rdd, [uTT]); self.tt(t[3], a128i, Sr, M, rdd, [uTT])
                self.tt(t[0], t[0], t[1], S, [uTT], [uTT]); self.tt(t[2], t[2], t[3], A, [uTT], [uTT])
                self.tt(t[0], t[0], W2[:, 0, :], A, [uTT, uR], [uTT])
                self.tt(t[2], t[2], W2[:, 1, :], A, [uTT, uR], [uTT])
                self.tt(t[1], t[0], Sr, S, [uTT, uSFIN], [uTT]); self.tt(t[3], t[2], Si, S, [uTT, uSFIN], [uTT])
                self.stt(SOWN[:, 0, :, i], t[1], hf, Sr, M, A, [uTT, self.uhf, uSFIN], [uSOWN])
                self.stt(SOWN[:, 1, :, i], t[3], hf, Si, M, A, [uTT, self.uhf, uSFIN], [uSOWN])
                self.cp(W1[:, 0, :], t[0], [uTT], [uR]); self.cp(W1[:, 1, :], t[2], [uTT], [uR])
                self.tt(t[0], a128r, W1[:, 0, :], M, [uAp, uR], [uTT]); self.tt(t[1], a128i, W1[:, 1, :], M, [uAp, uR], [uTT])
                self.tt(t[2], a128r, W1[:, 1, :], M, [uAp, uR], [uTT]); self.tt(t[3], a128i, W1[:, 0, :], M, [uAp, uR], [uTT])
                self.tt(t[0], t[0], t[1], S, [uTT], [uTT]); self.tt(t[2], t[2], t[3], A, [uTT], [uTT])
                self.tt(Sr, t[0], W3[:, 0, :], A, [uTT, uR], [uSFIN])
                self.tt(Si, t[2], W3[:, 1, :], A, [uTT, uR], [uSFIN])
            SMv = SM[:, :, :, 0:256].rearrange("p r g (b m) -> p r g b m", m=32)
            cur = 0
            self.cp(R[:, cur, 0, :, 0:8], SOWN[:, 0, :, :], [uSOWN], [uR])
            self.cp(R[:, cur, 1, :, 0:8], SOWN[:, 1, :, :], [uSOWN], [uR], eng="gpsimd")
            for m in range(32):
                self.cp(SMv[:, 0, :, :, m], R[:, cur, 0, :, 0:8], [uR], [uSM], eng="scalar")
                self.cp(SMv[:, 1, :, :, m], R[:, cur, 1, :, 0:8], [uR], [uSM], eng="scalar")
                if m == 31:
                    break
                pr, pi_ = cstep(a4r, a4i, R[:, cur, 0, :, 0:8], R[:, cur, 1, :, 0:8], (8, 8))
                nxt = 1 - cur
                self.tt(R[:, nxt, 0, :, 0:8], pr, Zv[:, 0, :, :, m], A, [uTT, uZB], [uR])
                self.tt(R[:, nxt, 1, :, 0:8], pi_, Zv[:, 1, :, :, m], A, [uTT, uZB], [uR], eng="gpsimd")
                cur = nxt
            self.dma(S0b[:], s0T[:, :, gsl, :], writes=[uS0])
            s0r, s0i = S0b[:, 0, :, :], S0b[:, 1, :, :]
            self.cp(SM[:, 0, :, 256:272], s0r, [uS0], [uSM], eng="scalar")
            self.cp(SM[:, 1, :, 256:272], s0i, [uS0], [uSM], eng="scalar")
            a4r16 = A4p[:, 0, gsl].unsqueeze(2).to_broadcast([64, 8, 16])
            a4i16 = A4p[:, 1, gsl].unsqueeze(2).to_broadcast([64, 8, 16])
            pr, pi_ = cstep(a4r16, a4i16, s0r, s0i, (8, 16))
            self.tt(SFSb[:, 0, :, :], pr, ZB[:, 0, :, 256:272], A, [uTT, uZB], [uSFS])
            self.tt(SFSb[:, 1, :, :], pi_, ZB[:, 1, :, 256:272], A, [uTT, uZB], [uSFS], eng="gpsimd")
            self.dma(sfs_o[:, :, gsl, :], SFSb[:], reads=[uSFS])
            SMf = SM[:].rearrange("p r g m -> p r (g m)")
            for c0 in range(0, 8 * 272, 512):
                n = min(512, 8 * 272 - c0)
                pb = self.bank()
                self.mm(PS[pb][:, 0:n], self.esel[:, 0, :], SMf[:, 0, c0:c0 + n], True, False, [self.uhf, uSM], [uPS[pb]])
                self.mm(PS[pb][:, 0:n], self.esel[:, 1, :], SMf[:, 1, c0:c0 + n], False, True, [self.uhf, uSM], [uPS[pb]])
                self.cp(SMBb[:, c0:c0 + n], PS[pb][:, 0:n], [uPS[pb]], [uSMBb], eng=self.evac_eng())
            self.dma(self.SMB_s[gb], SMBb[:, :], reads=[uSMBb])

        groupB = [(1024 + 128 * j, 128, False, False, None) for j in range(8)]
        groupA = [(128 * j, 128, False, True, 128 * j) for j in range(8)] + [(2048, 64, True, True, 1024)]
        for gi, grp in enumerate((groupB, groupA)):
            is_A = gi == 1
            toks = 0
            offs = []
            for bi, (r0, nr, is_s, is_own, orow) in enumerate(grp):
                self.ln_block(xa[r0:r0 + nr, :], nr, bi % 2, 0, 1, is_s, HG, uHG, toks)
                offs.append(toks)
                toks += nr
            ntok = toks
            chunks = [(0, 512), (512, 512)] + ([(1024, 64)] if is_A else [])
            for cb in range(2):
                def ev(ms, t0, n, ps, ups, cb=cb):
                    gb = cb * 4 + ms
                    if is_A:
                        self.cp(UO[:, gb, t0:t0 + n], ps, [ups], [uUO], eng=self.evac_eng())
                        if t0 + n == ntok:
                            ssm_batch(gb, UO[:, gb, :], uUO, ntok, True)
                    else:
                        ut, uut = UTb[gb % 2]
                        self.cp(ut[:, t0:t0 + n], ps, [ups], [uut], eng=self.evac_eng())
                        if t0 + n == ntok:
                            ssm_batch(gb, ut, uut, ntok, False)
                self.proj_fm(w_in, 0, 16, cb * 512, 512, lambda kt, t0, n: HG[:, kt, t0:t0 + n], [uHG], chunks, ev)
            for which, cbase in (("k", 2048), ("v", 3072)):
                wbs = [self.wload(w_in, 0, 16, cbase + cb * 512, 512) for cb in range(2)]
                for bi, (r0, nr, is_s, is_own, orow) in enumerate(grp):
                    st_i = bi % 2
                    for cb in range(2):
                        wb, uwb = wbs[cb]
                        pb = self.bank()
                        for kt in range(16):
                            self.mm(PS[pb][0:nr, :], HG[:, kt, offs[bi]:offs[bi] + nr], wb[:, kt, :], kt == 0, kt == 15,
                                    [uHG, uwb], [uPS[pb]])
                        self.cp(KST[st_i][0:nr, cb * 512:(cb + 1) * 512], PS[pb][0:nr, :], [uPS[pb]], [uKST[st_i]],
                                eng=self.evac_eng())
                    tok0 = (r0 if not is_s else 2048)
                    if is_own:
                        if is_s:
                            dst = (self.k_s if which == "k" else self.v_s)[0:64, :]
                        else:
                            dst = (self.k_own if which == "k" else self.v_own)[orow:orow + 128, :]
                        self.dma(dst, KST[st_i][0:nr, :], reads=[uKST[st_i]])
                    if which == "k":
                        for hh in range(8):
                            pq = 6 + (hh % 2)
                            self.tr(PS[pq][:, 0:nr], KST[st_i][0:nr, hh * 128:(hh + 1) * 128],
                                    ident_f[0:nr, 0:nr], [uKST[st_i], uID], [uPS[pq]])
                            self.cp(KTB[:, hh, 0:nr], PS[pq][:, 0:nr], [uPS[pq]], [uKTB], eng=self.evac_eng())
                        self.dma(self.KT_s[:, :, tok0:tok0 + nr].rearrange("h p t -> p h t"), KTB[:, :, 0:nr], reads=[uKTB])
                    else:
                        self.cp(VB[0:nr, :], KST[st_i][0:nr, :], [uKST[st_i]], [uVB], eng="gpsimd")
                        self.dma(self.V_s[:, tok0:tok0 + nr, :].rearrange("h t d -> t h d"),
                                 VB[0:nr, :].rearrange("t (h d) -> t h d", h=8), reads=[uVB])
        self.dma(sfin_o, SFIN[:], reads=[uSFIN])
        self.AR2.reset()

    def phase_ssm_y(self):
        sb, P = self.sb, self.P
        PS, uPS = self.PS, self.uPS
        M, A, S = ALU.mult, ALU.add, ALU.subtract
        UO, uUO, HG, uHG = self.UO, self.uUO, self.HG, self.uHG
        self.AR3.reset(keep=8 * NOWN * 2)
        self.AR4.reset()
        ZACT, uZACT = self.AR3.alloc([128, 8, NOWN], BF16, "ZACT")
        GLUO, uGLUO = self.AR4.alloc([128, 8, NOWN], BF16, "GLUO")
        KLt = [self.AR2.alloc([128, 4, 128], BF16, "KLt%d" % i) for i in range(2)]
        CALt = [self.AR2.alloc([128, 8, 4, 32], BF16, "CALt%d" % i) for i in range(2)]
        SMBt = [self.AR2.alloc([128, 8, 272], BF16, "SMBt%d" % i) for i in range(2)]
        Yt, uYt = self.AR2.alloc([128, 512], F32, "Yt")
        Tt, uTt = self.AR2.alloc([128, 512], F32, "Tt")
        chunks = [(0, 512, 0), (512, 512, 128), (1024, 64, 256)]
        for t8 in range(8):
            kl, ukl = KLt[t8 % 2]; cal, ucal = CALt[t8 % 2]; smb, usmb = SMBt[t8 % 2]
            self.dma(kl[:].rearrange("p a b -> p (a b)"), self.KL_s[:, t8 * 512:(t8 + 1) * 512], writes=[ukl])
            self.dma(cal[:].rearrange("p a b c -> p (a b c)"), self.CAL_s[:, t8 * 1024:(t8 + 1) * 1024], writes=[ucal])
            self.dma(smb[:].rearrange("p a b -> p (a b)"), self.SMB_s[t8], writes=[usmb])
            for (t0, n, m0) in chunks:
                nm = n // 4
                pb = self.bank()
                pv = PS[pb][:, 0:n].rearrange("p (m j) -> p m j", j=4)
                uv = UO[:, t8, t0:t0 + n].rearrange("p (m j) -> p m j", j=4)
                for l in range(4):
                    self.mm(pv[:, :, l:4], kl[:, l, :], uv[:, :, 0:4 - l], l == 0, False, [ukl, uUO], [uPS[pb]])
                for g8 in range(8):
                    pair = g8 // 2
                    for j in range(4):
                        self.mm(pv[32 * pair:32 * pair + 32, :, j], cal[:, g8, j, :], smb[:, g8, m0:m0 + nm], False,
                                (g8 == 7 and j == 3), [ucal, usmb], [uPS[pb]], tile_position=(0, 32 * pair))
                self.stt(Yt[:, 0:n], UO[:, t8, t0:t0 + n], self.dcol[:, t8:t8 + 1], PS[pb][:, 0:n], M, A,
                         [uUO, self.udcol, uPS[pb]], [uYt])
                self.tt(Tt[:, 0:n], Yt[:, 0:n], Yt[:, 0:n], M, [uYt], [uTt], eng="gpsimd")
                self.ts(Tt[:, 0:n], Tt[:, 0:n], 0.044715, 1.0, M, A, [uTt], [uTt], eng="gpsimd")
                self.tt(Tt[:, 0:n], Tt[:, 0:n], Yt[:, 0:n], M, [uTt, uYt], [uTt], eng="gpsimd")
                self.act(Tt[:, 0:n], Tt[:, 0:n], AF.Sigmoid, [uTt], [uTt], scale=1.5957691216057308)
                self.tt(ZACT[:, t8, t0:t0 + n], Yt[:, 0:n], Tt[:, 0:n], M, [uYt, uTt], [uZACT])
        pchunks = [(0, 512), (512, 512), (1024, 64)]
        for cb in range(2):
            def ev(ms, t0, n, ps, ups, cb=cb):
                self.act(Tt[:, 0:n], ps, AF.Sigmoid, [ups], [uTt])
                self.tt(GLUO[:, cb * 4 + ms, t0:t0 + n], ZACT[:, cb * 4 + ms, t0:t0 + n], Tt[:, 0:n], M, [uZACT, uTt], [uGLUO])
            self.proj_fm(self.w_glu, 0, 8, cb * 512, 512, lambda kt, t0, n: ZACT[:, kt, t0:t0 + n], [uZACT], pchunks, ev)
        self.AR3.reset()
        MIX, uMIX = self.AR3.alloc([128, 16, NOWN], BF16, "MIX")
        self.MIX, self.uMIX = MIX, uMIX
        self.gated_up(self.w_up_ssm, GLUO, uGLUO, 4096, first=True)
        self.AR2.reset(); self.AR4.reset()

    def gated_up(self, w_up, src, usrc, gate_c0, first):
        PS, uPS = self.PS, self.uPS
        HG, uHG, MIX, uMIX = self.HG, self.uHG, self.MIX, self.uMIX
        M, A = ALU.mult, ALU.add
        self.AR6.reset()
        SG = [self.AR6.alloc([128, 512], F32, "SG%d" % i) for i in range(2)]
        pchunks = [(0, 512), (512, 512), (1024, 64)]
        for cb in range(4):
            wA, uwA = self.wload(w_up, 0, 8, cb * 512, 512)
            wB, uwB = self.wload(self.w_in, 0, 16, gate_c0 + cb * 512, 512)
            for ms in range(4):
                mt = cb * 4 + ms
                for (t0, n) in pchunks:
                    pa = self.bank()
                    for kt in range(8):
                        self.mm(PS[pa][:, 0:n], wA[:, kt, ms * 128:(ms + 1) * 128], src[:, kt, t0:t0 + n], kt == 0, kt == 7,
                                [uwA, usrc], [uPS[pa]])
                    pg = self.bank()
                    for kt in range(16):
                        self.mm(PS[pg][:, 0:n], wB[:, kt, ms * 128:(ms + 1) * 128], HG[:, kt, t0:t0 + n], kt == 0, kt == 15,
                                [uwB, uHG], [uPS[pg]])
                    sg, usg = SG[(mt + (t0 // 512)) % 2]
                    self.act(sg[:, 0:n], PS[pg][:, 0:n], AF.Sigmoid, [uPS[pg]], [usg])
                    if first:
                        self.tt(MIX[:, mt, t0:t0 + n], sg[:, 0:n], PS[pa][:, 0:n], M, [usg, uPS[pa]], [uMIX])
                    else:
                        self.tt(sg[:, 0:n], sg[:, 0:n], PS[pa][:, 0:n], M, [usg, uPS[pa]], [usg])
                        self.tt(MIX[:, mt, t0:t0 + n], MIX[:, mt, t0:t0 + n], sg[:, 0:n], A, [usg, uMIX], [uMIX], eng="gpsimd")

    def phase_attn(self):
        sb, P, inp = self.sb, self.P, self.inp
        PS, uPS, XS, uXS = self.PS, self.uPS, self.XS, self.uXS
        ident_f, ident_b, uID = self.ident_f, self.ident_b, self.uID
        M, A, S = ALU.mult, ALU.add, ALU.subtract
        HG, uHG = self.HG, self.uHG
        SCALE = 0.125
        bt_d = inp("bias_tab", [128, 128]); tri_d = inp("tri", [128, 256], BF16)
        ka_d = inp("ka", [4, 17 * 128], BF16); qa_d = inp("qa", [4, 512], BF16)
        ms_d = inp("msk", [64, 16 * 64], BF16); lamv_d = inp("lamv", [1, 256]); subg_d = inp("subg", [128, 1])
        iota_d = inp("iota", [128, 1]); ptab_d = inp("ptab", [1, 256], I32)
        cache_k = inp("cache_k", [self.npool_rows, 1024]); cache_v = inp("cache_v", [self.npool_rows, 1024])
        A2, A4 = self.AR2, self.AR4
        self.AR5.reset(); self.AR6.reset()
        A3, A5, A6 = self.AR3, self.AR5, self.AR6
        QO, uQO = A2.alloc([128, 8, NOWN], BF16, "QO")
        AT, uAT = A4.alloc([128, 8, NOWN], BF16, "AT")
        self.AT, self.uAT = AT, uAT
        KTh = [A4.alloc([128, NTOK], BF16, "KTh%d" % i) for i in range(2)]
        Vh = [A4.alloc([128, 17, 129], BF16, "Vh%d" % i) for i in range(2)]
        cst = U("attn_const")
        BT = sb("BT", [128, 8, 2, 8], F32); TRI = sb("TRI", [128, 256], BF16)
        KA, _ = A5.alloc([4, 17, 128], BF16, "KA"); A5.cur.append(cst)
        QA = sb("QA", [4, 512], BF16); MS = sb("MS", [64, 16, 64], BF16)
        lamv = XS[1][0:1, 1024:1280].rearrange("p (a b) -> p a b", a=4); lsc = sb("lsc", [128, 8], F32); subg = sb("subg", [128, 1], F32)
        iota = sb("iota", [128, 1], F32); PTB = XS[1][:, 0:256].bitcast(I32); IDX = sb("IDX", [128, 256], I32)
        ones1 = sb("ones1", [1, 128], F32); CSEL = sb("CSEL", [36, 4], F32)
        for t_, d_ in ((BT[:].rearrange("p a b c -> p (a b c)"), bt_d), (TRI[:], tri_d), (KA[:].rearrange("p a b -> p (a b)"), ka_d),
                       (QA[:], qa_d), (MS[:].rearrange("p a b -> p (a b)"), ms_d), (lamv[:].rearrange("p a b -> p (a b)"), lamv_d),
                       (subg[:], subg_d), (iota[:], iota_d)):
            self.dma(t_, d_, writes=[cst])
        self.dma(PTB[:], ptab_d[0, :].partition_broadcast(128), writes=[cst])
        self.ts(IDX[:], PTB[:], 128.0, iota[:, 0:1], M, A, [cst], [cst])
        self.memset(ones1[:], 1.0, [cst])
        self.tt(lamv[:, 0, :], lamv[:, 0, :], lamv[:, 1, :], M, [cst], [cst])
        self.tt(lamv[:, 2, :], lamv[:, 2, :], lamv[:, 3, :], M, [cst], [cst])
        P.op("vector", lambda h: h.tensor_reduce(out=lamv[:, 1, 0:1], in_=lamv[:, 0, :], axis=AX.X, op=A), [cst], [cst])
        P.op("vector", lambda h: h.tensor_reduce(out=lamv[:, 1, 1:2], in_=lamv[:, 2, :], axis=AX.X, op=A), [cst], [cst])
        self.act(lamv[:, 1, 0:2], lamv[:, 1, 0:2], AF.Exp, [cst], [cst])
        self.tt(lamv[:, 1, 2:3], lamv[:, 1, 0:1], lamv[:, 1, 1:2], S, [cst], [cst])
        self.ts(lamv[:, 1, 2:3], lamv[:, 1, 2:3], 0.2, None, A, None, [cst], [cst])
        pb = self.bank()
        self.mm(PS[pb][:, 0:1], ones1[:, :], lamv[:, 1, 2:3], True, True, [cst], [uPS[pb]])
        self.cp(lsc[:, 0:1], PS[pb][:, 0:1], [uPS[pb]], [cst])
        self.ts(lsc[:, 1:2], lsc[:, 0:1], -1.0, None, M, None, [cst], [cst])
        self.ts(subg[:], subg[:], 0.8, None, M, None, [cst], [cst])
        self.memset(CSEL[:], 0.0, [cst])
        self.cp(CSEL[0:4, :], ident_f[0:4, 0:4], [cst, uID], [cst])
        self.ts(CSEL[32:36, :], ident_f[32:36, 32:36], lsc[32:36, 1:2], None, M, None, [cst, uID], [cst])
        pchunks = [(0, 512), (512, 512), (1024, 64)]
        for cb in range(2):
            def ev(ms, t0, n, ps, ups, cb=cb):
                self.cp(QO[:, cb * 4 + ms, t0:t0 + n], ps, [ups], [uQO], eng=self.evac_eng())
            self.proj_fm(self.w_in, 0, 16, 1024 + cb * 512, 512, lambda kt, t0, n: HG[:, kt, t0:t0 + n], [uHG], pchunks, ev)
        QP = [A2.alloc([128, 2, 128], BF16, "QP%d" % i) for i in range(2)]
        for qp, uqp in QP:
            self.memset(qp[:].rearrange("p a b -> p (a b)"), 0.0, [uqp])
        PT = [A2.alloc([128, 512], BF16, "PT%d" % i) for i in range(2)]
        ON, uON = A3.alloc([128, 4, 128], F32, "ON")
        ONb = XS[0][:, 1544:1672].bitcast(BF16); uONb = uXS[0]
        sc = sb("sc", [128, 16], F32); usc = U("sc")
        for v_, uv_ in Vh:
            self.memset(v_[:, :, 128:129], 1.0, [uv_])
        QSb = [A2.alloc([128, 8, 64], BF16, "QS%d" % i) for i in range(2)]
        for q_, uq_ in QSb:
            self.memset(q_[:].rearrange("p a b -> p (a b)"), 0.0, [uq_], eng="gpsimd")

        def finish_rows(nrows, o1, o2, z1, z2, dst_tok0, ntk, hsel):
            pass

        for hh in range(8):
            kt_, ukt = KTh[hh % 2]
            vv, uvv = Vh[hh % 2]
            self.dma(kt_[:, :], self.KT_s[hh], writes=[ukt])
            self.dma(vv[:, 0:16, 0:128], self.V_s[hh, 0:2048, :].rearrange("(j p) d -> p j d", p=128), writes=[uvv])
            self.dma(vv[0:64, 16, 0:128], self.V_s[hh, 2048:2112, :], writes=[uvv])
            for i in range(8):
                qp, uqp = QP[i % 2]
                self.cp(qp[0:64, 0, :], QO[0:64, hh, i * 128:(i + 1) * 128], [uQO], [uqp], eng="vector")
                self.cp(qp[64:128, 1, :], QO[64:128, hh, i * 128:(i + 1) * 128], [uQO], [uqp], eng="gpsimd")
                bo = 2 + 2 * (i % 2)
                blocks = []
                for j in range(i + 1):
                    blocks.append((j, 0, i - j))
                    blocks.append((8 + j, 1, i - j))
                for bi_, (kb, kind, rel) in enumerate(blocks):
                    ps_ = bi_ % 2
                    pt, upt = PT[bi_ % 2]
                    self.mm(PS[ps_][:, 0:128], kt_[:, kb * 128:(kb + 1) * 128], qp[:, 0, :], True, True, [ukt, uqp], [uPS[ps_]])
                    self.mm(PS[ps_][:, 128:256], kt_[:, kb * 128:(kb + 1) * 128], qp[:, 1, :], True, True, [ukt, uqp], [uPS[ps_]])
                    self.act(pt[:, 0:256], PS[ps_][:, 0:256], AF.Exp, [uPS[ps_], cst], [upt], scale=SCALE,
                             bias=BT[:, hh, kind, rel:rel + 1])
                    if kind == 0 and rel == 0:
                        self.tt(pt[:, 0:256], pt[:, 0:256], TRI[:, :], M, [upt, cst], [upt], eng="gpsimd")
                    first, last = bi_ == 0, bi_ == len(blocks) - 1
                    self.mm(PS[bo][:, 0:129], pt[:, 0:128], vv[:, kb, :], first, last, [upt, uvv], [uPS[bo]])
                    self.mm(PS[bo + 1][:, 0:129], pt[:, 128:256], vv[:, kb, :], first, last, [upt, uvv], [uPS[bo + 1]])
                c = 2 * (i % 2)
                P.op("vector", lambda h, bo=bo, c=c: h.reciprocal(out=sc[:, c:c + 1], in_=PS[bo][:, 128:129]), [uPS[bo]], [usc])
                P.op("vector", lambda h, bo=bo, c=c: h.reciprocal(out=sc[:, c + 1:c + 2], in_=PS[bo + 1][:, 128:129]), [uPS[bo + 1]], [usc])
                self.tt(sc[:, c + 1:c + 2], sc[:, c + 1:c + 2], lsc[:, 0:1], M, [usc, cst], [usc])
                o_ = ON[:, i % 2, 0:128]
                t_ = ON[:, 2 + i % 2, 0:128]
                self.ts(t_, PS[bo + 1][:, 0:128], sc[:, c + 1:c + 2], None, M, None, [uPS[bo + 1], usc], [uON])
                self.stt(o_, PS[bo][:, 0:128], sc[:, c:c + 1], t_, M, S, [uPS[bo], usc, uON], [uON])
                self.act(t_, o_, AF.Square, [uON], [uON, usc], accum_out=sc[:, 4 + c:5 + c])
                self.ts(sc[:, 5 + c:6 + c], sc[:, 4 + c:5 + c], 1.0 / 128, LN_EPS, M, A, [usc], [usc])
                self.act(sc[:, 5 + c:6 + c], sc[:, 5 + c:6 + c], AF.Sqrt, [usc], [usc])
                P.op("vector", lambda h, c=c: h.reciprocal(out=sc[:, 5 + c:6 + c], in_=sc[:, 5 + c:6 + c]), [usc], [usc])
                self.ts(ONb[:, (i % 2) * 128:(i % 2) * 128 + 128], o_, sc[:, 5 + c:6 + c], None, M, None, [uON, usc], [uONb])
                psb = PS[6 + i % 2][:].bitcast(BF16)
                self.tr(psb[:, 0:128], ONb[:, (i % 2) * 128:(i % 2) * 128 + 128], ident_b[:, :], [uONb, uID], [uPS[6 + i % 2]])
                self.ts(AT[:, hh, i * 128:(i + 1) * 128], psb[:, 0:128], subg[:, 0:1], None, M, None, [uPS[6 + i % 2], cst], [uAT])

        KP = [A5.alloc([128, 1024], BF16, "KP%d" % i) for i in range(2)]
        VP = [A6.alloc([128, 8, 129], BF16, "VP%d" % i) for i in range(2)]
        VPc = [(sb("VPc%d" % i, [128, 1024], BF16), U("VPc%d" % i)) for i in range(2)]
        for v_, uv_ in VP:
            self.memset(v_[:, :, 128:129], 1.0, [uv_])
        KTP = [A4.alloc([128, 8, 128], BF16, "KTP%d" % i) for i in range(2)]
        KTN, uKN = A4.alloc([128, 8, 64], BF16, "KTN"); VN, _ = A4.alloc([64, 8, 129], BF16, "VN")
        self.dma(KTN[:], self.KT_s[:, :, 2048:2112].rearrange("h p t -> p h t"), writes=[uKN])
        self.memset(VN[:, :, 128:129], 1.0, [uKN])
        self.dma(VN[:, :, 0:128], self.V_s[:, 2048:2112, :].rearrange("h t d -> t h d"), writes=[uKN])
        OACC = XS[0][0:36, 0:1032].rearrange("p (a b) -> p a b", a=8); uOA = uXS[0]
        OCBb = XS[0][0:4, 1032:1544].bitcast(BF16)
        OCB = XS[1][0:4, 0:1024].rearrange("p (a b) -> p a b", a=8); uOCB = uXS[1]
        OCT = XS[1][0:4, 1024:2048].rearrange("p (a b) -> p a b", a=8)
        hb = [(0, 3), (3, 6), (6, 8)]
        ck3 = cache_v.rearrange("r (h d) -> r h d", h=8)
        n_pg = 0
        for s_ in range(16):
            QS_, uQS = QSb[s_ % 2]
            srcq = QO[:, :, 1024 + 4 * s_:1024 + 4 * s_ + 4]
            self.cp(QS_[0:64, :, 0:4], srcq[0:64], [uQO], [uQS], eng="vector")
            self.cp(QS_[64:128, :, 32:36], srcq[64:128], [uQO], [uQS], eng="gpsimd")
            for slot in range(17):
                if slot < 16:
                    kp, ukp = KP[n_pg % 2]; vp, uvp = VP[n_pg % 2]; ktp, uktp = KTP[n_pg % 2]
                    col = s_ * 16 + slot
                    P.op("gpsimd", lambda h, kp=kp, col=col: h.indirect_dma_start(
                        out=kp[:, :], out_offset=None, in_=cache_k[:, :],
                        in_offset=bass.IndirectOffsetOnAxis(ap=IDX[:, col:col + 1], axis=0)), [cst], [ukp], dma=True)
                    vpc, uvpc = VPc[n_pg % 2]
                    P.op("gpsimd", lambda h, vpc=vpc, col=col: h.indirect_dma_start(
                        out=vpc[:, :], out_offset=None, in_=cache_v[:, :],
                        in_offset=bass.IndirectOffsetOnAxis(ap=IDX[:, col:col + 1], axis=0)), [cst], [uvpc], dma=True)
                    self.cp(vp[:, :, 0:128], vpc[:, :].rearrange("p (h d) -> p h d", h=8), [uvpc], [uvp], eng=self.evac_eng())
                    pq = 6 + n_pg % 2
                    psb = PS[pq][:].bitcast(BF16)
                    for hh in range(8):
                        self.tr(psb[:, hh * 128:(hh + 1) * 128], kp[:, hh * 128:(hh + 1) * 128], ident_b[:, :], [ukp, uID], [uPS[pq]])
                    self.cp(ktp[:].rearrange("p a b -> p (a b)"), psb[:, 0:1024], [uPS[pq]], [uktp], eng=self.evac_eng())
                    nk = 128
                    kt_of = lambda hh, ktp=ktp: ktp[:, hh, :]
                    v_of = lambda hh, vp=vp: vp[:, hh, :]
                    rd_k, rd_v = [uktp], [uvp]
                    n_pg += 1
                else:
                    nk = 64
                    kt_of = lambda hh: KTN[:, hh, :]
                    v_of = lambda hh: VN[:, hh, :]
                    rd_k, rd_v = [uKN], [uKN]
                ps_ = n_pg % 2
                pt, upt = PT[n_pg % 2]
                self.mm(PS[ps_][0:nk, :], KA[:, slot, 0:nk], QA[:, :], True, False, [cst], [uPS[ps_]])
                for hh in range(8):
                    self.mm(PS[ps_][0:nk, hh * 64:(hh + 1) * 64], kt_of(hh), QS_[:, hh, :], False, hh == 7,
                            rd_k + [uQS], [uPS[ps_]])
                self.act(pt[0:nk, :], PS[ps_][0:nk, :], AF.Exp, [uPS[ps_]], [upt], scale=SCALE)
                if slot == 16:
                    self.tt(pt[0:64, :].rearrange("p (h c) -> p h c", h=8), pt[0:64, :].rearrange("p (h c) -> p h c", h=8),
                            MS[:, s_, :].unsqueeze(1).to_broadcast([64, 8, 64]), M, [upt, cst], [upt], eng="gpsimd")
                for bi_, (h0, h1) in enumerate(hb):
                    pb_ = 2 + bi_
                    for hh in range(h0, h1):
                        self.mm(PS[pb_][0:36, (hh - h0) * 129:(hh - h0 + 1) * 129], pt[0:nk, hh * 64:hh * 64 + 36], v_of(hh),
                                True, True, [upt] + rd_v, [uPS[pb_]])
                    nh = h1 - h0
                    dst = OACC[:, h0:h1, :].rearrange("p a b -> p (a b)")
                    if slot == 0:
                        self.cp(dst, PS[pb_][0:36, 0:nh * 129], [uPS[pb_]], [uOA])
                    else:
                        self.tt(dst, dst, PS[pb_][0:36, 0:nh * 129], A, [uPS[pb_], uOA], [uOA])
            P.op("vector", lambda h: h.reciprocal(out=OACC[:, :, 128:129], in_=OACC[:, :, 128:129]), [uOA], [uOA])
            self.tt(OACC[:, :, 0:128], OACC[:, :, 0:128], OACC[:, :, 128:129].to_broadcast([36, 8, 128]), M, [uOA], [uOA])
            for half in range(2):
                pb_ = 5
                self.mm(PS[pb_][0:4, :], CSEL[:, :], OACC[:, 4 * half:4 * half + 4, 0:128], True, True, [cst, uOA], [uPS[pb_]])
                self.cp(OCB[:, 4 * half:4 * half + 4, :], PS[pb_][0:4, :].rearrange("p (a b) -> p a b", a=4), [uPS[pb_]], [uOCB])
            self.tt(OCT[:], OCB[:], OCB[:], M, [uOCB], [uOCB])
            P.op("vector", lambda h: h.tensor_reduce(out=sc[0:4, 8:16], in_=OCT[:], axis=AX.X, op=A), [uOCB], [usc])
            self.ts(sc[0:4, 8:16], sc[0:4, 8:16], 1.0 / 128, LN_EPS, M, A, [usc], [usc])
            self.act(sc[0:4, 8:16], sc[0:4, 8:16], AF.Sqrt, [usc], [usc])
            P.op("vector", lambda h: h.reciprocal(out=sc[0:4, 8:16], in_=sc[0:4, 8:16]), [usc], [usc])
            self.tt(OCBb[:].rearrange("p (a b) -> p a b", a=8), OCB[:], sc[0:4, 8:16].unsqueeze(2).to_broadcast([4, 8, 128]), M,
                    [uOCB, usc], [uOCB])
            psb = PS[6 + s_ % 2][:].bitcast(BF16)
            for hh in range(8):
                self.tr(psb[:, hh * 4:hh * 4 + 4], OCBb[:, hh * 128:(hh + 1) * 128], ident_b[0:4, 0:4], [uOCB, uID], [uPS[6 + s_ % 2]])
            self.ts(AT[:, :, 1024 + 4 * s_:1024 + 4 * s_ + 4], psb[:, 0:32].rearrange("p (a b) -> p a b", a=8), subg[:, 0:1], None,
                    M, None, [uPS[6 + s_ % 2], cst], [uAT])
        self.gated_up(self.w_up_att, AT, uAT, 6144, first=False)
        self.AR2.reset(); self.AR4.reset(); self.AR1.reset()

    def phase_rest(self):
        sb, P, inp = self.sb, self.P, self.inp
        PS, uPS, XS, uXS = self.PS, self.uPS, self.XS, self.uXS
        ident_f, ident_b, uID = self.ident_f, self.ident_b, self.uID
        M, A, S = ALU.mult, ALU.add, ALU.subtract
        A1, A2, A3, A4, A5, A6 = self.AR1, self.AR2, self.AR3, self.AR4, self.AR5, self.AR6
        w_o = inp("w_o", [D, D]); w_ff1 = inp("w_ff1", [D, 8192]); w_ff2 = inp("w_ff2", [8192, D])
        lnp = inp("lnp", [4, D])
        y_own = self.outp("y_own", [1024, D]); y_s = self.outp("y_s", [64, D])
        X1_s = self.scratch("X1_s", [NOWN, D])
        MIX, uMIX = self.MIX, self.uMIX
        pchunks = [(0, 512), (512, 512), (1024, 64)]
        blocks = [(128 * i, 128, 128 * i, False) for i in range(8)] + [(2048, 64, 1024, True)]
        MO, uMO = A4.alloc([128, 16, NOWN], BF16, "MO")
        for cb in range(4):
            def ev(ms, t0, n, ps, ups, cb=cb):
                self.cp(MO[:, cb * 4 + ms, t0:t0 + n], ps, [ups], [uMO], eng=self.evac_eng())
            self.proj_fm(w_o, 0, 16, cb * 512, 512, lambda kt, t0, n: MIX[:, kt, t0:t0 + n], [uMIX], pchunks, ev)
        A3.reset(); A5.reset(); A6.reset()
        mod2, umod2 = A3.alloc([128, 2, 16, 65], F32, "mod2")
        self.mod, self.umod = mod2, umod2
        self.dma(mod2[:].rearrange("p a b c -> p (a b c)"), self.MOD2_s, writes=[umod2])
        G1, uG1 = A3.alloc([128, D], F32, "G1"); LNG, uLNG = A3.alloc([128, D], F32, "LNG"); LNB, uLNB = A3.alloc([128, D], F32, "LNB")
        self.dma(LNG[:], lnp[0, :].partition_broadcast(128), writes=[uLNG])
        self.dma(LNB[:], lnp[1, :].partition_broadcast(128), writes=[uLNB])
        xn0, uxn0 = A2.alloc([128, D], BF16, "XN0"); xn1, uxn1 = A2.alloc([128, D], BF16, "XN1")
        self.XN = [xn0, xn1]; self.uXN = [uxn0, uxn1]
        H2, uH2 = A1.alloc([128, 16, NOWN], BF16, "H2")
        mv, ust = self.mv, self.ust

        def res_ln(bi, r0, nr, tok0, is_s, src_fm, usrc_fm, src_f32, xsrc, gidx, G, uG, lng, ulng, lnb, ulnb, out_fn):
            xs, ux = XS[0], uXS[0]
            rt, urt = XS[1], uXS[1]
            self.dma(xs[0:nr, :], xsrc, writes=[ux])
            if bi == 0 or is_s:
                self.dma(G[0:nr, :], self.GS[gidx, 128:192, :] if is_s else self.GS[gidx, 0:128, :], writes=[uG])
            if not src_f32:
                for half in range(2):
                    pb = 6 + half
                    psb = PS[pb][:].bitcast(BF16)
                    for j in range(8):
                        kt = half * 8 + j
                        self.tr(psb[0:nr, j * 128:(j + 1) * 128], src_fm(kt)[:, tok0:tok0 + nr], ident_b[:, :], usrc_fm + [uID], [uPS[pb]])
                    self.tt(rt[0:nr, half * 1024:(half + 1) * 1024], psb[0:nr, 0:1024], G[0:nr, half * 1024:(half + 1) * 1024], M,
                            [uPS[pb], uG], [urt])
            else:
                for q4 in range(4):
                    pb = 2 + q4
                    for j in range(4):
                        kt = q4 * 4 + j
                        self.tr(PS[pb][0:nr, j * 128:(j + 1) * 128], src_fm(kt)[:, tok0:tok0 + nr], ident_f[:, :], usrc_fm + [uID], [uPS[pb]])
                    self.tt(rt[0:nr, q4 * 512:(q4 + 1) * 512], PS[pb][0:nr, :], G[0:nr, q4 * 512:(q4 + 1) * 512], M,
                            [uPS[pb], uG], [urt])
            self.stt(xs[0:nr, :], xs[0:nr, :], ALPHA, rt[0:nr, :], M, A, [ux, urt], [ux])
            self.ln_stats(xs, ux, nr, 0)
            self.ts(xs[0:nr, :], xs[0:nr, :], mv[0:nr, 0, 0:1], mv[0:nr, 0, 3:4], S, M, [ux, ust[0]], [ux])
            self.tt(xs[0:nr, :], xs[0:nr, :], lng[0:nr, :], M, [ux, ulng], [ux], eng="gpsimd")
            self.tt(xs[0:nr, :], xs[0:nr, :], lnb[0:nr, :], A, [ux, ulnb], [ux])
            out_fn(xs, ux)

        for bi, (r0, nr, tok0, is_s) in enumerate(blocks):
            def after1(xs, ux, tok0=tok0, nr=nr, is_s=is_s, bi=bi):
                self.dma(X1_s[tok0:tok0 + nr, :], xs[0:nr, :], reads=[ux])
                self.ln_stats(xs, ux, nr, 1)
                xn, uxn = self.XN[bi % 2], self.uXN[bi % 2]
                self.ts(xn[0:nr, :], xs[0:nr, :], mv[0:nr, 1, 0:1], mv[0:nr, 1, 3:4], S, M, [ux, ust[1]], [uxn])
                self.xn_to_fm(xn, uxn, nr, 0, 1, is_s, H2, uH2, tok0)
            res_ln(bi, r0, nr, tok0, is_s, lambda kt: MO[:, kt, :], [uMO], False, self.xa[r0:r0 + nr, :], 0, G1, uG1,
                   LNG, uLNG, LNB, uLNB, after1)
        A2.reset(); A3.reset(); A4.reset()
        HID, uHID = A2.alloc([128, 8, NOWN], BF16, "HID")
        ACCl, uACCl = A4.alloc([128, 8, NOWN], F32, "ACCl")
        ACCh, uACCh = A3.alloc([128, 8, NOWN], F32, "ACCh")
        RT = [A6.alloc([128, 512], F32, "RT%d" % i) for i in range(2)]
        rk = {"n": 0}
        for hc in range(8):
            for cb in range(2):
                def ev1(ms, t0, n, ps, ups, cb=cb):
                    rt_, urt_ = RT[rk["n"] % 2]
                    rk["n"] += 1
                    self.act(rt_[:, 0:n], ps, AF.Relu, [ups], [urt_])
                    self.tt(HID[:, cb * 4 + ms, t0:t0 + n], rt_[:, 0:n], rt_[:, 0:n], M, [urt_], [uHID],
                            eng=("gpsimd" if rk["n"] % 2 else "vector"))
                self.proj_fm(w_ff1, 0, 16, hc * 1024 + cb * 512, 512, lambda kt, t0, n: H2[:, kt, t0:t0 + n], [uH2], pchunks, ev1)
            for cb in range(4):
                def ev2(ms, t0, n, ps, ups, cb=cb, hc=hc):
                    mt = cb * 4 + ms
                    acc, uacc = (ACCl, uACCl) if mt < 8 else (ACCh, uACCh)
                    o = acc[:, mt % 8, t0:t0 + n]
                    if hc == 0:
                        self.cp(o, ps, [ups], [uacc], eng=self.evac_eng())
                    else:
                        self.tt(o, o, ps, A, [ups, uacc], [uacc])
                self.proj_fm(w_ff2, hc * 1024, 8, cb * 512, 512, lambda kt, t0, n: HID[:, kt, t0:t0 + n], [uHID], pchunks, ev2)
        A2.reset(); A5.reset()
        G2, uG2 = A2.alloc([128, D], F32, "G2"); LNG2, uLNG2 = A2.alloc([128, D], F32, "LNG2")
        LNB2, uLNB2 = A5.alloc([128, D], F32, "LNB2")
        self.dma(LNG2[:], lnp[2, :].partition_broadcast(128), writes=[uLNG2])
        self.dma(LNB2[:], lnp[3, :].partition_broadcast(128), writes=[uLNB2])
        uACC = U("accboth")
        for bi, (r0, nr, tok0, is_s) in enumerate(blocks):
            def after2(xs, ux, tok0=tok0, nr=nr, is_s=is_s):
                dst = y_s[0:64, :] if is_s else y_own[tok0:tok0 + nr, :]
                self.dma(dst, xs[0:nr, :], reads=[ux])
            res_ln(bi, r0, nr, tok0, is_s, lambda kt: (ACCl if kt < 8 else ACCh)[:, kt % 8, :], [uACCl, uACCh], True,
                   X1_s[tok0:tok0 + nr, :], 1, G2, uG2, LNG2, uLNG2, LNB2, uLNB2, after2)

    def finish(self):
        self.P.emit(self.st)
        self.st.close()


SLOPES = np.array([2.0 ** (-(hh + 1)) for hh in range(8)], np.float64)


def _bias_tab(h):
    kk = np.arange(128, dtype=np.float64)[:, None, None, None]
    sl = SLOPES[None, :, None, None]
    rel = np.arange(8, dtype=np.float64)[None, None, None, :]
    own = sl * (kk - 256.0 * rel)
    oth = sl * (kk - 256.0 * rel + 128.0 * (1 - 2 * h))
    bt = np.concatenate([np.broadcast_to(own, (128, 8, 1, 8)), np.broadcast_to(oth, (128, 8, 1, 8))], axis=2).copy()
    if h == 0:
        bt[:, :, 1, 0] = -30000.0
    return bt.reshape(128, 128).astype(np.float32)


def _consts():
    tri = (np.arange(128)[None, :] >= np.arange(128)[:, None]).astype(np.float32)
    tri = np.concatenate([tri, tri], axis=1).astype(NPBF)
    ka = np.zeros((4, 17, 128), np.float32)
    for slot in range(16):
        kpos = 128 * slot + np.arange(128)
        ka[0, slot] = (kpos // 256) * 256
        ka[1, slot] = kpos % 256
    ka[0, 16] = 2048.0
    ka[1, 16] = np.arange(128) % 4
    ka[2] = 1.0
    ka[3] = 1.0
    qa = np.zeros((4, 8, 64), np.float32)
    for hh in range(8):
        s8 = 8.0 * SLOPES[hh]
        for c0 in (0, 32):
            for t in range(4):
                qa[:, hh, c0 + t] = [s8, s8, -s8 * 2048.0, -s8 * t]
    ms = np.zeros((64, 16, 64), np.float32)
    for kk in range(64):
        for c0 in (0, 32):
            for t in range(4):
                if kk % 4 <= t:
                    ms[kk, kk // 4, c0 + t] = 1.0
    return tri, ka.reshape(4, 17 * 128).astype(NPBF), qa.reshape(4, 512).astype(NPBF), ms.reshape(64, 1024).astype(NPBF)


TRI_C, KA_C, QA_C, MS_C = _consts()


def _core_maps(inputs, cores, small_pool):
    xp = np.asarray(inputs["x_prompt"])
    xsamp = np.asarray(inputs["x_sample"])
    maps = []
    ident = np.eye(128, dtype=np.float32)
    esel = np.zeros((64, 2, 128), np.float32)
    esel[np.arange(64), 0, np.arange(64)] = 1.0
    esel[np.arange(64), 1, 64 + np.arange(64)] = 1.0
    esel = esel.reshape(64, 256).astype(NPBF)
    for c in cores:
        b, h = c // 2, c % 2
        own = [2 * i + h for i in range(8)]
        oth = [2 * i + 1 - h for i in range(8)]
        xb = xp[b].reshape(16, 128, D)
        xa = np.concatenate([xb[own].reshape(1024, D), xb[oth].reshape(1024, D),
                             xsamp[16 * c:16 * c + 16].reshape(64, D)], axis=0)
        call = np.concatenate([np.asarray(inputs["c_prompt"])[b:b + 1],
                               np.repeat(np.asarray(inputs["c_sample"])[16 * c:16 * c + 16], 4, axis=0)], axis=0)
        m = {
            "xa": np.ascontiguousarray(xa),
            "call": np.ascontiguousarray(call),
            "w_ada": np.asarray(inputs["w_ada"])[0],
            "b_ada": np.asarray(inputs["b_ada"])[0].reshape(96, 128),
            "w_in": np.asarray(inputs["w_in"])[0],
            "w_glu": np.asarray(inputs["w_glu"])[0], "w_up_ssm": np.asarray(inputs["w_up_ssm"])[0],
            "w_up_att": np.asarray(inputs["w_up_att"])[0], "w_o": np.asarray(inputs["w_o"])[0],
            "w_ff1": np.asarray(inputs["w_ff1"])[0], "w_ff2": np.asarray(inputs["w_ff2"])[0],
            "ident_f": ident,
            "ident_b": ident.astype(NPBF),
            "a_re": np.asarray(inputs["ssm_a_re"])[0], "a_im": np.asarray(inputs["ssm_a_im"])[0],
            "log_dt": np.asarray(inputs["ssm_log_dt"])[0].reshape(64, 1),
            "b_re": np.asarray(inputs["ssm_b_re"])[0].reshape(64, 1024), "b_im": np.asarray(inputs["ssm_b_im"])[0].reshape(64, 1024),
            "c_re": np.asarray(inputs["ssm_c_re"])[0].reshape(64, 1024), "c_im": np.asarray(inputs["ssm_c_im"])[0].reshape(64, 1024),
            "dvec": np.asarray(inputs["ssm_d"])[0].reshape(8, 128),
            "hflag": np.full((64, 1), float(h), np.float32),
            "lnp": np.stack([np.asarray(inputs[k])[0] for k in ("ln1_g", "ln1_b", "ln2_g", "ln2_b")]).astype(np.float32),
            "bias_tab": _bias_tab(h), "tri": TRI_C, "ka": KA_C, "qa": QA_C, "msk": MS_C,
            "lamv": np.concatenate([np.asarray(inputs[k])[0] for k in ("lam_q1", "lam_k1", "lam_q2", "lam_k2")]).reshape(1, 256).astype(np.float32),
            "subg": np.asarray(inputs["subln_g"])[0].reshape(128, 1), "iota": np.arange(128, dtype=np.float32).reshape(128, 1),
            "esel": esel,
            "s0T": np.ascontiguousarray(np.stack([np.asarray(inputs["state_ssm_re"])[0, 16 * c:16 * c + 16],
                                                  np.asarray(inputs["state_ssm_im"])[0, 16 * c:16 * c + 16]], 0).transpose(3, 0, 2, 1)),
        }
        pt = np.asarray(inputs["page_table"])[16 * c:16 * c + 16].astype(np.int32)
        if small_pool:
            ck = np.asarray(inputs["cache_k"])[0][pt.reshape(-1)].reshape(256 * 128, 1024)
            cv = np.asarray(inputs["cache_v"])[0][pt.reshape(-1)].reshape(256 * 128, 1024)
            pt = np.arange(256, dtype=np.int32).reshape(16, 16)
        else:
            ck = np.asarray(inputs["cache_k"])[0].reshape(-1, 1024)
            cv = np.asarray(inputs["cache_v"])[0].reshape(-1, 1024)
        m["cache_k"] = ck; m["cache_v"] = cv; m["ptab"] = pt.reshape(1, 256)
        maps.append(m)
    return maps


def _assemble(results, cores, outs):
    for ci, c in enumerate(cores):
        b, h = c // 2, c % 2
        r = results[ci]
        own = [2 * i + h for i in range(8)]
        if "k_own" in r:
            ko = r["k_own"].reshape(8, 128, 8, 128)
            vo = r["v_own"].reshape(8, 128, 8, 128)
            for i, g in enumerate(own):
                outs[2][0, b, g * 128:(g + 1) * 128] = ko[i]
                outs[3][0, b, g * 128:(g + 1) * 128] = vo[i]
            outs[6][0, 16 * c:16 * c + 16] = r["k_s"].reshape(16, 4, 8, 128)
            outs[7][0, 16 * c:16 * c + 16] = r["v_s"].reshape(16, 4, 8, 128)
        if "y_own" in r:
            yo = r["y_own"].reshape(8, 128, D)
            for i, g in enumerate(own):
                outs[0][b, g * 128:(g + 1) * 128] = yo[i]
            outs[1][16 * c:16 * c + 16] = r["y_s"].reshape(16, 4, D)
        if "sfin_o" in r:
            if h == 0:
                outs[4][0, b] = r["sfin_o"][:, 0, :].T
                outs[5][0, b] = r["sfin_o"][:, 1, :].T
            outs[8][0, 16 * c:16 * c + 16] = r["sfs_o"][:, 0].transpose(2, 1, 0)
            outs[9][0, 16 * c:16 * c + 16] = r["sfs_o"][:, 1].transpose(2, 1, 0)
    return outs


def _empty_outs():
    return [np.zeros((4, 2048, 2048), np.float32), np.zeros((128, 4, 2048), np.float32),
            np.zeros((1, 4, 2048, 8, 128), np.float32), np.zeros((1, 4, 2048, 8, 128), np.float32),
            np.zeros((1, 4, 64, 64), np.float32), np.zeros((1, 4, 64, 64), np.float32),
            np.zeros((1, 128, 4, 8, 128), np.float32), np.zeros((1, 128, 4, 8, 128), np.float32),
            np.zeros((1, 128, 64, 64), np.float32), np.zeros((1, 128, 64, 64), np.float32)]


def run(inputs, cores=tuple(range(8)), small_pool=False, stage=99):
    bld = Builder(256 * 128 if small_pool else 2560 * 128, stage)
    bld.build()
    maps = _core_maps(inputs, cores, small_pool)
    maps = [{k: v for k, v in m.items() if k in bld.din} for m in maps]
    res = run_bass_kernel_spmd(bld.nc, maps, core_ids=list(range(len(cores))))
    return _assemble(res.results, cores, _empty_outs())


def kernel(**inputs):
    outs = run(inputs)
    return tuple(outs)
```

```python
import numpy as np
import ml_dtypes
from contextlib import ExitStack
import concourse.bass as bass
import concourse.mybir as mybir
from concourse.alu_op_type import AluOpType as ALU
from concourse.bass_utils import run_bass_kernel_spmd

F32 = mybir.dt.float32
BF16 = mybir.dt.bfloat16
I32 = mybir.dt.int32
U32 = mybir.dt.uint32
AF = mybir.ActivationFunctionType
AX = mybir.AxisListType
NPBF = ml_dtypes.bfloat16

D = 2048
NTOK = 2112
NOWN = 1088
LN_EPS = 1e-5
ALPHA = 2.0 ** 0.25
ENGS = ("sync", "gpsimd", "scalar", "vector", "tensor")
NDMASEM = 8
SAME_ENG_SYNC = True


class U:
    __slots__ = ("name", "w", "r")

    def __init__(self, name=""):
        self.name = name
        self.w = None
        self.r = []


class Op:
    __slots__ = ("eng", "fn", "deps", "dma", "sig", "sem", "val", "idx", "throttle")

    def __init__(self, eng, fn, dma):
        self.eng = eng
        self.fn = fn
        self.dma = dma
        self.deps = set()
        self.sig = False
        self.sem = None
        self.val = None
        self.throttle = None


class Prog:
    def __init__(self, nc):
        self.nc = nc
        self.ops = []

    def op(self, eng, fn, reads=(), writes=(), dma=False):
        o = Op(eng, fn, dma)
        o.idx = len(self.ops)
        for u in reads:
            if u.w is not None:
                o.deps.add(u.w)
        for u in writes:
            if u.w is not None:
                o.deps.add(u.w)
            for r in u.r:
                o.deps.add(r)
        for u in reads:
            u.r.append(o.idx)
        for u in writes:
            u.w = o.idx
            u.r = []
        o.deps.discard(o.idx)
        self.ops.append(o)
        return o

    def emit(self, stack):
        nc = self.nc
        ops = self.ops
        for o in ops:
            if o.dma:
                o.sig = True
        for o in ops:
            for d in o.deps:
                p = ops[d]
                if p.dma:
                    continue
                if p.eng == o.eng and (p.eng == "tensor" or not SAME_ENG_SYNC):
                    continue
                p.sig = True
        esem = {e: stack.enter_context(nc.semaphore("s_" + e)) for e in ENGS}
        dsem = {e: [stack.enter_context(nc.semaphore("d_%s%d" % (e, i))) for i in range(NDMASEM)]
                for e in ("sync", "gpsimd")}
        ecnt = {e: 0 for e in ENGS}
        dcnt = {e: 0 for e in ("sync", "gpsimd")}
        dhist = {e: [] for e in ("sync", "gpsimd")}
        for o in ops:
            if not o.sig:
                continue
            if o.dma:
                n = dcnt[o.eng]
                dcnt[o.eng] += 1
                o.sem = ("d", o.eng, n % NDMASEM)
                o.val = 16 * (n // NDMASEM + 1)
                if n >= NDMASEM:
                    o.throttle = dhist[o.eng][n - NDMASEM]
                dhist[o.eng].append(o.idx)
            else:
                ecnt[o.eng] += 1
                o.sem = ("e", o.eng, 0)
                o.val = ecnt[o.eng]

        def semh(key):
            return esem[key[1]] if key[0] == "e" else dsem[key[1]][key[2]]

        per_eng = {e: [o for o in ops if o.eng == e] for e in ENGS}
        final_waits = {}
        for e in ("sync", "gpsimd"):
            last = {}
            for o in per_eng[e]:
                if o.dma:
                    last[o.sem] = o.val
            final_waits[e] = last

        def run_engine(ename, h):
            seen = {}
            for o in per_eng[ename]:
                need = {}
                dl = list(o.deps)
                if o.throttle is not None:
                    dl.append(o.throttle)
                for d in dl:
                    p = ops[d]
                    if not p.sig:
                        continue
                    if (not p.dma) and p.eng == ename and (ename == "tensor" or not SAME_ENG_SYNC):
                        continue
                    if need.get(p.sem, 0) < p.val:
                        need[p.sem] = p.val
                for k, v in need.items():
                    if seen.get(k, 0) < v:
                        h.wait_ge(semh(k), v)
                        seen[k] = v
                inst = o.fn(h)
                if o.sig:
                    inst.then_inc(semh(o.sem), 16 if o.dma else 1)
            for k, v in final_waits.get(ename, {}).items():
                if seen.get(k, 0) < v:
                    h.wait_ge(semh(k), v)

        block = stack.enter_context(nc.Block())

        @block.sync
        def _(h):
            run_engine("sync", h)

        @block.gpsimd
        def _(h):
            run_engine("gpsimd", h)

        @block.scalar
        def _(h):
            run_engine("scalar", h)

        @block.vector
        def _(h):
            run_engine("vector", h)

        @block.tensor
        def _(h):
            run_engine("tensor", h)


class Arena:
    def __init__(self, bld, name, nbytes):
        self.t = bld.st.enter_context(bld.nc.sbuf_tensor("ar_" + name, [128, nbytes // 4], F32))
        self.nbytes = nbytes
        self.off = 0
        self.cur = []
        self.prev = []
        self.name = name

    def reset(self, keep=0):
        self.prev = self.cur + self.prev
        self.cur = []
        self.off = (keep + 31) // 32 * 32

    def alloc(self, shape, dt, name=""):
        esz = 4 if dt in (F32, I32, U32) else 2
        n = 1
        for s_ in shape[1:]:
            n *= s_
        nb = (n * esz + 31) // 32 * 32
        assert self.off + nb <= self.nbytes, (self.name, name, self.off, nb, self.nbytes)
        v = self.t[0:shape[0], self.off // 4:(self.off + nb) // 4]
        if esz == 2:
            v = v.bitcast(dt)
        elif dt != F32:
            v = v.bitcast(dt)
        v = v[:, 0:n]
        if len(shape) == 3:
            v = v.rearrange("p (a b) -> p a b", a=shape[1])
        elif len(shape) == 4:
            v = v.rearrange("p (a b c) -> p a b c", a=shape[1], b=shape[2])
        elif len(shape) == 5:
            v = v.rearrange("p (a b c d) -> p a b c d", a=shape[1], b=shape[2], c=shape[3])
        self.off += nb
        u = U(self.name + ":" + name)
        for pu in self.prev:
            if pu.w is not None:
                u.r.append(pu.w)
            u.r.extend(pu.r)
        self.cur.append(u)
        return v, u


class Builder:
    def __init__(self, npool_rows, stage):
        self.stage = stage
        self.nc = bass.Bass("TRN2", target_bir_lowering=False)
        self.P = Prog(self.nc)
        self.st = ExitStack()
        self.npool_rows = npool_rows
        self.din = {}
        self.dout = {}
        self._evk = 0
        self._pb = 0

    def inp(self, name, shape, dt=F32):
        t = self.nc.dram_tensor(name, list(shape), dt, kind="ExternalInput").ap()
        self.din[name] = t
        return t

    def outp(self, name, shape, dt=F32):
        t = self.nc.dram_tensor(name, list(shape), dt, kind="ExternalOutput").ap()
        self.dout[name] = t
        return t

    def scratch(self, name, shape, dt=F32):
        return self.nc.dram_tensor(name, list(shape), dt, kind="Internal").ap()

    def sb(self, name, shape, dt=F32):
        return self.st.enter_context(self.nc.sbuf_tensor("sb_" + name, list(shape), dt))

    def dma(self, out, in_, reads=(), writes=(), eng="sync"):
        return self.P.op(eng, lambda h: h.dma_start(out=out, in_=in_), reads, writes, dma=True)

    def mm(self, out, lhsT, rhs, start, stop, reads, writes, **kw):
        return self.P.op("tensor", lambda h: h.matmul(out, lhsT=lhsT, rhs=rhs, start=start, stop=stop, **kw),
                         reads, writes)

    def tr(self, out, in_, ident, reads, writes):
        return self.P.op("tensor", lambda h: h.transpose(out, in_, ident), reads, writes)

    def act(self, out, in_, func, reads, writes, **kw):
        return self.P.op("scalar", lambda h: h.activation(out=out, in_=in_, func=func, **kw), reads, writes)

    def tt(self, out, in0, in1, op, reads, writes, eng="vector"):
        return self.P.op(eng, lambda h: h.tensor_tensor(out=out, in0=in0, in1=in1, op=op), reads, writes)

    def ts(self, out, in0, s1, s2, op0, op1, reads, writes, eng="vector"):
        if op1 is None:
            return self.P.op(eng, lambda h: h.tensor_scalar(out=out, in0=in0, scalar1=s1, scalar2=None, op0=op0),
                             reads, writes)
        return self.P.op(eng, lambda h: h.tensor_scalar(out=out, in0=in0, scalar1=s1, scalar2=s2, op0=op0, op1=op1),
                         reads, writes)

    def stt(self, out, in0, scalar, in1, op0, op1, reads, writes):
        return self.P.op("vector", lambda h: h.scalar_tensor_tensor(out=out, in0=in0, scalar=scalar, in1=in1,
                                                                    op0=op0, op1=op1), reads, writes)

    def cp(self, out, in_, reads, writes, eng="vector"):
        if eng == "scalar":
            return self.P.op(eng, lambda h: h.copy(out=out, in_=in_), reads, writes)
        return self.P.op(eng, lambda h: h.tensor_copy(out=out, in_=in_), reads, writes)

    def evac_eng(self):
        self._evk += 1
        return "scalar" if (self._evk & 1) else "vector"

    def bank(self, n=6):
        self._pb = (self._pb + 1) % n
        return self._pb

    def memset(self, ap, val, writes, eng="vector"):
        return self.P.op(eng, lambda h: h.memset(ap, val), (), writes)

    def cmul(self, or_, oi, ar, ai, br, bi, t1, t2, rd, wr, eng="vector"):
        M, A, S = ALU.mult, ALU.add, ALU.subtract
        self.tt(t1, ar, br, M, rd, wr, eng)
        self.tt(t2, ai, bi, M, rd, wr, eng)
        self.tt(or_, t1, t2, S, rd + wr, wr, eng)
        self.tt(t1, ar, bi, M, rd + wr, wr, eng)
        self.tt(t2, ai, br, M, rd + wr, wr, eng)
        self.tt(oi, t1, t2, A, rd + wr, wr, eng)

    def build(self):
        self.setup()
        self.phase_ada()
        self.phase_ssm_gen()
        self.phase1()
        if self.stage >= 2:
            self.phase_ssm_y()
        if self.stage >= 3:
            self.phase_attn()
        if self.stage >= 4:
            self.phase_rest()
        self.finish()

    def setup(self):
        nc = self.nc
        inp, outp, sb = self.inp, self.outp, self.sb
        self.xa = inp("xa", [NTOK, D])
        self.call = inp("call", [65, D])
        self.w_ada = inp("w_ada", [D, 6 * D])
        self.b_ada = inp("b_ada", [96, 128])
        self.w_in = inp("w_in", [D, 8192])
        self.w_glu = inp("w_glu", [1024, 1024])
        self.w_up_ssm = inp("w_up_ssm", [1024, D])
        self.w_up_att = inp("w_up_att", [1024, D])
        ident_f_d = inp("ident_f", [128, 128])
        ident_b_d = inp("ident_b", [128, 128], BF16)
        self.k_own = outp("k_own", [1024, 1024])
        self.v_own = outp("v_own", [1024, 1024])
        self.k_s = outp("k_s", [64, 1024])
        self.v_s = outp("v_s", [64, 1024])
        self.KT_s = self.scratch("KT_s", [8, 128, NTOK], BF16)
        self.V_s = self.scratch("V_s", [8, NTOK, 128], BF16)
        self.GS = self.scratch("GS", [2, 192, D])
        self.WB = [sb("WB%d" % i, [128, 16, 512], BF16) for i in range(2)]
        self.uWB = [U("WB0"), U("WB1")]
        self.XS = [sb("XS%d" % i, [128, D], F32) for i in range(2)]
        self.uXS = [U("XS0"), U("XS1")]
        self.ident_f = sb("ident_f", [128, 128], F32)
        self.ident_b = sb("ident_b", [128, 128], BF16)
        self.uID = U("ident")
        self.dma(self.ident_f[:], ident_f_d, writes=[self.uID])
        self.dma(self.ident_b[:], ident_b_d, writes=[self.uID])
        self.PS = [self.st.enter_context(nc.psum_tensor("ps%d" % i, [128, 512], F32)) for i in range(8)]
        self.uPS = [U("ps%d" % i) for i in range(8)]
        self.wn = 0
        self.AR1 = Arena(self, "A1", 34816)
        self.AR2 = Arena(self, "A2", 22 * 1024)
        self.AR3 = Arena(self, "A3", 38 * 1024)
        self.AR4 = Arena(self, "A4", 42 * 1024)
        self.AR5 = Arena(self, "A5", 8704)
        self.AR6 = Arena(self, "A6", 4352)

    def wload(self, wview, k0, nkt, c0, ncols):
        i = self.wn % 2
        self.wn += 1
        src = wview[k0:k0 + 128 * nkt, c0:c0 + ncols].rearrange("(kt p) n -> p kt n", p=128)
        self.dma(self.WB[i][:, 0:nkt, 0:ncols], src, writes=[self.uWB[i]], eng="gpsimd")
        return self.WB[i], self.uWB[i]

    def proj_fm(self, wview, k0, nkt, c0, ncols, actf, uact, chunks, evac):
        wb, uwb = self.wload(wview, k0, nkt, c0, ncols)
        PS, uPS = self.PS, self.uPS
        for ms in range(ncols // 128):
            for (t0, n) in chunks:
                pb = self.bank()
                for kt in range(nkt):
                    self.mm(PS[pb][:, 0:n], wb[:, kt, ms * 128:(ms + 1) * 128], actf(kt, t0, n), kt == 0, kt == nkt - 1,
                            [uwb] + uact, [uPS[pb]])
                evac(ms, t0, n, PS[pb][:, 0:n], uPS[pb])

    def phase_ada(self):
        sb, P = self.sb, self.P
        PS, uPS, XS, uXS = self.PS, self.uPS, self.XS, self.uXS
        ident_f, uID = self.ident_f, self.uID
        cT, ucT = self.AR1.alloc([128, 16, 65], BF16, "cT")
        cTp, _ = self.AR1.alloc([128, 16, 128], BF16, "cTp")
        gb, ugb = self.AR1.alloc([128, D], F32, "gb")
        ba = sb("ba", [128, 96], F32)
        uba = U("ba")
        self.mod, self.umod = self.AR5.alloc([128, 2, 16, 65], F32, "mod")
        mod2, umod2 = self.AR3.alloc([128, 2, 16, 65], F32, "mod2")
        self.MOD2_s = self.scratch("MOD2_s", [128, 2 * 16 * 65])
        self.dma(XS[0][0:65, :], self.call, writes=[uXS[0]])
        self.act(XS[0][0:65, :], XS[0][0:65, :], AF.Silu, [uXS[0]], [uXS[0]])
        for kt in range(16):
            b = kt % 2
            self.tr(PS[b][:, 0:65], XS[0][0:65, kt * 128:(kt + 1) * 128], ident_f[0:65, 0:65],
                    [uXS[0], uID], [uPS[b]])
            self.cp(cT[:, kt, :], PS[b][:, 0:65], [uPS[b]], [ucT], eng=self.evac_eng())
        self.cp(cTp[:], cT[:, :, 0:1].to_broadcast([128, 16, 128]), [ucT], [ucT], eng="vector")
        self.dma(XS[1][0:96, 0:128], self.b_ada, writes=[uXS[1]])
        self.tr(PS[2][:, 0:96], XS[1][0:96, 0:128], ident_f[0:96, 0:96], [uXS[1], uID], [uPS[2]])
        self.cp(ba[:], PS[2][:, 0:96], [uPS[2]], [uba])
        b_ada_flat = self.b_ada.rearrange("a b -> (a b)")
        for cb in range(24):
            kind = cb // 4
            wb, uwb = self.wload(self.w_ada, 0, 16, cb * 512, 512)
            if kind in (0, 1, 3, 4):
                mk = {0: 0, 1: 1, 3: 2, 4: 3}[kind]
                for ms in range(4):
                    tile_i = (cb % 4) * 4 + ms
                    pb = 3 + (ms % 2)
                    for kt in range(16):
                        self.mm(PS[pb][:, 0:65], wb[:, kt, ms * 128:(ms + 1) * 128], cT[:, kt, :], kt == 0, kt == 15,
                                [uwb, ucT], [uPS[pb]])
                    col = cb * 4 + ms
                    md, umd = (self.mod, self.umod) if mk < 2 else (mod2, umod2)
                    self.ts(md[:, mk % 2, tile_i, :], PS[pb][:, 0:65], ba[:, col:col + 1],
                            1.0 if kind in (1, 4) else 0.0, ALU.add, ALU.add, [uPS[pb], uba], [umd])
            else:
                gi = 0 if kind == 2 else 1
                if cb % 4 == 0:
                    src = b_ada_flat[kind * D:(kind + 1) * D].partition_broadcast(128)
                    self.dma(gb[:], src, writes=[ugb])
                c0 = (cb % 4) * 512
                for kt in range(16):
                    self.mm(PS[5][:, :], cTp[:, kt, :], wb[:, kt, :], kt == 0, kt == 15, [uwb, ucT], [uPS[5]])
                self.tt(XS[0][:, c0:c0 + 512], PS[5][:, :], gb[:, c0:c0 + 512], ALU.add, [uPS[5], ugb], [uXS[0]])
                for kt in range(16):
                    self.mm(PS[6][0:64, :], cT[:, kt, 1:65], wb[:, kt, :], kt == 0, kt == 15, [uwb, ucT], [uPS[6]])
                self.tt(XS[1][0:64, c0:c0 + 512], PS[6][0:64, :], gb[0:64, c0:c0 + 512], ALU.add, [uPS[6], ugb], [uXS[1]])
                if cb % 4 == 3:
                    self.dma(self.GS[gi, 0:128, :], XS[0][:, :], reads=[uXS[0]])
                    self.dma(self.GS[gi, 128:192, :], XS[1][0:64, :], reads=[uXS[1]])
        self.dma(self.MOD2_s, mod2[:].rearrange("p a b c -> p (a b c)"), reads=[umod2])
        self.AR1.reset()
        self.AR3.reset()
    def phase_ssm_gen(self):
        sb, P, inp = self.sb, self.P, self.inp
        PS, uPS, XS, uXS = self.PS, self.uPS, self.XS, self.uXS
        ident_f, ident_b, uID = self.ident_f, self.ident_b, self.uID
        M, A, S = ALU.mult, ALU.add, ALU.subtract
        a_re_d = inp("a_re", [64, 64]); a_im_d = inp("a_im", [64, 64]); ldt_d = inp("log_dt", [64, 1])
        b_re_d = inp("b_re", [64, 1024]); b_im_d = inp("b_im", [64, 1024])
        c_re_d = inp("c_re", [64, 1024]); c_im_d = inp("c_im", [64, 1024])
        dvec_d = inp("dvec", [8, 128]); hflag_d = inp("hflag", [64, 1]); esel_d = inp("esel", [64, 256], BF16)
        self.KL_s = self.scratch("KL_s", [128, 8 * 4 * 128], BF16)
        self.CAL_s = self.scratch("CAL_s", [128, 64 * 4 * 32], BF16)
        A1, A2, A3 = self.AR1, self.AR2, self.AR3
        ug = U("gen")
        rd, wr = [ug], [ug]
        A1.cur.append(ug); A2.cur.append(ug); A3.cur.append(ug)

        def sm(name):
            v, _ = A1.alloc([64, 64], F32, name)
            return v
        are, aim, xr, xi, e_, cs, sn, Ar, Ai, t1, t2, t3 = [sm("s%d" % i) for i in range(12)]
        den, rden, m1, Fr, Fi, A128r, A128i = [sm("q%d" % i) for i in range(7)]
        Apr, _ = A1.alloc([64, 5, 64], F32, "Apr")
        Api, _ = A1.alloc([64, 5, 64], F32, "Api")
        cols, _ = A1.alloc([64, 4], F32, "cols")
        CAp, uCAp = A1.alloc([128, 64, 5, 16], BF16, "CAp")
        KL, uKL = A1.alloc([128, 8, 4, 128], BF16, "KL")
        brc, _ = A1.alloc([64, 1024], F32, "brc"); bic, _ = A1.alloc([64, 1024], F32, "bic")
        br = brc.rearrange("g (p c) -> g p c", c=16); bi = bic.rearrange("g (p c) -> g p c", c=16)
        cr = brc.rearrange("g (c p) -> g c p", c=16); ci = bic.rearrange("g (c p) -> g c p", c=16)
        Bbr, _ = A2.alloc([64, 64, 16], F32, "Bbr"); Bbi, _ = A2.alloc([64, 64, 16], F32, "Bbi")
        T1, _ = A2.alloc([64, 1024], F32, "T1"); T2, _ = A2.alloc([64, 1024], F32, "T2")
        Wg, uWg = A3.alloc([64, 4, 16, 2, 64], BF16, "Wg")
        CAg, uCAg = A3.alloc([64, 5, 16, 2, 64], BF16, "CAg")
        self.WT, self.uWT = self.AR4.alloc([128, 16, 4, 128], BF16, "WT")
        self.A4p = sb("A4p", [64, 2, 64], F32); self.A128p = sb("A128p", [64, 2, 64], F32); self.uAp = U("Ap")
        self.dcol = sb("dcol", [128, 8], F32); self.udcol = U("dcol")
        self.hf = sb("hf", [64, 1], F32); self.esel = sb("esel", [64, 2, 128], BF16); self.uhf = U("hf")
        for t_, d_ in ((are, a_re_d), (aim, a_im_d)):
            self.dma(t_, d_, writes=wr)
        self.dma(cols[:, 0:1], ldt_d, writes=wr)
        self.dma(br.rearrange("g p c -> g (p c)"), b_re_d, writes=wr)
        self.dma(bi.rearrange("g p c -> g (p c)"), b_im_d, writes=wr)
        self.dma(self.hf[:], hflag_d, writes=[self.uhf])
        self.dma(self.esel[:].rearrange("p a b -> p (a b)"), esel_d, writes=[self.uhf])
        self.memset(cols[:, 1:2], float(np.pi / 2), wr)
        self.memset(cols[:, 2:3], 0.0, wr)
        self.act(cols[:, 3:4], cols[:, 0:1], AF.Exp, rd, wr)
        self.ts(xr, are, cols[:, 3:4], None, M, None, rd, wr)
        self.ts(xi, aim, cols[:, 3:4], None, M, None, rd, wr)
        self.act(e_, xr, AF.Exp, rd, wr, scale=1.0 / 16)
        self.act(cs, xi, AF.Sin, rd, wr, scale=1.0 / 16, bias=cols[:, 1:2])
        self.act(sn, xi, AF.Sin, rd, wr, scale=1.0 / 16, bias=cols[:, 2:3])
        self.tt(Ar, e_, cs, M, rd, wr)
        self.tt(Ai, e_, sn, M, rd, wr)

        def csq(r_, i_):
            self.tt(t1, r_, r_, M, rd, wr)
            self.tt(t2, i_, i_, M, rd, wr)
            self.tt(t3, r_, i_, M, rd, wr)
            self.tt(r_, t1, t2, S, rd, wr)
            self.ts(i_, t3, 2.0, None, M, None, rd, wr)
        for _ in range(4):
            csq(Ar, Ai)
        self.memset(Apr[:, 0, :], 1.0, wr)
        self.memset(Api[:, 0, :], 0.0, wr)
        self.cp(Apr[:, 1, :], Ar, rd, wr)
        self.cp(Api[:, 1, :], Ai, rd, wr)
        self.cmul(Apr[:, 2, :], Api[:, 2, :], Ar, Ai, Ar, Ai, t1, t2, rd, wr)
        self.cmul(Apr[:, 3, :], Api[:, 3, :], Apr[:, 2, :], Api[:, 2, :], Ar, Ai, t1, t2, rd, wr)
        self.cmul(Apr[:, 4, :], Api[:, 4, :], Apr[:, 2, :], Api[:, 2, :], Apr[:, 2, :], Api[:, 2, :], t1, t2, rd, wr)
        self.cp(A128r, Apr[:, 4, :], rd, wr)
        self.cp(A128i, Api[:, 4, :], rd, wr)
        for _ in range(5):
            csq(A128r, A128i)
        self.tt(t1, are, are, M, rd, wr)
        self.tt(t2, aim, aim, M, rd, wr)
        self.tt(den, t1, t2, A, rd, wr)
        P.op("vector", lambda h: h.reciprocal(out=rden, in_=den), rd, wr)
        self.ts(m1, Ar, -1.0, None, A, None, rd, wr)
        self.tt(t1, m1, are, M, rd, wr)
        self.tt(t2, Ai, aim, M, rd, wr)
        self.tt(t1, t1, t2, A, rd, wr)
        self.tt(Fr, t1, rden, M, rd, wr)
        self.tt(t1, Ai, are, M, rd, wr)
        self.tt(t2, m1, aim, M, rd, wr)
        self.tt(t1, t1, t2, S, rd, wr)
        self.tt(Fi, t1, rden, M, rd, wr)
        T1a = T1.rearrange("g (p c) -> g p c", c=16)
        T2a = T2.rearrange("g (p c) -> g p c", c=16)
        Frb = Fr.unsqueeze(2).to_broadcast([64, 64, 16])
        Fib = Fi.unsqueeze(2).to_broadcast([64, 64, 16])
        self.cmul(Bbr, Bbi, Frb, Fib, br, bi, T1a, T2a, rd, wr)
        for i in range(4):
            k = 3 - i
            pr = Apr[:, k, :].unsqueeze(2).to_broadcast([64, 64, 16])
            pi_ = Api[:, k, :].unsqueeze(2).to_broadcast([64, 64, 16])
            o_r = Wg[:, i, :, 0, :].rearrange("g c p -> g p c")
            o_i = Wg[:, i, :, 1, :].rearrange("g c p -> g p c")
            self.cmul(o_r, o_i, pr, pi_, Bbr, Bbi, T1a, T2a, rd, wr + [uWg])
        self.dma(brc, c_re_d, reads=rd, writes=wr)
        self.dma(bic, c_im_d, reads=rd, writes=wr)
        T1c = T1.rearrange("g (c p) -> g c p", c=16)
        T2c = T2.rearrange("g (c p) -> g c p", c=16)
        for k in range(5):
            pr = Apr[:, k, :].unsqueeze(1).to_broadcast([64, 16, 64])
            pi_ = Api[:, k, :].unsqueeze(1).to_broadcast([64, 16, 64])
            self.tt(T1c, cr, pr, M, rd, wr)
            self.tt(T2c, ci, pi_, M, rd, wr)
            self.tt(CAg[:, k, :, 0, :], T1c, T2c, S, rd, wr + [uCAg])
            self.tt(T1c, cr, pi_, M, rd, wr)
            self.tt(T2c, ci, pr, M, rd, wr)
            self.stt(CAg[:, k, :, 1, :], T1c, -1.0, T2c, M, S, rd, wr + [uCAg])
        for k in range(5):
            pb = 6 + (k % 2)
            psb = PS[pb][:].bitcast(BF16)
            for c in range(16):
                self.tr(psb[:, c * 64:(c + 1) * 64], CAg[:, k, c, :, :].rearrange("g r p -> g (r p)"),
                        ident_b[0:64, 0:64], [uCAg, uID], [uPS[pb]])
            self.cp(CAp[:, :, k, :], psb[:, 0:1024].rearrange("p (c g) -> p g c", c=16), [uPS[pb]], [uCAp],
                    eng=self.evac_eng())
        A2.reset()
        Wpp, uWpp = A2.alloc([128, 4, 64, 32], BF16, "Wpp")
        self.memset(Wpp[:].rearrange("p a b c -> p (a b c)"), 0.0, [uWpp], eng="gpsimd")
        for i in range(4):
            pb = 6 + (i % 2)
            psb = PS[pb][:].bitcast(BF16)
            for c in range(16):
                self.tr(psb[:, c * 64:(c + 1) * 64], Wg[:, i, c, :, :].rearrange("g r p -> g (r p)"),
                        ident_b[0:64, 0:64], [uWg, uID], [uPS[pb]])
            self.cp(Wpp[:, i, :, 0:16], psb[:, 0:1024].rearrange("p (c g) -> p g c", c=16), [uPS[pb]], [uWpp],
                    eng=self.evac_eng())
        A3.reset()
        BpK, uBpK = A3.alloc([128, 64, 128], BF16, "BpK")
        CAL, uCAL = A3.alloc([128, 64, 4, 32], BF16, "CAL")
        self.memset(BpK[:].rearrange("p a b -> p (a b)"), 0.0, [uBpK], eng="gpsimd")
        BpKv = BpK.rearrange("p (t g8) (blk c) -> p t g8 blk c", g8=8, c=16)
        Wp3 = Wpp[:, 3, :, :].rearrange("p (t g8) c -> p t g8 c", g8=8)
        for g8 in range(8):
            self.cp(BpKv[:, :, g8, g8, :], Wp3[:, :, g8, 0:16], [uWpp], [uBpK], eng=("vector" if g8 % 2 else "gpsimd"))
        for kk in range(8):
            pb = 6 + (kk % 2)
            psb = PS[pb][:].bitcast(BF16)
            j = 0
            for t in (2 * kk, 2 * kk + 1):
                for i in range(4):
                    self.tr(psb[:, j * 128:(j + 1) * 128], Wpp[:, i, 4 * t:4 * t + 4, :].rearrange("p a b -> p (a b)"),
                            ident_b[:, :], [uWpp, uID], [uPS[pb]])
                    j += 1
            self.cp(self.WT[:, 2 * kk:2 * kk + 2, :, :].rearrange("p a b c -> p (a b c)"), psb[:, 0:1024],
                    [uPS[pb]], [self.uWT], eng=self.evac_eng())
        for t8 in range(8):
            pb = self.bank()
            pv = PS[pb][:, :].rearrange("p (l c) -> p l c", l=4)
            for g8 in range(8):
                g = 8 * t8 + g8
                self.mm(pv[:, :, 16 * g8:16 * g8 + 16], BpK[:, g, :], CAp[:, g, 0:4, :], True, True,
                        [uBpK, uCAp], [uPS[pb]])
            self.cp(KL[:, t8, :, :].rearrange("p l c -> p (l c)"), PS[pb][:, :], [uPS[pb]], [uKL], eng=self.evac_eng())
        self.dma(self.KL_s, KL[:].rearrange("p a b c -> p (a b c)"), reads=[uKL])
        self.memset(CAL[:].rearrange("p a b c -> p (a b c)"), 0.0, [uCAL], eng="gpsimd")
        CALv = CAL.rearrange("p (gp two) j c -> p gp two j c", two=2)
        CApv = CAp.rearrange("p (gp two) k c -> p gp two k c", two=2)
        self.cp(CALv[:, :, 0, :, 0:16], CApv[:, :, 0, 1:5, :], [uCAp], [uCAL], eng="vector")
        self.cp(CALv[:, :, 1, :, 16:32], CApv[:, :, 1, 1:5, :], [uCAp], [uCAL], eng="gpsimd")
        self.dma(self.CAL_s, CAL[:].rearrange("p a b c -> p (a b c)"), reads=[uCAL])
        for idx, (src_r, src_i, dst) in enumerate(((Apr[:, 4, :], Api[:, 4, :], self.A4p), (A128r, A128i, self.A128p))):
            for ri, src in enumerate((src_r, src_i)):
                pb = self.bank()
                self.tr(PS[pb][0:64, 0:64], src, ident_f[0:64, 0:64], rd + [uID], [uPS[pb]])
                self.cp(dst[:, ri, :], PS[pb][0:64, 0:64], [uPS[pb]], [self.uAp], eng=self.evac_eng())
        self.dma(XS[1][0:8, 0:128], dvec_d, writes=[uXS[1]])
        pb = self.bank()
        self.tr(PS[pb][:, 0:8], XS[1][0:8, 0:128], ident_f[0:8, 0:8], [uXS[1], uID], [uPS[pb]])
        self.cp(self.dcol[:], PS[pb][:, 0:8], [uPS[pb]], [self.udcol])
        A1.reset(); A2.reset(); A3.reset()
        self.TBL_s = self.scratch("TBL_s", [64, 4, 64, 32])
        TB, uTB = A3.alloc([64, 4, 64, 32], F32, "TB")
        q1, _ = A2.alloc([64, 64, 16], F32, "q1"); q2, _ = A2.alloc([64, 64, 16], F32, "q2")
        pw, _ = A2.alloc([64, 4, 64], F32, "pw")
        s1, _ = A2.alloc([64, 64], F32, "s1"); s2, _ = A2.alloc([64, 64], F32, "s2"); s3, _ = A2.alloc([64, 64], F32, "s3")
        ut = U("tbl")
        A2.cur.append(ut)
        rdt, wrt = [ut, self.uAp, uTB], [ut]
        a4r, a4i = self.A4p[:, 0, :], self.A4p[:, 1, :]
        self.tt(s1, a4r, a4r, M, rdt, wrt); self.tt(s2, a4i, a4i, M, rdt, wrt); self.tt(s1, s1, s2, A, rdt, wrt)
        P.op("vector", lambda h: h.reciprocal(out=s2, in_=s1), rdt, wrt)
        self.cp(pw[:, 0, :], a4r, rdt, wrt); self.cp(pw[:, 1, :], a4i, rdt, wrt)
        self.tt(pw[:, 2, :], a4r, s2, M, rdt, wrt)
        self.stt(pw[:, 3, :], a4i, -1.0, s2, M, M, rdt, wrt)
        self.memset(TB[:, 0, :, 0:1], 1.0, [uTB]); self.memset(TB[:, 1, :, 0:1], 0.0, [uTB])
        self.cp(TB[:, 2, :, 0:1], pw[:, 2, :].unsqueeze(2), rdt, [uTB]); self.cp(TB[:, 3, :, 0:1], pw[:, 3, :].unsqueeze(2), rdt, [uTB])
        for k in (1, 2, 4, 8, 16):
            for base in (0, 2):
                br_ = pw[:, base, :].unsqueeze(2).to_broadcast([64, 64, k])
                bi_ = pw[:, base + 1, :].unsqueeze(2).to_broadcast([64, 64, k])
                self.cmul(TB[:, base, :, k:2 * k], TB[:, base + 1, :, k:2 * k], TB[:, base, :, 0:k], TB[:, base + 1, :, 0:k],
                          br_, bi_, q1[:, :, 0:k], q2[:, :, 0:k], rdt, wrt + [uTB])
            if k < 16:
                for base in (0, 2):
                    r_, i_ = pw[:, base, :], pw[:, base + 1, :]
                    self.tt(s1, r_, r_, M, rdt, wrt); self.tt(s2, i_, i_, M, rdt, wrt); self.tt(s3, r_, i_, M, rdt, wrt)
                    self.tt(r_, s1, s2, S, rdt, wrt); self.ts(i_, s3, 2.0, None, M, None, rdt, wrt)
        self.dma(self.TBL_s.rearrange("p a b c -> p (a b c)"), TB[:].rearrange("p a b c -> p (a b c)"), reads=[uTB])
        A2.reset(); A3.reset()
    def ln_block(self, src_rows, nrows, par, mk_sh, mk_sc, sample, dst, dst_u, dst_tok0):
        P = self.P
        PS, uPS = self.PS, self.uPS
        xs, ux = self.XS[par], self.uXS[par]
        xn, uxn = self.XN[par], self.uXN[par]
        stt, mv, ust = self.stt_t, self.mv, self.ust
        mod, umod = self.mod, self.umod
        self.dma(xs[0:nrows, :], src_rows, writes=[ux])
        self.ln_stats(xs, ux, nrows, par)
        self.ts(xn[0:nrows, :], xs[0:nrows, :], mv[0:nrows, par, 0:1], mv[0:nrows, par, 3:4], ALU.subtract, ALU.mult,
                [ux, ust[par]], [uxn])
        self.xn_to_fm(xn, uxn, nrows, mk_sh, mk_sc, sample, dst, dst_u, dst_tok0)

    def ln_stats(self, xs, ux, nrows, par):
        P = self.P
        stt, mv, ust = self.stt_t, self.mv, self.ust
        for c in range(4):
            P.op("vector", lambda h, c=c: h.bn_stats(out=stt[0:nrows, par, c, :], in_=xs[0:nrows, c * 512:(c + 1) * 512]),
                 [ux], [ust[par]])
        P.op("vector", lambda h: h.bn_aggr(out=mv[0:nrows, par, 0:2], in_=stt[0:nrows, par, :, :].rearrange("p a b -> p (a b)")),
             [ust[par]], [ust[par]])
        self.act(mv[0:nrows, par, 2:3], mv[0:nrows, par, 1:2], AF.Sqrt, [ust[par], self.ueps], [ust[par]],
                 bias=self.eps_t[0:nrows, :], scale=1.0)
        P.op("vector", lambda h: h.reciprocal(out=mv[0:nrows, par, 3:4], in_=mv[0:nrows, par, 2:3]), [ust[par]], [ust[par]])

    def xn_to_fm(self, xn, uxn, nrows, mk_sh, mk_sc, sample, dst, dst_u, dst_tok0):
        PS, uPS = self.PS, self.uPS
        mod, umod = self.mod, self.umod
        for half in range(2):
            pb = 6 + half
            psb = PS[pb][:].bitcast(BF16)
            for j in range(8):
                kt = half * 8 + j
                self.tr(psb[:, j * 128:j * 128 + nrows], xn[0:nrows, kt * 128:(kt + 1) * 128],
                        self.ident_b[0:nrows, 0:nrows], [uxn, self.uID], [uPS[pb]])
            for j in range(8):
                kt = half * 8 + j
                src = psb[:, j * 128:j * 128 + nrows]
                o = dst[:, kt, dst_tok0:dst_tok0 + nrows]
                if not sample:
                    if j % 2 == 0:
                        self.act(o, src, AF.Identity, [uPS[pb], umod], [dst_u],
                                 scale=mod[:, mk_sc, kt, 0:1], bias=mod[:, mk_sh, kt, 0:1])
                    else:
                        self.ts(o, src, mod[:, mk_sc, kt, 0:1], mod[:, mk_sh, kt, 0:1], ALU.mult, ALU.add,
                                [uPS[pb], umod], [dst_u])
                else:
                    self.tt(o, src, mod[:, mk_sc, kt, 1:65], ALU.mult, [uPS[pb], umod], [dst_u])
                    self.tt(o, o, mod[:, mk_sh, kt, 1:65], ALU.add, [umod, dst_u], [dst_u], eng="gpsimd")

    def phase1(self):
        sb, P = self.sb, self.P
        PS, uPS, XS, uXS = self.PS, self.uPS, self.XS, self.uXS
        ident_f, ident_b, uID = self.ident_f, self.ident_b, self.uID
        xa, w_in = self.xa, self.w_in
        M, A, S = ALU.mult, ALU.add, ALU.subtract
        A4 = self.AR4
        xn0, uxn0 = A4.alloc([128, D], BF16, "XN0"); xn1, uxn1 = A4.alloc([128, D], BF16, "XN1")
        self.XN = [xn0, xn1]; self.uXN = [uxn0, uxn1]
        self.stt_t = sb("stt", [128, 2, 4, 6], F32)
        self.mv = sb("mv", [128, 2, 4], F32)
        self.ust = [U("st0"), U("st1")]
        self.eps_t = sb("eps_t", [128, 1], F32)
        self.ueps = U("eps")
        self.memset(self.eps_t[:], LN_EPS, [self.ueps])
        HG, uHG = self.AR1.alloc([128, 16, NOWN], BF16, "HG")
        self.HG, self.uHG = HG, uHG
        ZB, uZB = self.AR2.alloc([64, 2, 8, 256], F32, "ZB")
        ZBs, _ = self.AR2.alloc([64, 2, 8, 16], F32, "ZBs")
        self.AR2.cur.append(uZB)
        UTb = [self.AR2.alloc([128, 1024], BF16, "UTb%d" % i) for i in range(2)]
        UO, uUO = self.AR3.alloc([128, 8, NOWN], BF16, "UO")
        self.UO, self.uUO = UO, uUO
        UPb = [self.AR3.alloc([128, 2, NOWN], BF16, "UPb%d" % i) for i in range(2)]
        SM, uSM = self.AR3.alloc([64, 2, 8, 272], BF16, "SM")
        SMBb, uSMBb = self.AR6.alloc([128, 8 * 272], BF16, "SMBb")
        KST = [XS[0][:, 0:1024], XS[1][:, 0:1024]]
        uKST = uXS
        KTB, uKTB = A4.alloc([128, 8, 128], BF16, "KTB")
        VB, uVB = A4.alloc([128, 1024], BF16, "VB")
        for i in range(2):
            self.memset(UPb[i][0][:].rearrange("p a b -> p (a b)"), 0.0, [UPb[i][1]], eng="gpsimd")
        ZBLK, uZBLK = A4.alloc([64, 2, 64, 8], F32, "ZBLKo")
        ZOWN, _ = A4.alloc([64, 2, 8, 8], F32, "ZOWN")
        A4.cur.append(uZBLK)
        uR = U("Rdummy")
        TBb, uTBb = self.AR3.alloc([64, 4, 8, 32], F32, "TBb")
        C31, uC31 = A4.alloc([64, 2, 8, 8], F32, "C31")
        BS, _ = A4.alloc([64, 2, 64], F32, "BS"); A4.cur.append(uC31)
        onec, _ = A4.alloc([64, 1], F32, "onec")
        self.memset(onec[:], 1.0, [uC31]); self.memset(BS[:].rearrange("p a b -> p (a b)"), 0.0, [uC31])
        uP2 = U("pass2"); A4.cur.append(uP2)
        F1, _ = A4.alloc([64, 2, 8, 8], F32, "F1"); F2, _ = A4.alloc([64, 2, 8, 8], F32, "F2")
        WW, _ = A4.alloc([64, 2, 8, 8], F32, "WW"); SALL, _ = A4.alloc([64, 2, 8, 9], F32, "SALL")
        A256, _ = A4.alloc([64, 2, 8], F32, "A256")
        TT, uTT = A4.alloc([64, 4, 8, 16], F32, "TT")
        SOWN, uSOWN = A4.alloc([64, 2, 8, 8], F32, "SOWN")
        SFIN, uSFIN = A4.alloc([64, 2, 64], F32, "SFIN")
        S0b, uS0 = A4.alloc([64, 2, 8, 16], F32, "S0b")
        SFSb, uSFS = A4.alloc([64, 2, 8, 16], F32, "SFSb")
        self.SMB_s = self.scratch("SMB_s", [8, 128, 8 * 272], BF16)
        A4p, A128p, uAp = self.A4p, self.A128p, self.uAp
        s0T = self.inp("s0T", [64, 2, 64, 16])
        sfin_o = self.outp("sfin_o", [64, 2, 64])
        sfs_o = self.outp("sfs_o", [64, 2, 64, 16])

        def cstep(ar, ai, sr, si, shape_n):
            g_, l_ = shape_n
            t = [TT[:, k, 0:g_, 0:l_] for k in range(4)]
            rdd = [uR, uAp, uZB, uZBLK, uSOWN, uSFIN, uSFS, uS0, uC31, uP2]
            self.tt(t[0], ar, sr, M, rdd, [uTT])
            self.tt(t[1], ai, si, M, rdd, [uTT])
            self.tt(t[2], ar, si, M, rdd, [uTT], eng="gpsimd")
            self.tt(t[3], ai, sr, M, rdd, [uTT], eng="gpsimd")
            self.tt(t[0], t[0], t[1], S, [uTT], [uTT])
            self.tt(t[2], t[2], t[3], A, [uTT], [uTT], eng="gpsimd")
            return t[0], t[2]

        def ssm_batch(gb, usrc, uusrc, ntok, is_A):
            nm = ntok // 4
            up, uup = UPb[gb % 2]
            gsl = slice(8 * gb, 8 * gb + 8)
            self.dma(TBb[:], self.TBL_s[:, :, gsl, :], writes=[uTBb])
            for g8 in range(8):
                self.dma(up[32 * (g8 % 4):32 * (g8 % 4) + 16, g8 // 4, 0:ntok], usrc[16 * g8:16 * g8 + 16, 0:ntok],
                         reads=[uusrc], writes=[uup])
            for g8 in range(8):
                q, tt_ = g8 % 4, g8 // 4
                t = 2 * gb + tt_
                upv = up[32 * q:32 * q + 32, tt_, 0:ntok].rearrange("p (m i) -> p m i", i=4)
                for ri in range(2):
                    pb = self.bank()
                    for i in range(4):
                        self.mm(PS[pb][0:64, 0:nm], self.WT[32 * q:32 * q + 32, t, i, 64 * ri:64 * ri + 64], upv[:, :, i],
                                i == 0, i == 3, [self.uWT, uup], [uPS[pb]], tile_position=(32 * q, 0))
                    self.cp(ZB[:, ri, g8, :], PS[pb][0:64, 0:256], [uPS[pb]], [uZB], eng=self.evac_eng())
                    if is_A:
                        self.cp(ZBs[:, ri, g8, :], PS[pb][0:64, 256:272], [uPS[pb]], [uZB], eng=self.evac_eng())
            tA = self.XN[0][0:64, :].bitcast(F32); utA = self.uXN[0]
            tB = self.XN[1][0:64, :].bitcast(F32); utB = self.uXN[1]
            for hf in range(2):
                g4 = slice(4 * hf, 4 * hf + 4)
                Zr = ZB[:, 0, g4, :].rearrange("p g (b m) -> p g b m", m=32)
                Zi = ZB[:, 1, g4, :].rearrange("p g (b m) -> p g b m", m=32)
                nr_ = TBb[:, 2, g4, :].unsqueeze(2).to_broadcast([64, 4, 8, 32])
                ni_ = TBb[:, 3, g4, :].unsqueeze(2).to_broadcast([64, 4, 8, 32])
                t1 = tA.rearrange("p (g b m) -> p g b m", g=4, b=8)
                t2 = tB.rearrange("p (g b m) -> p g b m", g=4, b=8)
                self.tt(t1, nr_, Zr, M, [uTBb, uZB], [utA])
                self.tt(t2, ni_, Zi, M, [uTBb, uZB], [utB], eng="gpsimd")
                self.tt(t1, t1, t2, S, [utA, utB], [utA])
                self.tt(t2, ni_, Zr, M, [uTBb, uZB, utA], [utB], eng="gpsimd")
                self.cp(Zr, t1, [utA, utB], [uZB])
                self.tt(t1, nr_, Zi, M, [uTBb, uZB], [utA])
                self.tt(Zi, t1, t2, A, [utA, utB], [uZB])
            a128r = A128p[:, 0, gsl].unsqueeze(2).to_broadcast([64, 8, 8])
            a128i = A128p[:, 1, gsl].unsqueeze(2).to_broadcast([64, 8, 8])
            if not is_A:
                for ri in range(2):
                    P.op("vector", lambda h, ri=ri: h.tensor_reduce(out=C31[:, ri, :, :], in_=ZB[:, ri, :, :].rearrange("p g (b m) -> p g b m", m=32),
                                                                     axis=AX.X, op=A), [uZB], [uC31])
                pr, pi_ = cstep(a128r, a128i, C31[:, 0, :, :], C31[:, 1, :, :], (8, 8))
                self.cp(ZBLK[:, 0, gsl, :], pr, [uTT], [uZBLK])
                self.cp(ZBLK[:, 1, gsl, :], pi_, [uTT], [uZBLK], eng="gpsimd")
                return
            for ri in range(2):
                flat = ZB[:, ri, :, :].rearrange("p g m -> p (g m)")
                P.op("vector", lambda h, flat=flat: h.tensor_tensor_scan(out=flat, data0=onec[:, 0:1].to_broadcast([64, 2048]), data1=flat,
                                                                          initial=0.0, op0=M, op1=A), [uZB, uC31], [uZB])
                seg = flat.rearrange("p (s m) -> p s m", m=32)
                self.cp(BS[:, ri, 1:64], seg[:, 0:63, 31], [uZB], [uC31], eng="gpsimd")
                self.tt(seg, seg, BS[:, ri, :].unsqueeze(2).to_broadcast([64, 64, 32]), S, [uZB, uC31], [uZB])
            Cv = [ZB[:, ri, :, :].rearrange("p g (b m) -> p g b m", m=32) for ri in range(2)]
            pr, pi_ = cstep(a128r, a128i, Cv[0][:, :, :, 31], Cv[1][:, :, :, 31], (8, 8))
            self.cp(ZOWN[:, 0, :, :], pr, [uTT], [uZBLK])
            self.cp(ZOWN[:, 1, :, :], pi_, [uTT], [uZBLK], eng="gpsimd")
            hf_ = self.hf[:, 0:1]
            for ri in range(2):
                X = ZOWN[:, ri, :, :]
                Y = ZBLK[:, ri, gsl, :]
                self.tt(F2[:, ri], Y, X, S, [uZBLK], [uP2])
                self.stt(F1[:, ri], F2[:, ri], hf_, X, M, A, [uP2, self.uhf, uZBLK], [uP2])
                self.tt(F2[:, ri], X, Y, A, [uZBLK, uP2], [uP2])
                self.tt(F2[:, ri], F2[:, ri], F1[:, ri], S, [uP2], [uP2])
            pr, pi_ = cstep(a128r, a128i, F1[:, 0], F1[:, 1], (8, 8))
            self.tt(WW[:, 0], pr, F2[:, 0], A, [uTT, uP2], [uP2])
            self.tt(WW[:, 1], pi_, F2[:, 1], A, [uTT, uP2], [uP2], eng="gpsimd")
            t = [TT[:, k, :, 0] for k in range(4)]
            b128r, b128i = A128p[:, 0, gsl], A128p[:, 1, gsl]
            self.tt(t[0], b128r, b128r, M, [uAp], [uTT]); self.tt(t[1], b128i, b128i, M, [uAp], [uTT])
            self.tt(A256[:, 0, :], t[0], t[1], S, [uTT], [uP2])
            self.tt(t[2], b128r, b128i, M, [uAp], [uTT])
            self.ts(A256[:, 1, :], t[2], 2.0, None, M, None, [uTT], [uP2])
            self.memset(SALL[:, :, :, 0:1], 0.0, [uP2])
            for i in range(8):
                sr, si = SALL[:, 0, :, i], SALL[:, 1, :, i]
                self.tt(t[0], A256[:, 0, :], sr, M, [uP2], [uTT]); self.tt(t[1], A256[:, 1, :], si, M, [uP2], [uTT])
                self.tt(t[2], A256[:, 0, :], si, M, [uP2], [uTT], eng="gpsimd"); self.tt(t[3], A256[:, 1, :], sr, M, [uP2], [uTT], eng="gpsimd")
                self.tt(t[0], t[0], t[1], S, [uTT], [uTT]); self.tt(t[2], t[2], t[3], A, [uTT], [uTT], eng="gpsimd")
                self.tt(SALL[:, 0, :, i + 1], t[0], WW[:, 0, :, i], A, [uTT, uP2], [uP2])
                self.tt(SALL[:, 1, :, i + 1], t[2], WW[:, 1, :, i], A, [uTT, uP2], [uP2], eng="gpsimd")
            pr, pi_ = cstep(a128r, a128i, SALL[:, 0, :, 0:8], SALL[:, 1, :, 0:8], (8, 8))
            for ri, pp in ((0, pr), (1, pi_)):
                self.tt(F2[:, ri], pp, F1[:, ri], A, [uTT, uP2], [uP2])
                self.tt(F2[:, ri], F2[:, ri], SALL[:, ri, :, 0:8], S, [uP2], [uP2])
                self.stt(SOWN[:, ri, :, :], F2[:, ri], hf_, SALL[:, ri, :, 0:8], M, A, [uP2, self.uhf], [uSOWN])
            self.cp(SFIN[:, 0, gsl], SALL[:, 0, :, 8], [uP2], [uSFIN])
            self.cp(SFIN[:, 1, gsl], SALL[:, 1, :, 8], [uP2], [uSFIN])
            for pc in range(4):
                g2 = slice(2 * pc, 2 * pc + 2)
                Dr = tA[:, 0:512].rearrange("p (g b m) -> p g b m", g=2, b=8)
                Di = tA[:, 512:1024].rearrange("p (g b m) -> p g b m", g=2, b=8)
                p1 = tB[:, 0:512].rearrange("p (g b m) -> p g b m", g=2, b=8)
                p2 = tB[:, 512:1024].rearrange("p (g b m) -> p g b m", g=2, b=8)
                for ri, Dx in ((0, Dr), (1, Di)):
                    e_ = "vector" if ri == 0 else "gpsimd"
                    self.tt(Dx[:, :, :, 1:32], Cv[ri][:, g2, :, 0:31], SOWN[:, ri, g2, :].unsqueeze(3).to_broadcast([64, 2, 8, 31]), A,
                            [uZB, uSOWN], [utA], eng=e_)
                    self.cp(Dx[:, :, :, 0:1], SOWN[:, ri, g2, :].unsqueeze(3), [uSOWN], [utA], eng=e_)
                tr_ = TBb[:, 0, g2, :].unsqueeze(2).to_broadcast([64, 2, 8, 32])
                ti_ = TBb[:, 1, g2, :].unsqueeze(2).to_broadcast([64, 2, 8, 32])
                smr = SM[:, 0, g2, 0:256].rearrange("p g (b m) -> p g b m", m=32)
                smi = SM[:, 1, g2, 0:256].rearrange("p g (b m) -> p g b m", m=32)
                self.tt(p1, tr_, Dr, M, [uTBb, utA], [utB]); self.tt(p2, ti_, Di, M, [uTBb, utA], [utB], eng="gpsimd")
                self.tt(smr, p1, p2, S, [utB], [uSM])
                self.tt(p1, tr_, Di, M, [uTBb, utA, uSM], [utB]); self.tt(p2, ti_, Dr, M, [uTBb, utA, uSM], [utB], eng="gpsimd")
                self.tt(smi, p1, p2, A, [utB], [uSM])
            self.dma(S0b[:], s0T[:, :, gsl, :], writes=[uS0])
            s0r, s0i = S0b[:, 0, :, :], S0b[:, 1, :, :]
            self.cp(SM[:, 0, :, 256:272], s0r, [uS0], [uSM], eng="scalar")
            self.cp(SM[:, 1, :, 256:272], s0i, [uS0], [uSM], eng="scalar")
            a4r16 = A4p[:, 0, gsl].unsqueeze(2).to_broadcast([64, 8, 16])
            a4i16 = A4p[:, 1, gsl].unsqueeze(2).to_broadcast([64, 8, 16])
            pr, pi_ = cstep(a4r16, a4i16, s0r, s0i, (8, 16))
            self.tt(SFSb[:, 0, :, :], pr, ZBs[:, 0, :, :], A, [uTT, uZB], [uSFS])
            self.tt(SFSb[:, 1, :, :], pi_, ZBs[:, 1, :, :], A, [uTT, uZB], [uSFS], eng="gpsimd")
            self.dma(sfs_o[:, :, gsl, :], SFSb[:], reads=[uSFS])
            SMf = SM[:].rearrange("p r g m -> p r (g m)")
            for c0 in range(0, 8 * 272, 512):
                n = min(512, 8 * 272 - c0)
                pb = self.bank()
                self.mm(PS[pb][:, 0:n], self.esel[:, 0, :], SMf[:, 0, c0:c0 + n], True, False, [self.uhf, uSM], [uPS[pb]])
                self.mm(PS[pb][:, 0:n], self.esel[:, 1, :], SMf[:, 1, c0:c0 + n], False, True, [self.uhf, uSM], [uPS[pb]])
                self.cp(SMBb[:, c0:c0 + n], PS[pb][:, 0:n], [uPS[pb]], [uSMBb], eng=self.evac_eng())
            self.dma(self.SMB_s[gb], SMBb[:, :], reads=[uSMBb])

        groupB = [(1024 + 128 * j, 128, False, False, None) for j in range(8)]
        groupA = [(128 * j, 128, False, True, 128 * j) for j in range(8)] + [(2048, 64, True, True, 1024)]
        for gi, grp in enumerate((groupB, groupA)):
            is_A = gi == 1
            toks = 0
            offs = []
            for bi, (r0, nr, is_s, is_own, orow) in enumerate(grp):
                self.ln_block(xa[r0:r0 + nr, :], nr, bi % 2, 0, 1, is_s, HG, uHG, toks)
                offs.append(toks)
                toks += nr
            ntok = toks
            chunks = [(0, 512), (512, 512)] + ([(1024, 64)] if is_A else [])
            for cb in range(2):
                def ev(ms, t0, n, ps, ups, cb=cb):
                    gb = cb * 4 + ms
                    if is_A:
                        self.cp(UO[:, gb, t0:t0 + n], ps, [ups], [uUO], eng=self.evac_eng())
                        if t0 + n == ntok:
                            ssm_batch(gb, UO[:, gb, :], uUO, ntok, True)
                    else:
                        ut, uut = UTb[gb % 2]
                        self.cp(ut[:, t0:t0 + n], ps, [ups], [uut], eng=self.evac_eng())
                        if t0 + n == ntok:
                            ssm_batch(gb, ut, uut, ntok, False)
                self.proj_fm(w_in, 0, 16, cb * 512, 512, lambda kt, t0, n: HG[:, kt, t0:t0 + n], [uHG], chunks, ev)
            for which, cbase in (("k", 2048), ("v", 3072)):
                wbs = [self.wload(w_in, 0, 16, cbase + cb * 512, 512) for cb in range(2)]
                for bi, (r0, nr, is_s, is_own, orow) in enumerate(grp):
                    st_i = bi % 2
                    for cb in range(2):
                        wb, uwb = wbs[cb]
                        pb = self.bank()
                        for kt in range(16):
                            self.mm(PS[pb][0:nr, :], HG[:, kt, offs[bi]:offs[bi] + nr], wb[:, kt, :], kt == 0, kt == 15,
                                    [uHG, uwb], [uPS[pb]])
                        self.cp(KST[st_i][0:nr, cb * 512:(cb + 1) * 512], PS[pb][0:nr, :], [uPS[pb]], [uKST[st_i]],
                                eng=self.evac_eng())
                    tok0 = (r0 if not is_s else 2048)
                    if is_own:
                        if is_s:
                            dst = (self.k_s if which == "k" else self.v_s)[0:64, :]
                        else:
                            dst = (self.k_own if which == "k" else self.v_own)[orow:orow + 128, :]
                        self.dma(dst, KST[st_i][0:nr, :], reads=[uKST[st_i]])
                    if which == "k":
                        for hh in range(8):
                            pq = 6 + (hh % 2)
                            self.tr(PS[pq][:, 0:nr], KST[st_i][0:nr, hh * 128:(hh + 1) * 128],
                                    ident_f[0:nr, 0:nr], [uKST[st_i], uID], [uPS[pq]])
                            self.cp(KTB[:, hh, 0:nr], PS[pq][:, 0:nr], [uPS[pq]], [uKTB], eng=self.evac_eng())
                        self.dma(self.KT_s[:, :, tok0:tok0 + nr].rearrange("h p t -> p h t"), KTB[:, :, 0:nr], reads=[uKTB])
                    else:
                        self.cp(VB[0:nr, :], KST[st_i][0:nr, :], [uKST[st_i]], [uVB], eng="gpsimd")
                        self.dma(self.V_s[:, tok0:tok0 + nr, :].rearrange("h t d -> t h d"),
                                 VB[0:nr, :].rearrange("t (h d) -> t h d", h=8), reads=[uVB])
        self.dma(sfin_o, SFIN[:], reads=[uSFIN])
        self.AR2.reset()

    def phase_ssm_y(self):
        sb, P = self.sb, self.P
        PS, uPS = self.PS, self.uPS
        M, A, S = ALU.mult, ALU.add, ALU.subtract
        UO, uUO, HG, uHG = self.UO, self.uUO, self.HG, self.uHG
        self.AR3.reset(keep=8 * NOWN * 2)
        self.AR4.reset()
        ZACT, uZACT = self.AR3.alloc([128, 8, NOWN], BF16, "ZACT")
        GLUO, uGLUO = self.AR4.alloc([128, 8, NOWN], BF16, "GLUO")
        KLt = [self.AR2.alloc([128, 4, 128], BF16, "KLt%d" % i) for i in range(2)]
        CALt = [self.AR2.alloc([128, 8, 4, 32], BF16, "CALt%d" % i) for i in range(2)]
        SMBt = [self.AR2.alloc([128, 8, 272], BF16, "SMBt%d" % i) for i in range(2)]
        Yt, uYt = self.AR2.alloc([128, 512], F32, "Yt")
        Tt, uTt = self.AR2.alloc([128, 512], F32, "Tt")
        chunks = [(0, 512, 0), (512, 512, 128), (1024, 64, 256)]
        for t8 in range(8):
            kl, ukl = KLt[t8 % 2]; cal, ucal = CALt[t8 % 2]; smb, usmb = SMBt[t8 % 2]
            self.dma(kl[:].rearrange("p a b -> p (a b)"), self.KL_s[:, t8 * 512:(t8 + 1) * 512], writes=[ukl])
            self.dma(cal[:].rearrange("p a b c -> p (a b c)"), self.CAL_s[:, t8 * 1024:(t8 + 1) * 1024], writes=[ucal])
            self.dma(smb[:].rearrange("p a b -> p (a b)"), self.SMB_s[t8], writes=[usmb])
            for (t0, n, m0) in chunks:
                nm = n // 4
                pb = self.bank()
                pv = PS[pb][:, 0:n].rearrange("p (m j) -> p m j", j=4)
                uv = UO[:, t8, t0:t0 + n].rearrange("p (m j) -> p m j", j=4)
                for l in range(4):
                    self.mm(pv[:, :, l:4], kl[:, l, :], uv[:, :, 0:4 - l], l == 0, False, [ukl, uUO], [uPS[pb]])
                for g8 in range(8):
                    pair = g8 // 2
                    for j in range(4):
                        self.mm(pv[32 * pair:32 * pair + 32, :, j], cal[:, g8, j, :], smb[:, g8, m0:m0 + nm], False,
                                (g8 == 7 and j == 3), [ucal, usmb], [uPS[pb]], tile_position=(0, 32 * pair))
                self.stt(Yt[:, 0:n], UO[:, t8, t0:t0 + n], self.dcol[:, t8:t8 + 1], PS[pb][:, 0:n], M, A,
                         [uUO, self.udcol, uPS[pb]], [uYt])
                self.tt(Tt[:, 0:n], Yt[:, 0:n], Yt[:, 0:n], M, [uYt], [uTt], eng="gpsimd")
                self.ts(Tt[:, 0:n], Tt[:, 0:n], 0.044715, 1.0, M, A, [uTt], [uTt], eng="gpsimd")
                self.tt(Tt[:, 0:n], Tt[:, 0:n], Yt[:, 0:n], M, [uTt, uYt], [uTt], eng="gpsimd")
                self.act(Tt[:, 0:n], Tt[:, 0:n], AF.Sigmoid, [uTt], [uTt], scale=1.5957691216057308)
                self.tt(ZACT[:, t8, t0:t0 + n], Yt[:, 0:n], Tt[:, 0:n], M, [uYt, uTt], [uZACT])
        pchunks = [(0, 512), (512, 512), (1024, 64)]
        for cb in range(2):
            def ev(ms, t0, n, ps, ups, cb=cb):
                self.act(Tt[:, 0:n], ps, AF.Sigmoid, [ups], [uTt])
                self.tt(GLUO[:, cb * 4 + ms, t0:t0 + n], ZACT[:, cb * 4 + ms, t0:t0 + n], Tt[:, 0:n], M, [uZACT, uTt], [uGLUO])
            self.proj_fm(self.w_glu, 0, 8, cb * 512, 512, lambda kt, t0, n: ZACT[:, kt, t0:t0 + n], [uZACT], pchunks, ev)
        self.AR3.reset()
        MIX, uMIX = self.AR3.alloc([128, 16, NOWN], BF16, "MIX")
        self.MIX, self.uMIX = MIX, uMIX
        self.gated_up(self.w_up_ssm, GLUO, uGLUO, 4096, first=True)
        self.AR2.reset(); self.AR4.reset()

    def gated_up(self, w_up, src, usrc, gate_c0, first):
        PS, uPS = self.PS, self.uPS
        HG, uHG, MIX, uMIX = self.HG, self.uHG, self.MIX, self.uMIX
        M, A = ALU.mult, ALU.add
        self.AR6.reset()
        SG = [self.AR6.alloc([128, 512], F32, "SG%d" % i) for i in range(2)]
        pchunks = [(0, 512), (512, 512), (1024, 64)]
        for cb in range(4):
            wA, uwA = self.wload(w_up, 0, 8, cb * 512, 512)
            wB, uwB = self.wload(self.w_in, 0, 16, gate_c0 + cb * 512, 512)
            for ms in range(4):
                mt = cb * 4 + ms
                for (t0, n) in pchunks:
                    pa = self.bank()
                    for kt in range(8):
                        self.mm(PS[pa][:, 0:n], wA[:, kt, ms * 128:(ms + 1) * 128], src[:, kt, t0:t0 + n], kt == 0, kt == 7,
                                [uwA, usrc], [uPS[pa]])
                    pg = self.bank()
                    for kt in range(16):
                        self.mm(PS[pg][:, 0:n], wB[:, kt, ms * 128:(ms + 1) * 128], HG[:, kt, t0:t0 + n], kt == 0, kt == 15,
                                [uwB, uHG], [uPS[pg]])
                    sg, usg = SG[(mt + (t0 // 512)) % 2]
                    self.act(sg[:, 0:n], PS[pg][:, 0:n], AF.Sigmoid, [uPS[pg]], [usg])
                    if first:
                        self.tt(MIX[:, mt, t0:t0 + n], sg[:, 0:n], PS[pa][:, 0:n], M, [usg, uPS[pa]], [uMIX])
                    else:
                        self.tt(sg[:, 0:n], sg[:, 0:n], PS[pa][:, 0:n], M, [usg, uPS[pa]], [usg])
                        self.tt(MIX[:, mt, t0:t0 + n], MIX[:, mt, t0:t0 + n], sg[:, 0:n], A, [usg, uMIX], [uMIX], eng="gpsimd")

    def phase_attn(self):
        sb, P, inp = self.sb, self.P, self.inp
        PS, uPS, XS, uXS = self.PS, self.uPS, self.XS, self.uXS
        ident_f, ident_b, uID = self.ident_f, self.ident_b, self.uID
        M, A, S = ALU.mult, ALU.add, ALU.subtract
        HG, uHG = self.HG, self.uHG
        SCALE = 0.125
        bt_d = inp("bias_tab", [128, 128]); tri_d = inp("tri", [128, 256], BF16)
        ka_d = inp("ka", [4, 17 * 128], BF16); qa_d = inp("qa", [4, 512], BF16)
        ms_d = inp("msk", [64, 16 * 64], BF16); lamv_d = inp("lamv", [1, 256]); subg_d = inp("subg", [128, 1])
        iota_d = inp("iota", [128, 1]); ptab_d = inp("ptab", [1, 256], I32)
        cache_k = inp("cache_k", [self.npool_rows, 1024]); cache_v = inp("cache_v", [self.npool_rows, 1024])
        A2, A4 = self.AR2, self.AR4
        self.AR5.reset(); self.AR6.reset()
        A3, A5, A6 = self.AR3, self.AR5, self.AR6
        QO, uQO = A2.alloc([128, 8, NOWN], BF16, "QO")
        AT, uAT = A4.alloc([128, 8, NOWN], BF16, "AT")
        self.AT, self.uAT = AT, uAT
        KTh = [A4.alloc([128, NTOK], BF16, "KTh%d" % i) for i in range(2)]
        Vh = [A4.alloc([128, 17, 129], BF16, "Vh%d" % i) for i in range(2)]
        cst = U("attn_const")
        BT = sb("BT", [128, 8, 2, 8], F32); TRI = sb("TRI", [128, 256], BF16)
        KA, _ = A5.alloc([4, 17, 128], BF16, "KA"); A5.cur.append(cst)
        QA = sb("QA", [4, 512], BF16); MS = sb("MS", [64, 16, 64], BF16)
        lamv = XS[1][0:1, 1024:1280].rearrange("p (a b) -> p a b", a=4); lsc = sb("lsc", [128, 8], F32); subg = sb("subg", [128, 1], F32)
        iota = sb("iota", [128, 1], F32); PTB = XS[1][:, 0:256].bitcast(I32); IDX = sb("IDX", [128, 256], I32)
        ones1 = sb("ones1", [1, 128], F32); CSEL = sb("CSEL", [36, 4], F32)
        for t_, d_ in ((BT[:].rearrange("p a b c -> p (a b c)"), bt_d), (TRI[:], tri_d), (KA[:].rearrange("p a b -> p (a b)"), ka_d),
                       (QA[:], qa_d), (MS[:].rearrange("p a b -> p (a b)"), ms_d), (lamv[:].rearrange("p a b -> p (a b)"), lamv_d),
                       (subg[:], subg_d), (iota[:], iota_d)):
            self.dma(t_, d_, writes=[cst])
        self.dma(PTB[:], ptab_d[0, :].partition_broadcast(128), writes=[cst])
        self.ts(IDX[:], PTB[:], 128.0, iota[:, 0:1], M, A, [cst], [cst])
        self.memset(ones1[:], 1.0, [cst])
        self.tt(lamv[:, 0, :], lamv[:, 0, :], lamv[:, 1, :], M, [cst], [cst])
        self.tt(lamv[:, 2, :], lamv[:, 2, :], lamv[:, 3, :], M, [cst], [cst])
        P.op("vector", lambda h: h.tensor_reduce(out=lamv[:, 1, 0:1], in_=lamv[:, 0, :], axis=AX.X, op=A), [cst], [cst])
        P.op("vector", lambda h: h.tensor_reduce(out=lamv[:, 1, 1:2], in_=lamv[:, 2, :], axis=AX.X, op=A), [cst], [cst])
        self.act(lamv[:, 1, 0:2], lamv[:, 1, 0:2], AF.Exp, [cst], [cst])
        self.tt(lamv[:, 1, 2:3], lamv[:, 1, 0:1], lamv[:, 1, 1:2], S, [cst], [cst])
        self.ts(lamv[:, 1, 2:3], lamv[:, 1, 2:3], 0.2, None, A, None, [cst], [cst])
        pb = self.bank()
        self.mm(PS[pb][:, 0:1], ones1[:, :], lamv[:, 1, 2:3], True, True, [cst], [uPS[pb]])
        self.cp(lsc[:, 0:1], PS[pb][:, 0:1], [uPS[pb]], [cst])
        self.ts(lsc[:, 1:2], lsc[:, 0:1], -1.0, None, M, None, [cst], [cst])
        self.ts(subg[:], subg[:], 0.8, None, M, None, [cst], [cst])
        self.memset(CSEL[:], 0.0, [cst])
        self.cp(CSEL[0:4, :], ident_f[0:4, 0:4], [cst, uID], [cst])
        self.ts(CSEL[32:36, :], ident_f[32:36, 32:36], lsc[32:36, 1:2], None, M, None, [cst, uID], [cst])
        pchunks = [(0, 512), (512, 512), (1024, 64)]
        for cb in range(2):
            def ev(ms, t0, n, ps, ups, cb=cb):
                self.cp(QO[:, cb * 4 + ms, t0:t0 + n], ps, [ups], [uQO], eng=self.evac_eng())
            self.proj_fm(self.w_in, 0, 16, 1024 + cb * 512, 512, lambda kt, t0, n: HG[:, kt, t0:t0 + n], [uHG], pchunks, ev)
        QP = [A2.alloc([128, 2, 128], BF16, "QP%d" % i) for i in range(2)]
        for qp, uqp in QP:
            self.memset(qp[:].rearrange("p a b -> p (a b)"), 0.0, [uqp])
        PT = [A2.alloc([128, 512], BF16, "PT%d" % i) for i in range(2)]
        ON, uON = A3.alloc([128, 4, 128], F32, "ON")
        ONb = XS[0][:, 1544:1672].bitcast(BF16); uONb = uXS[0]
        sc = sb("sc", [128, 16], F32); usc = U("sc")
        for v_, uv_ in Vh:
            self.memset(v_[:, :, 128:129], 1.0, [uv_])
        QSb = [A2.alloc([128, 8, 64], BF16, "QS%d" % i) for i in range(2)]
        for q_, uq_ in QSb:
            self.memset(q_[:].rearrange("p a b -> p (a b)"), 0.0, [uq_], eng="gpsimd")

        def finish_rows(nrows, o1, o2, z1, z2, dst_tok0, ntk, hsel):
            pass

        for hh in range(8):
            kt_, ukt = KTh[hh % 2]
            vv, uvv = Vh[hh % 2]
            self.dma(kt_[:, :], self.KT_s[hh], writes=[ukt])
            self.dma(vv[:, 0:16, 0:128], self.V_s[hh, 0:2048, :].rearrange("(j p) d -> p j d", p=128), writes=[uvv])
            self.dma(vv[0:64, 16, 0:128], self.V_s[hh, 2048:2112, :], writes=[uvv])
            for i in range(8):
                qp, uqp = QP[i % 2]
                self.cp(qp[0:64, 0, :], QO[0:64, hh, i * 128:(i + 1) * 128], [uQO], [uqp], eng="vector")
                self.cp(qp[64:128, 1, :], QO[64:128, hh, i * 128:(i + 1) * 128], [uQO], [uqp], eng="gpsimd")
                bo = 2 + 2 * (i % 2)
                blocks = []
                for j in range(i + 1):
                    blocks.append((j, 0, i - j))
                    blocks.append((8 + j, 1, i - j))
                for bi_, (kb, kind, rel) in enumerate(blocks):
                    ps_ = bi_ % 2
                    pt, upt = PT[bi_ % 2]
                    self.mm(PS[ps_][:, 0:128], kt_[:, kb * 128:(kb + 1) * 128], qp[:, 0, :], True, True, [ukt, uqp], [uPS[ps_]])
                    self.mm(PS[ps_][:, 128:256], kt_[:, kb * 128:(kb + 1) * 128], qp[:, 1, :], True, True, [ukt, uqp], [uPS[ps_]])
                    self.act(pt[:, 0:256], PS[ps_][:, 0:256], AF.Exp, [uPS[ps_], cst], [upt], scale=SCALE,
                             bias=BT[:, hh, kind, rel:rel + 1])
                    if kind == 0 and rel == 0:
                        self.tt(pt[:, 0:256], pt[:, 0:256], TRI[:, :], M, [upt, cst], [upt], eng="gpsimd")
                    first, last = bi_ == 0, bi_ == len(blocks) - 1
                    self.mm(PS[bo][:, 0:129], pt[:, 0:128], vv[:, kb, :], first, last, [upt, uvv], [uPS[bo]])
                    self.mm(PS[bo + 1][:, 0:129], pt[:, 128:256], vv[:, kb, :], first, last, [upt, uvv], [uPS[bo + 1]])
                c = 2 * (i % 2)
                P.op("vector", lambda h, bo=bo, c=c: h.reciprocal(out=sc[:, c:c + 1], in_=PS[bo][:, 128:129]), [uPS[bo]], [usc])
                P.op("vector", lambda h, bo=bo, c=c: h.reciprocal(out=sc[:, c + 1:c + 2], in_=PS[bo + 1][:, 128:129]), [uPS[bo + 1]], [usc])
                self.tt(sc[:, c + 1:c + 2], sc[:, c + 1:c + 2], lsc[:, 0:1], M, [usc, cst], [usc])
                o_ = ON[:, i % 2, 0:128]
                t_ = ON[:, 2 + i % 2, 0:128]
                self.ts(t_, PS[bo + 1][:, 0:128], sc[:, c + 1:c + 2], None, M, None, [uPS[bo + 1], usc], [uON])
                self.stt(o_, PS[bo][:, 0:128], sc[:, c:c + 1], t_, M, S, [uPS[bo], usc, uON], [uON])
                self.act(t_, o_, AF.Square, [uON], [uON, usc], accum_out=sc[:, 4 + c:5 + c])
                self.ts(sc[:, 5 + c:6 + c], sc[:, 4 + c:5 + c], 1.0 / 128, LN_EPS, M, A, [usc], [usc])
                self.act(sc[:, 5 + c:6 + c], sc[:, 5 + c:6 + c], AF.Sqrt, [usc], [usc])
                P.op("vector", lambda h, c=c: h.reciprocal(out=sc[:, 5 + c:6 + c], in_=sc[:, 5 + c:6 + c]), [usc], [usc])
                self.ts(ONb[:, (i % 2) * 128:(i % 2) * 128 + 128], o_, sc[:, 5 + c:6 + c], None, M, None, [uON, usc], [uONb])
                psb = PS[6 + i % 2][:].bitcast(BF16)
                self.tr(psb[:, 0:128], ONb[:, (i % 2) * 128:(i % 2) * 128 + 128], ident_b[:, :], [uONb, uID], [uPS[6 + i % 2]])
                self.ts(AT[:, hh, i * 128:(i + 1) * 128], psb[:, 0:128], subg[:, 0:1], None, M, None, [uPS[6 + i % 2], cst], [uAT])

        KP = [A5.alloc([128, 1024], BF16, "KP%d" % i) for i in range(2)]
        VP = [A6.alloc([128, 8, 129], BF16, "VP%d" % i) for i in range(2)]
        VPc = [(sb("VPc0", [128, 1024], BF16), U("VPc0")), A3.alloc([128, 1024], BF16, "VPc1")]
        for v_, uv_ in VP:
            self.memset(v_[:, :, 128:129], 1.0, [uv_])
        KTP = [A4.alloc([128, 8, 128], BF16, "KTP%d" % i) for i in range(2)]
        KTN, uKN = A4.alloc([128, 8, 64], BF16, "KTN"); VN, _ = A4.alloc([64, 8, 129], BF16, "VN")
        self.dma(KTN[:], self.KT_s[:, :, 2048:2112].rearrange("h p t -> p h t"), writes=[uKN])
        self.memset(VN[:, :, 128:129], 1.0, [uKN])
        self.dma(VN[:, :, 0:128], self.V_s[:, 2048:2112, :].rearrange("h t d -> t h d"), writes=[uKN])
        OACC = XS[0][0:36, 0:1032].rearrange("p (a b) -> p a b", a=8); uOA = uXS[0]
        OCBb = XS[0][0:4, 1032:1544].bitcast(BF16)
        OCB = XS[1][0:4, 0:1024].rearrange("p (a b) -> p a b", a=8); uOCB = uXS[1]
        OCT = XS[1][0:4, 1024:2048].rearrange("p (a b) -> p a b", a=8)
        hb = [(0, 3), (3, 6), (6, 8)]
        ck3 = cache_v.rearrange("r (h d) -> r h d", h=8)
        n_pg = 0
        for s_ in range(16):
            QS_, uQS = QSb[s_ % 2]
            srcq = QO[:, :, 1024 + 4 * s_:1024 + 4 * s_ + 4]
            self.cp(QS_[0:64, :, 0:4], srcq[0:64], [uQO], [uQS], eng="vector")
            self.cp(QS_[64:128, :, 32:36], srcq[64:128], [uQO], [uQS], eng="gpsimd")
            for slot in range(17):
                if slot < 16:
                    kp, ukp = KP[n_pg % 2]; vp, uvp = VP[n_pg % 2]; ktp, uktp = KTP[n_pg % 2]
                    col = s_ * 16 + slot
                    P.op("gpsimd", lambda h, kp=kp, col=col: h.indirect_dma_start(
                        out=kp[:, :], out_offset=None, in_=cache_k[:, :],
                        in_offset=bass.IndirectOffsetOnAxis(ap=IDX[:, col:col + 1], axis=0)), [cst], [ukp], dma=True)
                    vpc, uvpc = VPc[n_pg % 2]
                    P.op("gpsimd", lambda h, vpc=vpc, col=col: h.indirect_dma_start(
                        out=vpc[:, :], out_offset=None, in_=cache_v[:, :],
                        in_offset=bass.IndirectOffsetOnAxis(ap=IDX[:, col:col + 1], axis=0)), [cst], [uvpc], dma=True)
                    self.cp(vp[:, :, 0:128], vpc[:, :].rearrange("p (h d) -> p h d", h=8), [uvpc], [uvp], eng=self.evac_eng())
                    pq = 6 + n_pg % 2
                    psb = PS[pq][:].bitcast(BF16)
                    for hh in range(8):
                        self.tr(psb[:, hh * 128:(hh + 1) * 128], kp[:, hh * 128:(hh + 1) * 128], ident_b[:, :], [ukp, uID], [uPS[pq]])
                    self.cp(ktp[:].rearrange("p a b -> p (a b)"), psb[:, 0:1024], [uPS[pq]], [uktp], eng=self.evac_eng())
                    nk = 128
                    kt_of = lambda hh, ktp=ktp: ktp[:, hh, :]
                    v_of = lambda hh, vp=vp: vp[:, hh, :]
                    rd_k, rd_v = [uktp], [uvp]
                    n_pg += 1
                else:
                    nk = 64
                    kt_of = lambda hh: KTN[:, hh, :]
                    v_of = lambda hh: VN[:, hh, :]
                    rd_k, rd_v = [uKN], [uKN]
                ps_ = n_pg % 2
                pt, upt = PT[n_pg % 2]
                self.mm(PS[ps_][0:nk, :], KA[:, slot, 0:nk], QA[:, :], True, False, [cst], [uPS[ps_]])
                for hh in range(8):
                    self.mm(PS[ps_][0:nk, hh * 64:(hh + 1) * 64], kt_of(hh), QS_[:, hh, :], False, hh == 7,
                            rd_k + [uQS], [uPS[ps_]])
                self.act(pt[0:nk, :], PS[ps_][0:nk, :], AF.Exp, [uPS[ps_]], [upt], scale=SCALE)
                if slot == 16:
                    self.tt(pt[0:64, :].rearrange("p (h c) -> p h c", h=8), pt[0:64, :].rearrange("p (h c) -> p h c", h=8),
                            MS[:, s_, :].unsqueeze(1).to_broadcast([64, 8, 64]), M, [upt, cst], [upt], eng="gpsimd")
                for bi_, (h0, h1) in enumerate(hb):
                    pb_ = 2 + bi_
                    for hh in range(h0, h1):
                        self.mm(PS[pb_][0:36, (hh - h0) * 129:(hh - h0 + 1) * 129], pt[0:nk, hh * 64:hh * 64 + 36], v_of(hh),
                                True, True, [upt] + rd_v, [uPS[pb_]])
                    nh = h1 - h0
                    dst = OACC[:, h0:h1, :].rearrange("p a b -> p (a b)")
                    if slot == 0:
                        self.cp(dst, PS[pb_][0:36, 0:nh * 129], [uPS[pb_]], [uOA])
                    else:
                        self.tt(dst, dst, PS[pb_][0:36, 0:nh * 129], A, [uPS[pb_], uOA], [uOA])
            P.op("vector", lambda h: h.reciprocal(out=OACC[:, :, 128:129], in_=OACC[:, :, 128:129]), [uOA], [uOA])
            self.tt(OACC[:, :, 0:128], OACC[:, :, 0:128], OACC[:, :, 128:129].to_broadcast([36, 8, 128]), M, [uOA], [uOA])
            for half in range(2):
                pb_ = 5
                self.mm(PS[pb_][0:4, :], CSEL[:, :], OACC[:, 4 * half:4 * half + 4, 0:128], True, True, [cst, uOA], [uPS[pb_]])
                self.cp(OCB[:, 4 * half:4 * half + 4, :], PS[pb_][0:4, :].rearrange("p (a b) -> p a b", a=4), [uPS[pb_]], [uOCB])
            self.tt(OCT[:], OCB[:], OCB[:], M, [uOCB], [uOCB])
            P.op("vector", lambda h: h.tensor_reduce(out=sc[0:4, 8:16], in_=OCT[:], axis=AX.X, op=A), [uOCB], [usc])
            self.ts(sc[0:4, 8:16], sc[0:4, 8:16], 1.0 / 128, LN_EPS, M, A, [usc], [usc])
            self.act(sc[0:4, 8:16], sc[0:4, 8:16], AF.Sqrt, [usc], [usc])
            P.op("vector", lambda h: h.reciprocal(out=sc[0:4, 8:16], in_=sc[0:4, 8:16]), [usc], [usc])
            self.tt(OCBb[:].rearrange("p (a b) -> p a b", a=8), OCB[:], sc[0:4, 8:16].unsqueeze(2).to_broadcast([4, 8, 128]), M,
                    [uOCB, usc], [uOCB])
            psb = PS[6 + s_ % 2][:].bitcast(BF16)
            for hh in range(8):
                self.tr(psb[:, hh * 4:hh * 4 + 4], OCBb[:, hh * 128:(hh + 1) * 128], ident_b[0:4, 0:4], [uOCB, uID], [uPS[6 + s_ % 2]])
            self.ts(AT[:, :, 1024 + 4 * s_:1024 + 4 * s_ + 4], psb[:, 0:32].rearrange("p (a b) -> p a b", a=8), subg[:, 0:1], None,
                    M, None, [uPS[6 + s_ % 2], cst], [uAT])
        self.gated_up(self.w_up_att, AT, uAT, 6144, first=False)
        self.AR2.reset(); self.AR4.reset(); self.AR1.reset()

    def phase_rest(self):
        sb, P, inp = self.sb, self.P, self.inp
        PS, uPS, XS, uXS = self.PS, self.uPS, self.XS, self.uXS
        ident_f, ident_b, uID = self.ident_f, self.ident_b, self.uID
        M, A, S = ALU.mult, ALU.add, ALU.subtract
        A1, A2, A3, A4, A5, A6 = self.AR1, self.AR2, self.AR3, self.AR4, self.AR5, self.AR6
        w_o = inp("w_o", [D, D]); w_ff1 = inp("w_ff1", [D, 8192]); w_ff2 = inp("w_ff2", [8192, D])
        lnp = inp("lnp", [4, D])
        y_own = self.outp("y_own", [1024, D]); y_s = self.outp("y_s", [64, D])
        X1_s = self.scratch("X1_s", [NOWN, D])
        MIX, uMIX = self.MIX, self.uMIX
        pchunks = [(0, 512), (512, 512), (1024, 64)]
        blocks = [(128 * i, 128, 128 * i, False) for i in range(8)] + [(2048, 64, 1024, True)]
        MO, uMO = A4.alloc([128, 16, NOWN], BF16, "MO")
        for cb in range(4):
            def ev(ms, t0, n, ps, ups, cb=cb):
                self.cp(MO[:, cb * 4 + ms, t0:t0 + n], ps, [ups], [uMO], eng=self.evac_eng())
            self.proj_fm(w_o, 0, 16, cb * 512, 512, lambda kt, t0, n: MIX[:, kt, t0:t0 + n], [uMIX], pchunks, ev)
        A3.reset(); A5.reset(); A6.reset()
        mod2, umod2 = A3.alloc([128, 2, 16, 65], F32, "mod2")
        self.mod, self.umod = mod2, umod2
        self.dma(mod2[:].rearrange("p a b c -> p (a b c)"), self.MOD2_s, writes=[umod2])
        G1, uG1 = A3.alloc([128, D], F32, "G1"); LNG, uLNG = A3.alloc([128, D], F32, "LNG"); LNB, uLNB = A3.alloc([128, D], F32, "LNB")
        self.dma(LNG[:], lnp[0, :].partition_broadcast(128), writes=[uLNG])
        self.dma(LNB[:], lnp[1, :].partition_broadcast(128), writes=[uLNB])
        xn0, uxn0 = A2.alloc([128, D], BF16, "XN0"); xn1, uxn1 = A2.alloc([128, D], BF16, "XN1")
        self.XN = [xn0, xn1]; self.uXN = [uxn0, uxn1]
        H2, uH2 = A1.alloc([128, 16, NOWN], BF16, "H2")
        mv, ust = self.mv, self.ust

        def res_ln(bi, r0, nr, tok0, is_s, src_fm, usrc_fm, src_f32, xsrc, gidx, G, uG, lng, ulng, lnb, ulnb, out_fn):
            xs, ux = XS[0], uXS[0]
            rt, urt = XS[1], uXS[1]
            self.dma(xs[0:nr, :], xsrc, writes=[ux])
            if bi == 0 or is_s:
                self.dma(G[0:nr, :], self.GS[gidx, 128:192, :] if is_s else self.GS[gidx, 0:128, :], writes=[uG])
            if not src_f32:
                for half in range(2):
                    pb = 6 + half
                    psb = PS[pb][:].bitcast(BF16)
                    for j in range(8):
                        kt = half * 8 + j
                        self.tr(psb[0:nr, j * 128:(j + 1) * 128], src_fm(kt)[:, tok0:tok0 + nr], ident_b[:, :], usrc_fm + [uID], [uPS[pb]])
                    self.tt(rt[0:nr, half * 1024:(half + 1) * 1024], psb[0:nr, 0:1024], G[0:nr, half * 1024:(half + 1) * 1024], M,
                            [uPS[pb], uG], [urt])
            else:
                for q4 in range(4):
                    pb = 2 + q4
                    for j in range(4):
                        kt = q4 * 4 + j
                        self.tr(PS[pb][0:nr, j * 128:(j + 1) * 128], src_fm(kt)[:, tok0:tok0 + nr], ident_f[:, :], usrc_fm + [uID], [uPS[pb]])
                    self.tt(rt[0:nr, q4 * 512:(q4 + 1) * 512], PS[pb][0:nr, :], G[0:nr, q4 * 512:(q4 + 1) * 512], M,
                            [uPS[pb], uG], [urt])
            self.stt(xs[0:nr, :], xs[0:nr, :], ALPHA, rt[0:nr, :], M, A, [ux, urt], [ux])
            self.ln_stats(xs, ux, nr, 0)
            self.ts(xs[0:nr, :], xs[0:nr, :], mv[0:nr, 0, 0:1], mv[0:nr, 0, 3:4], S, M, [ux, ust[0]], [ux])
            self.tt(xs[0:nr, :], xs[0:nr, :], lng[0:nr, :], M, [ux, ulng], [ux], eng="gpsimd")
            self.tt(xs[0:nr, :], xs[0:nr, :], lnb[0:nr, :], A, [ux, ulnb], [ux])
            out_fn(xs, ux)

        for bi, (r0, nr, tok0, is_s) in enumerate(blocks):
            def after1(xs, ux, tok0=tok0, nr=nr, is_s=is_s, bi=bi):
                self.dma(X1_s[tok0:tok0 + nr, :], xs[0:nr, :], reads=[ux])
                self.ln_stats(xs, ux, nr, 1)
                xn, uxn = self.XN[bi % 2], self.uXN[bi % 2]
                self.ts(xn[0:nr, :], xs[0:nr, :], mv[0:nr, 1, 0:1], mv[0:nr, 1, 3:4], S, M, [ux, ust[1]], [uxn])
                self.xn_to_fm(xn, uxn, nr, 0, 1, is_s, H2, uH2, tok0)
            res_ln(bi, r0, nr, tok0, is_s, lambda kt: MO[:, kt, :], [uMO], False, self.xa[r0:r0 + nr, :], 0, G1, uG1,
                   LNG, uLNG, LNB, uLNB, after1)
        A2.reset(); A3.reset(); A4.reset()
        HID, uHID = A2.alloc([128, 8, NOWN], BF16, "HID")
        ACCl, uACCl = A4.alloc([128, 8, NOWN], F32, "ACCl")
        ACCh, uACCh = A3.alloc([128, 8, NOWN], F32, "ACCh")
        RT = [A6.alloc([128, 512], F32, "RT%d" % i) for i in range(2)]
        rk = {"n": 0}
        for hc in range(8):
            for cb in range(2):
                def ev1(ms, t0, n, ps, ups, cb=cb):
                    rt_, urt_ = RT[rk["n"] % 2]
                    rk["n"] += 1
                    self.act(rt_[:, 0:n], ps, AF.Relu, [ups], [urt_])
                    self.tt(HID[:, cb * 4 + ms, t0:t0 + n], rt_[:, 0:n], rt_[:, 0:n], M, [urt_], [uHID],
                            eng=("gpsimd" if rk["n"] % 2 else "vector"))
                self.proj_fm(w_ff1, 0, 16, hc * 1024 + cb * 512, 512, lambda kt, t0, n: H2[:, kt, t0:t0 + n], [uH2], pchunks, ev1)
            for cb in range(4):
                def ev2(ms, t0, n, ps, ups, cb=cb, hc=hc):
                    mt = cb * 4 + ms
                    acc, uacc = (ACCl, uACCl) if mt < 8 else (ACCh, uACCh)
                    o = acc[:, mt % 8, t0:t0 + n]
                    if hc == 0:
                        self.cp(o, ps, [ups], [uacc], eng=self.evac_eng())
                    else:
                        self.tt(o, o, ps, A, [ups, uacc], [uacc])
                self.proj_fm(w_ff2, hc * 1024, 8, cb * 512, 512, lambda kt, t0, n: HID[:, kt, t0:t0 + n], [uHID], pchunks, ev2)
        A2.reset(); A5.reset()
        G2, uG2 = A2.alloc([128, D], F32, "G2"); LNG2, uLNG2 = A2.alloc([128, D], F32, "LNG2")
        LNB2, uLNB2 = A5.alloc([128, D], F32, "LNB2")
        self.dma(LNG2[:], lnp[2, :].partition_broadcast(128), writes=[uLNG2])
        self.dma(LNB2[:], lnp[3, :].partition_broadcast(128), writes=[uLNB2])
        uACC = U("accboth")
        for bi, (r0, nr, tok0, is_s) in enumerate(blocks):
            def after2(xs, ux, tok0=tok0, nr=nr, is_s=is_s):
                dst = y_s[0:64, :] if is_s else y_own[tok0:tok0 + nr, :]
                self.dma(dst, xs[0:nr, :], reads=[ux])
            res_ln(bi, r0, nr, tok0, is_s, lambda kt: (ACCl if kt < 8 else ACCh)[:, kt % 8, :], [uACCl, uACCh], True,
                   X1_s[tok0:tok0 + nr, :], 1, G2, uG2, LNG2, uLNG2, LNB2, uLNB2, after2)

    def finish(self):
        self.P.emit(self.st)
        self.st.close()


SLOPES = np.array([2.0 ** (-(hh + 1)) for hh in range(8)], np.float64)


def _bias_tab(h):
    kk = np.arange(128, dtype=np.float64)[:, None, None, None]
    sl = SLOPES[None, :, None, None]
    rel = np.arange(8, dtype=np.float64)[None, None, None, :]
    own = sl * (kk - 256.0 * rel)
    oth = sl * (kk - 256.0 * rel + 128.0 * (1 - 2 * h))
    bt = np.concatenate([np.broadcast_to(own, (128, 8, 1, 8)), np.broadcast_to(oth, (128, 8, 1, 8))], axis=2).copy()
    if h == 0:
        bt[:, :, 1, 0] = -30000.0
    return bt.reshape(128, 128).astype(np.float32)


def _consts():
    tri = (np.arange(128)[None, :] >= np.arange(128)[:, None]).astype(np.float32)
    tri = np.concatenate([tri, tri], axis=1).astype(NPBF)
    ka = np.zeros((4, 17, 128), np.float32)
    for slot in range(16):
        kpos = 128 * slot + np.arange(128)
        ka[0, slot] = (kpos // 256) * 256
        ka[1, slot] = kpos % 256
    ka[0, 16] = 2048.0
    ka[1, 16] = np.arange(128) % 4
    ka[2] = 1.0
    ka[3] = 1.0
    qa = np.zeros((4, 8, 64), np.float32)
    for hh in range(8):
        s8 = 8.0 * SLOPES[hh]
        for c0 in (0, 32):
            for t in range(4):
                qa[:, hh, c0 + t] = [s8, s8, -s8 * 2048.0, -s8 * t]
    ms = np.zeros((64, 16, 64), np.float32)
    for kk in range(64):
        for c0 in (0, 32):
            for t in range(4):
                if kk % 4 <= t:
                    ms[kk, kk // 4, c0 + t] = 1.0
    return tri, ka.reshape(4, 17 * 128).astype(NPBF), qa.reshape(4, 512).astype(NPBF), ms.reshape(64, 1024).astype(NPBF)


TRI_C, KA_C, QA_C, MS_C = _consts()


def _core_maps(inputs, cores, small_pool):
    xp = np.asarray(inputs["x_prompt"])
    xsamp = np.asarray(inputs["x_sample"])
    maps = []
    ident = np.eye(128, dtype=np.float32)
    esel = np.zeros((64, 2, 128), np.float32)
    esel[np.arange(64), 0, np.arange(64)] = 1.0
    esel[np.arange(64), 1, 64 + np.arange(64)] = 1.0
    esel = esel.reshape(64, 256).astype(NPBF)
    for c in cores:
        b, h = c // 2, c % 2
        own = [2 * i + h for i in range(8)]
        oth = [2 * i + 1 - h for i in range(8)]
        xb = xp[b].reshape(16, 128, D)
        xa = np.concatenate([xb[own].reshape(1024, D), xb[oth].reshape(1024, D),
                             xsamp[16 * c:16 * c + 16].reshape(64, D)], axis=0)
        call = np.concatenate([np.asarray(inputs["c_prompt"])[b:b + 1],
                               np.repeat(np.asarray(inputs["c_sample"])[16 * c:16 * c + 16], 4, axis=0)], axis=0)
        m = {
            "xa": np.ascontiguousarray(xa),
            "call": np.ascontiguousarray(call),
            "w_ada": np.asarray(inputs["w_ada"])[0],
            "b_ada": np.asarray(inputs["b_ada"])[0].reshape(96, 128),
            "w_in": np.asarray(inputs["w_in"])[0],
            "w_glu": np.asarray(inputs["w_glu"])[0], "w_up_ssm": np.asarray(inputs["w_up_ssm"])[0],
            "w_up_att": np.asarray(inputs["w_up_att"])[0], "w_o": np.asarray(inputs["w_o"])[0],
            "w_ff1": np.asarray(inputs["w_ff1"])[0], "w_ff2": np.asarray(inputs["w_ff2"])[0],
            "ident_f": ident,
            "ident_b": ident.astype(NPBF),
            "a_re": np.asarray(inputs["ssm_a_re"])[0], "a_im": np.asarray(inputs["ssm_a_im"])[0],
            "log_dt": np.asarray(inputs["ssm_log_dt"])[0].reshape(64, 1),
            "b_re": np.asarray(inputs["ssm_b_re"])[0].reshape(64, 1024), "b_im": np.asarray(inputs["ssm_b_im"])[0].reshape(64, 1024),
            "c_re": np.asarray(inputs["ssm_c_re"])[0].reshape(64, 1024), "c_im": np.asarray(inputs["ssm_c_im"])[0].reshape(64, 1024),
            "dvec": np.asarray(inputs["ssm_d"])[0].reshape(8, 128),
            "hflag": np.full((64, 1), float(h), np.float32),
            "lnp": np.stack([np.asarray(inputs[k])[0] for k in ("ln1_g", "ln1_b", "ln2_g", "ln2_b")]).astype(np.float32),
            "bias_tab": _bias_tab(h), "tri": TRI_C, "ka": KA_C, "qa": QA_C, "msk": MS_C,
            "lamv": np.concatenate([np.asarray(inputs[k])[0] for k in ("lam_q1", "lam_k1", "lam_q2", "lam_k2")]).reshape(1, 256).astype(np.float32),
            "subg": np.asarray(inputs["subln_g"])[0].reshape(128, 1), "iota": np.arange(128, dtype=np.float32).reshape(128, 1),
            "esel": esel,
            "s0T": np.ascontiguousarray(np.stack([np.asarray(inputs["state_ssm_re"])[0, 16 * c:16 * c + 16],
                                                  np.asarray(inputs["state_ssm_im"])[0, 16 * c:16 * c + 16]], 0).transpose(3, 0, 2, 1)),
        }
        pt = np.asarray(inputs["page_table"])[16 * c:16 * c + 16].astype(np.int32)
        if small_pool:
            ck = np.asarray(inputs["cache_k"])[0][pt.reshape(-1)].reshape(256 * 128, 1024)
            cv = np.asarray(inputs["cache_v"])[0][pt.reshape(-1)].reshape(256 * 128, 1024)
            pt = np.arange(256, dtype=np.int32).reshape(16, 16)
        else:
            ck = np.asarray(inputs["cache_k"])[0].reshape(-1, 1024)
            cv = np.asarray(inputs["cache_v"])[0].reshape(-1, 1024)
        m["cache_k"] = ck; m["cache_v"] = cv; m["ptab"] = pt.reshape(1, 256)
        maps.append(m)
    return maps


def _assemble(results, cores, outs):
    for ci, c in enumerate(cores):
        b, h = c // 2, c % 2
        r = results[ci]
        own = [2 * i + h for i in range(8)]
        if "k_own" in r:
            ko = r["k_own"].reshape(8, 128, 8, 128)
            vo = r["v_own"].reshape(8, 128, 8, 128)
            for i, g in enumerate(own):
                outs[2][0, b, g * 128:(g + 1) * 128] = ko[i]
                outs[3][0, b, g * 128:(g + 1) * 128] = vo[i]
            outs[6][0, 16 * c:16 * c + 16] = r["k_s"].reshape(16, 4, 8, 128)
            outs[7][0, 16 * c:16 * c + 16] = r["v_s"].reshape(16, 4, 8, 128)
        if "y_own" in r:
            yo = r["y_own"].reshape(8, 128, D)
            for i, g in enumerate(own):
                outs[0][b, g * 128:(g + 1) * 128] = yo[i]
            outs[1][16 * c:16 * c + 16] = r["y_s"].reshape(16, 4, D)
        if "sfin_o" in r:
            if h == 0:
                outs[4][0, b] = r["sfin_o"][:, 0, :].T
                outs[5][0, b] = r["sfin_o"][:, 1, :].T
            outs[8][0, 16 * c:16 * c + 16] = r["sfs_o"][:, 0].transpose(2, 1, 0)
            outs[9][0, 16 * c:16 * c + 16] = r["sfs_o"][:, 1].transpose(2, 1, 0)
    return outs


def _empty_outs():
    return [np.zeros((4, 2048, 2048), np.float32), np.zeros((128, 4, 2048), np.float32),
            np.zeros((1, 4, 2048, 8, 128), np.float32), np.zeros((1, 4, 2048, 8, 128), np.float32),
            np.zeros((1, 4, 64, 64), np.float32), np.zeros((1, 4, 64, 64), np.float32),
            np.zeros((1, 128, 4, 8, 128), np.float32), np.zeros((1, 128, 4, 8, 128), np.float32),
            np.zeros((1, 128, 64, 64), np.float32), np.zeros((1, 128, 64, 64), np.float32)]


def run(inputs, cores=tuple(range(8)), small_pool=False, stage=99, trace=False):
    bld = Builder(256 * 128 if small_pool else 2560 * 128, stage)
    bld.build()
    maps = _core_maps(inputs, cores, small_pool)
    maps = [{k: v for k, v in m.items() if k in bld.din} for m in maps]
    res = run_bass_kernel_spmd(bld.nc, maps, core_ids=list(range(len(cores))), **({"trace": True} if trace else {}))
    if trace:
        print("EXEC_NS", res.exec_time_ns)
    return _assemble(res.results, cores, _empty_outs())


def kernel(**inputs):
    outs = run(inputs)
    return tuple(outs)
```

```python
import numpy as np
import ml_dtypes
from contextlib import ExitStack
import concourse.bass as bass
import concourse.mybir as mybir
from concourse.alu_op_type import AluOpType as ALU
from concourse.bass_utils import run_bass_kernel_spmd

F32 = mybir.dt.float32
BF16 = mybir.dt.bfloat16
I32 = mybir.dt.int32
U32 = mybir.dt.uint32
AF = mybir.ActivationFunctionType
AX = mybir.AxisListType
NPBF = ml_dtypes.bfloat16

D = 2048
NTOK = 2112
NOWN = 1088
LN_EPS = 1e-5
ALPHA = 2.0 ** 0.25
ENGS = ("sync", "gpsimd", "scalar", "vector", "tensor")
NDMASEM = 8
SAME_ENG_SYNC = True


class U:
    __slots__ = ("name", "w", "r")

    def __init__(self, name=""):
        self.name = name
        self.w = None
        self.r = []


class Op:
    __slots__ = ("eng", "fn", "deps", "dma", "sig", "sem", "val", "idx", "throttle")

    def __init__(self, eng, fn, dma):
        self.eng = eng
        self.fn = fn
        self.dma = dma
        self.deps = set()
        self.sig = False
        self.sem = None
        self.val = None
        self.throttle = None


class Prog:
    def __init__(self, nc):
        self.nc = nc
        self.ops = []

    def op(self, eng, fn, reads=(), writes=(), dma=False):
        o = Op(eng, fn, dma)
        o.idx = len(self.ops)
        for u in reads:
            if u.w is not None:
                o.deps.add(u.w)
        for u in writes:
            if u.w is not None:
                o.deps.add(u.w)
            for r in u.r:
                o.deps.add(r)
        for u in reads:
            u.r.append(o.idx)
        for u in writes:
            u.w = o.idx
            u.r = []
        o.deps.discard(o.idx)
        self.ops.append(o)
        return o

    def emit(self, stack):
        nc = self.nc
        ops = self.ops
        for o in ops:
            if o.dma:
                o.sig = True
        for o in ops:
            for d in o.deps:
                p = ops[d]
                if p.dma:
                    continue
                if p.eng == o.eng and (p.eng == "tensor" or not SAME_ENG_SYNC):
                    continue
                p.sig = True
        esem = {e: stack.enter_context(nc.semaphore("s_" + e)) for e in ENGS}
        dsem = {e: [stack.enter_context(nc.semaphore("d_%s%d" % (e, i))) for i in range(NDMASEM)]
                for e in ("sync", "gpsimd")}
        ecnt = {e: 0 for e in ENGS}
        dcnt = {e: 0 for e in ("sync", "gpsimd")}
        dhist = {e: [] for e in ("sync", "gpsimd")}
        for o in ops:
            if not o.sig:
                continue
            if o.dma:
                n = dcnt[o.eng]
                dcnt[o.eng] += 1
                o.sem = ("d", o.eng, n % NDMASEM)
                o.val = 16 * (n // NDMASEM + 1)
                if n >= NDMASEM:
                    o.throttle = dhist[o.eng][n - NDMASEM]
                dhist[o.eng].append(o.idx)
            else:
                ecnt[o.eng] += 1
                o.sem = ("e", o.eng, 0)
                o.val = ecnt[o.eng]

        def semh(key):
            return esem[key[1]] if key[0] == "e" else dsem[key[1]][key[2]]

        per_eng = {e: [o for o in ops if o.eng == e] for e in ENGS}
        final_waits = {}
        for e in ("sync", "gpsimd"):
            last = {}
            for o in per_eng[e]:
                if o.dma:
                    last[o.sem] = o.val
            final_waits[e] = last

        def run_engine(ename, h):
            seen = {}
            for o in per_eng[ename]:
                need = {}
                dl = list(o.deps)
                if o.throttle is not None:
                    dl.append(o.throttle)
                for d in dl:
                    p = ops[d]
                    if not p.sig:
                        continue
                    if (not p.dma) and p.eng == ename and (ename == "tensor" or not SAME_ENG_SYNC):
                        continue
                    if need.get(p.sem, 0) < p.val:
                        need[p.sem] = p.val
                for k, v in need.items():
                    if seen.get(k, 0) < v:
                        h.wait_ge(semh(k), v)
                        seen[k] = v
                inst = o.fn(h)
                if o.sig:
                    inst.then_inc(semh(o.sem), 16 if o.dma else 1)
            for k, v in final_waits.get(ename, {}).items():
                if seen.get(k, 0) < v:
                    h.wait_ge(semh(k), v)

        block = stack.enter_context(nc.Block())

        @block.sync
        def _(h):
            run_engine("sync", h)

        @block.gpsimd
        def _(h):
            run_engine("gpsimd", h)

        @block.scalar
        def _(h):
            run_engine("scalar", h)

        @block.vector
        def _(h):
            run_engine("vector", h)

        @block.tensor
        def _(h):
            run_engine("tensor", h)


class Arena:
    def __init__(self, bld, name, nbytes):
        self.t = bld.st.enter_context(bld.nc.sbuf_tensor("ar_" + name, [128, nbytes // 4], F32))
        self.nbytes = nbytes
        self.off = 0
        self.cur = []
        self.prev = []
        self.name = name

    def reset(self, keep=0):
        self.prev = self.cur + self.prev
        self.cur = []
        self.off = (keep + 31) // 32 * 32

    def alloc(self, shape, dt, name=""):
        esz = 4 if dt in (F32, I32, U32) else 2
        n = 1
        for s_ in shape[1:]:
            n *= s_
        nb = (n * esz + 31) // 32 * 32
        assert self.off + nb <= self.nbytes, (self.name, name, self.off, nb, self.nbytes)
        v = self.t[0:shape[0], self.off // 4:(self.off + nb) // 4]
        if esz == 2:
            v = v.bitcast(dt)
        elif dt != F32:
            v = v.bitcast(dt)
        v = v[:, 0:n]
        if len(shape) == 3:
            v = v.rearrange("p (a b) -> p a b", a=shape[1])
        elif len(shape) == 4:
            v = v.rearrange("p (a b c) -> p a b c", a=shape[1], b=shape[2])
        elif len(shape) == 5:
            v = v.rearrange("p (a b c d) -> p a b c d", a=shape[1], b=shape[2], c=shape[3])
        self.off += nb
        u = U(self.name + ":" + name)
        for pu in self.prev:
            if pu.w is not None:
                u.r.append(pu.w)
            u.r.extend(pu.r)
        self.cur.append(u)
        return v, u


class Builder:
    def __init__(self, npool_rows, stage):
        self.stage = stage
        self.nc = bass.Bass("TRN2", target_bir_lowering=False)
        self.P = Prog(self.nc)
        self.st = ExitStack()
        self.npool_rows = npool_rows
        self.din = {}
        self.dout = {}
        self._evk = 0
        self._pb = 0

    def inp(self, name, shape, dt=F32):
        t = self.nc.dram_tensor(name, list(shape), dt, kind="ExternalInput").ap()
        self.din[name] = t
        return t

    def outp(self, name, shape, dt=F32):
        t = self.nc.dram_tensor(name, list(shape), dt, kind="ExternalOutput").ap()
        self.dout[name] = t
        return t

    def scratch(self, name, shape, dt=F32):
        return self.nc.dram_tensor(name, list(shape), dt, kind="Internal").ap()

    def sb(self, name, shape, dt=F32):
        return self.st.enter_context(self.nc.sbuf_tensor("sb_" + name, list(shape), dt))

    def dma(self, out, in_, reads=(), writes=(), eng="sync"):
        return self.P.op(eng, lambda h: h.dma_start(out=out, in_=in_), reads, writes, dma=True)

    def mm(self, out, lhsT, rhs, start, stop, reads, writes, **kw):
        return self.P.op("tensor", lambda h: h.matmul(out, lhsT=lhsT, rhs=rhs, start=start, stop=stop, **kw),
                         reads, writes)

    def tr(self, out, in_, ident, reads, writes):
        return self.P.op("tensor", lambda h: h.transpose(out, in_, ident), reads, writes)

    def act(self, out, in_, func, reads, writes, **kw):
        return self.P.op("scalar", lambda h: h.activation(out=out, in_=in_, func=func, **kw), reads, writes)

    def tt(self, out, in0, in1, op, reads, writes, eng="vector"):
        return self.P.op(eng, lambda h: h.tensor_tensor(out=out, in0=in0, in1=in1, op=op), reads, writes)

    def ts(self, out, in0, s1, s2, op0, op1, reads, writes, eng="vector"):
        if op1 is None:
            return self.P.op(eng, lambda h: h.tensor_scalar(out=out, in0=in0, scalar1=s1, scalar2=None, op0=op0),
                             reads, writes)
        return self.P.op(eng, lambda h: h.tensor_scalar(out=out, in0=in0, scalar1=s1, scalar2=s2, op0=op0, op1=op1),
                         reads, writes)

    def stt(self, out, in0, scalar, in1, op0, op1, reads, writes):
        return self.P.op("vector", lambda h: h.scalar_tensor_tensor(out=out, in0=in0, scalar=scalar, in1=in1,
                                                                    op0=op0, op1=op1), reads, writes)

    def cp(self, out, in_, reads, writes, eng="vector"):
        if eng == "scalar":
            return self.P.op(eng, lambda h: h.copy(out=out, in_=in_), reads, writes)
        return self.P.op(eng, lambda h: h.tensor_copy(out=out, in_=in_), reads, writes)

    def evac_eng(self):
        self._evk += 1
        return "scalar" if (self._evk & 1) else "vector"

    def bank(self, n=6):
        self._pb = (self._pb + 1) % n
        return self._pb

    def memset(self, ap, val, writes, eng="vector"):
        return self.P.op(eng, lambda h: h.memset(ap, val), (), writes)

    def cmul(self, or_, oi, ar, ai, br, bi, t1, t2, rd, wr, eng="vector"):
        M, A, S = ALU.mult, ALU.add, ALU.subtract
        self.tt(t1, ar, br, M, rd, wr, eng)
        self.tt(t2, ai, bi, M, rd, wr, eng)
        self.tt(or_, t1, t2, S, rd + wr, wr, eng)
        self.tt(t1, ar, bi, M, rd + wr, wr, eng)
        self.tt(t2, ai, br, M, rd + wr, wr, eng)
        self.tt(oi, t1, t2, A, rd + wr, wr, eng)

    def build(self):
        self.setup()
        self.ada_gen = self.phase_ada()
        self.ada_tick()
        self.phase_ssm_gen()
        for _ in self.ada_gen:
            pass
        self.phase1()
        if self.stage >= 2:
            self.phase_ssm_y()
        if self.stage >= 3:
            self.phase_attn()
        if self.stage >= 4:
            self.phase_rest()
        self.finish()

    def ada_tick(self):
        try:
            next(self.ada_gen)
        except StopIteration:
            pass

    def setup(self):
        nc = self.nc
        inp, outp, sb = self.inp, self.outp, self.sb
        self.xa = inp("xa", [NTOK, D])
        self.call = inp("call", [65, D])
        self.w_ada = inp("w_ada", [D, 6 * D])
        self.b_ada = inp("b_ada", [96, 128])
        self.w_in = inp("w_in", [D, 8192])
        self.w_glu = inp("w_glu", [1024, 1024])
        self.w_up_ssm = inp("w_up_ssm", [1024, D])
        self.w_up_att = inp("w_up_att", [1024, D])
        ident_f_d = inp("ident_f", [128, 128])
        ident_b_d = inp("ident_b", [128, 128], BF16)
        self.k_own = outp("k_own", [1024, 1024])
        self.v_own = outp("v_own", [1024, 1024])
        self.k_s = outp("k_s", [64, 1024])
        self.v_s = outp("v_s", [64, 1024])
        self.KT_s = self.scratch("KT_s", [8, 128, NTOK], BF16)
        self.V_s = self.scratch("V_s", [8, NTOK, 128], BF16)
        self.GS = self.scratch("GS", [2, 192, D])
        self.WB = [sb("WB%d" % i, [128, 16, 512], BF16) for i in range(2)]
        self.uWB = [U("WB0"), U("WB1")]
        self.XS = [sb("XS%d" % i, [128, D], F32) for i in range(2)]
        self.uXS = [U("XS0"), U("XS1")]
        self.ident_f = sb("ident_f", [128, 128], F32)
        self.ident_b = sb("ident_b", [128, 128], BF16)
        self.uID = U("ident")
        self.dma(self.ident_f[:], ident_f_d, writes=[self.uID])
        self.dma(self.ident_b[:], ident_b_d, writes=[self.uID])
        self.PS = [self.st.enter_context(nc.psum_tensor("ps%d" % i, [128, 512], F32)) for i in range(8)]
        self.uPS = [U("ps%d" % i) for i in range(8)]
        self.wn = 0
        self.AR1 = Arena(self, "A1", 34816)
        self.AR2 = Arena(self, "A2", 22 * 1024)
        self.AR3 = Arena(self, "A3", 38 * 1024)
        self.AR4 = Arena(self, "A4", 42 * 1024)
        self.AR5 = Arena(self, "A5", 8704)
        self.AR6 = Arena(self, "A6", 4352)
        self.WT, self.uWT = self.AR4.alloc([128, 16, 4, 128], BF16, "WT")

    def wload(self, wview, k0, nkt, c0, ncols):
        i = self.wn % 2
        self.wn += 1
        src = wview[k0:k0 + 128 * nkt, c0:c0 + ncols].rearrange("(kt p) n -> p kt n", p=128)
        self.dma(self.WB[i][:, 0:nkt, 0:ncols], src, writes=[self.uWB[i]], eng="gpsimd")
        return self.WB[i], self.uWB[i]

    def proj_fm(self, wview, k0, nkt, c0, ncols, actf, uact, chunks, evac):
        wb, uwb = self.wload(wview, k0, nkt, c0, ncols)
        PS, uPS = self.PS, self.uPS
        for ms in range(ncols // 128):
            for (t0, n) in chunks:
                pb = self.bank()
                for kt in range(nkt):
                    self.mm(PS[pb][:, 0:n], wb[:, kt, ms * 128:(ms + 1) * 128], actf(kt, t0, n), kt == 0, kt == nkt - 1,
                            [uwb] + uact, [uPS[pb]])
                evac(ms, t0, n, PS[pb][:, 0:n], uPS[pb])

    def phase_ada(self):
        sb, P = self.sb, self.P
        PS, uPS, XS, uXS = self.PS, self.uPS, self.XS, self.uXS
        ident_f, uID = self.ident_f, self.uID
        cT, ucT = self.AR4.alloc([128, 16, 65], BF16, "cT")
        cTp, _ = self.AR4.alloc([128, 16, 128], BF16, "cTp")
        gb, ugb = self.AR4.alloc([128, D], F32, "gb")
        ba = sb("ba", [128, 96], F32)
        uba = U("ba")
        self.mod, self.umod = self.AR5.alloc([128, 2, 16, 65], F32, "mod")
        mod2, umod2 = self.AR4.alloc([128, 2, 16, 65], F32, "mod2")
        self.MOD2_s = self.scratch("MOD2_s", [128, 2 * 16 * 65])
        self.dma(XS[0][0:65, :], self.call, writes=[uXS[0]])
        self.act(XS[0][0:65, :], XS[0][0:65, :], AF.Silu, [uXS[0]], [uXS[0]])
        for kt in range(16):
            b = kt % 2
            self.tr(PS[b][:, 0:65], XS[0][0:65, kt * 128:(kt + 1) * 128], ident_f[0:65, 0:65],
                    [uXS[0], uID], [uPS[b]])
            self.cp(cT[:, kt, :], PS[b][:, 0:65], [uPS[b]], [ucT], eng=self.evac_eng())
        self.cp(cTp[:], cT[:, :, 0:1].to_broadcast([128, 16, 128]), [ucT], [ucT], eng="vector")
        self.dma(XS[1][0:96, 0:128], self.b_ada, writes=[uXS[1]])
        self.tr(PS[2][:, 0:96], XS[1][0:96, 0:128], ident_f[0:96, 0:96], [uXS[1], uID], [uPS[2]])
        self.cp(ba[:], PS[2][:, 0:96], [uPS[2]], [uba])
        b_ada_flat = self.b_ada.rearrange("a b -> (a b)")
        for cb in range(24):
            kind = cb // 4
            wb, uwb = self.wload(self.w_ada, 0, 16, cb * 512, 512)
            if kind in (0, 1, 3, 4):
                mk = {0: 0, 1: 1, 3: 2, 4: 3}[kind]
                for ms in range(4):
                    tile_i = (cb % 4) * 4 + ms
                    pb = 3 + (ms % 2)
                    for kt in range(16):
                        self.mm(PS[pb][:, 0:65], wb[:, kt, ms * 128:(ms + 1) * 128], cT[:, kt, :], kt == 0, kt == 15,
                                [uwb, ucT], [uPS[pb]])
                    col = cb * 4 + ms
                    md, umd = (self.mod, self.umod) if mk < 2 else (mod2, umod2)
                    self.ts(md[:, mk % 2, tile_i, :], PS[pb][:, 0:65], ba[:, col:col + 1],
                            1.0 if kind in (1, 4) else 0.0, ALU.add, ALU.add, [uPS[pb], uba], [umd])
            else:
                gi = 0 if kind == 2 else 1
                if cb % 4 == 0:
                    src = b_ada_flat[kind * D:(kind + 1) * D].partition_broadcast(128)
                    self.dma(gb[:], src, writes=[ugb])
                c0 = (cb % 4) * 512
                for kt in range(16):
                    self.mm(PS[5][:, :], cTp[:, kt, :], wb[:, kt, :], kt == 0, kt == 15, [uwb, ucT], [uPS[5]])
                self.tt(XS[0][:, c0:c0 + 512], PS[5][:, :], gb[:, c0:c0 + 512], ALU.add, [uPS[5], ugb], [uXS[0]])
                for kt in range(16):
                    self.mm(PS[6][0:64, :], cT[:, kt, 1:65], wb[:, kt, :], kt == 0, kt == 15, [uwb, ucT], [uPS[6]])
                self.tt(XS[1][0:64, c0:c0 + 512], PS[6][0:64, :], gb[0:64, c0:c0 + 512], ALU.add, [uPS[6], ugb], [uXS[1]])
                if cb % 4 == 3:
                    self.dma(self.GS[gi, 0:128, :], XS[0][:, :], reads=[uXS[0]])
                    self.dma(self.GS[gi, 128:192, :], XS[1][0:64, :], reads=[uXS[1]])
            yield
        self.dma(self.MOD2_s, mod2[:].rearrange("p a b c -> p (a b c)"), reads=[umod2])
        yield
    def phase_ssm_gen(self):
        sb, P, inp = self.sb, self.P, self.inp
        PS, uPS, XS, uXS = self.PS, self.uPS, self.XS, self.uXS
        ident_f, ident_b, uID = self.ident_f, self.ident_b, self.uID
        M, A, S = ALU.mult, ALU.add, ALU.subtract
        a_re_d = inp("a_re", [64, 64]); a_im_d = inp("a_im", [64, 64]); ldt_d = inp("log_dt", [64, 1])
        b_re_d = inp("b_re", [64, 1024]); b_im_d = inp("b_im", [64, 1024])
        c_re_d = inp("c_re", [64, 1024]); c_im_d = inp("c_im", [64, 1024])
        dvec_d = inp("dvec", [8, 128]); hflag_d = inp("hflag", [64, 1]); esel_d = inp("esel", [64, 256], BF16)
        self.KL_s = self.scratch("KL_s", [128, 8 * 4 * 128], BF16)
        self.CAL_s = self.scratch("CAL_s", [128, 64 * 4 * 32], BF16)
        A1, A2, A3 = self.AR1, self.AR2, self.AR3
        ug = U("gen")
        rd, wr = [ug], [ug]
        A1.cur.append(ug); A2.cur.append(ug); A3.cur.append(ug)

        def sm(name):
            v, _ = A1.alloc([64, 64], F32, name)
            return v
        are, aim, xr, xi, e_, cs, sn, Ar, Ai, t1, t2, t3 = [sm("s%d" % i) for i in range(12)]
        den, rden, m1, Fr, Fi, A128r, A128i = [sm("q%d" % i) for i in range(7)]
        Apr, _ = A1.alloc([64, 5, 64], F32, "Apr")
        Api, _ = A1.alloc([64, 5, 64], F32, "Api")
        cols, _ = A1.alloc([64, 4], F32, "cols")
        CAp, uCAp = A1.alloc([128, 64, 5, 16], BF16, "CAp")
        KL, uKL = A1.alloc([128, 8, 4, 128], BF16, "KL")
        brc, _ = A1.alloc([64, 1024], F32, "brc"); bic, _ = A1.alloc([64, 1024], F32, "bic")
        br = brc.rearrange("g (p c) -> g p c", c=16); bi = bic.rearrange("g (p c) -> g p c", c=16)
        cr = brc.rearrange("g (c p) -> g c p", c=16); ci = bic.rearrange("g (c p) -> g c p", c=16)
        Bbr, _ = A2.alloc([64, 64, 16], F32, "Bbr"); Bbi, _ = A2.alloc([64, 64, 16], F32, "Bbi")
        T1, _ = A2.alloc([64, 1024], F32, "T1"); T2, _ = A2.alloc([64, 1024], F32, "T2")
        Wg, uWg = A3.alloc([64, 4, 16, 2, 64], BF16, "Wg")
        CAg, uCAg = A3.alloc([64, 5, 16, 2, 64], BF16, "CAg")
        self.A4p = sb("A4p", [64, 2, 64], F32); self.A128p = sb("A128p", [64, 2, 64], F32); self.uAp = U("Ap")
        self.dcol = sb("dcol", [128, 8], F32); self.udcol = U("dcol")
        self.hf = sb("hf", [64, 1], F32); self.esel = sb("esel", [64, 2, 128], BF16); self.uhf = U("hf")
        for t_, d_ in ((are, a_re_d), (aim, a_im_d)):
            self.dma(t_, d_, writes=wr)
        self.dma(cols[:, 0:1], ldt_d, writes=wr)
        self.dma(br.rearrange("g p c -> g (p c)"), b_re_d, writes=wr)
        self.dma(bi.rearrange("g p c -> g (p c)"), b_im_d, writes=wr)
        self.dma(self.hf[:], hflag_d, writes=[self.uhf])
        self.dma(self.esel[:].rearrange("p a b -> p (a b)"), esel_d, writes=[self.uhf])
        self.memset(cols[:, 1:2], float(np.pi / 2), wr)
        self.memset(cols[:, 2:3], 0.0, wr)
        self.act(cols[:, 3:4], cols[:, 0:1], AF.Exp, rd, wr)
        self.ts(xr, are, cols[:, 3:4], None, M, None, rd, wr)
        self.ts(xi, aim, cols[:, 3:4], None, M, None, rd, wr)
        self.act(e_, xr, AF.Exp, rd, wr, scale=1.0 / 16)
        self.act(cs, xi, AF.Sin, rd, wr, scale=1.0 / 16, bias=cols[:, 1:2])
        self.act(sn, xi, AF.Sin, rd, wr, scale=1.0 / 16, bias=cols[:, 2:3])
        self.tt(Ar, e_, cs, M, rd, wr)
        self.tt(Ai, e_, sn, M, rd, wr)

        def csq(r_, i_):
            self.tt(t1, r_, r_, M, rd, wr)
            self.tt(t2, i_, i_, M, rd, wr)
            self.tt(t3, r_, i_, M, rd, wr)
            self.tt(r_, t1, t2, S, rd, wr)
            self.ts(i_, t3, 2.0, None, M, None, rd, wr)
        for _ in range(4):
            csq(Ar, Ai)
            self.ada_tick()
        self.memset(Apr[:, 0, :], 1.0, wr)
        self.memset(Api[:, 0, :], 0.0, wr)
        self.cp(Apr[:, 1, :], Ar, rd, wr)
        self.cp(Api[:, 1, :], Ai, rd, wr)
        self.cmul(Apr[:, 2, :], Api[:, 2, :], Ar, Ai, Ar, Ai, t1, t2, rd, wr)
        self.cmul(Apr[:, 3, :], Api[:, 3, :], Apr[:, 2, :], Api[:, 2, :], Ar, Ai, t1, t2, rd, wr)
        self.cmul(Apr[:, 4, :], Api[:, 4, :], Apr[:, 2, :], Api[:, 2, :], Apr[:, 2, :], Api[:, 2, :], t1, t2, rd, wr)
        self.cp(A128r, Apr[:, 4, :], rd, wr)
        self.cp(A128i, Api[:, 4, :], rd, wr)
        for _ in range(5):
            csq(A128r, A128i)
            self.ada_tick()
        self.tt(t1, are, are, M, rd, wr)
        self.tt(t2, aim, aim, M, rd, wr)
        self.tt(den, t1, t2, A, rd, wr)
        P.op("vector", lambda h: h.reciprocal(out=rden, in_=den), rd, wr)
        self.ts(m1, Ar, -1.0, None, A, None, rd, wr)
        self.tt(t1, m1, are, M, rd, wr)
        self.tt(t2, Ai, aim, M, rd, wr)
        self.tt(t1, t1, t2, A, rd, wr)
        self.tt(Fr, t1, rden, M, rd, wr)
        self.tt(t1, Ai, are, M, rd, wr)
        self.tt(t2, m1, aim, M, rd, wr)
        self.tt(t1, t1, t2, S, rd, wr)
        self.tt(Fi, t1, rden, M, rd, wr)
        T1a = T1.rearrange("g (p c) -> g p c", c=16)
        T2a = T2.rearrange("g (p c) -> g p c", c=16)
        Frb = Fr.unsqueeze(2).to_broadcast([64, 64, 16])
        Fib = Fi.unsqueeze(2).to_broadcast([64, 64, 16])
        self.cmul(Bbr, Bbi, Frb, Fib, br, bi, T1a, T2a, rd, wr)
        for i in range(4):
            k = 3 - i
            pr = Apr[:, k, :].unsqueeze(2).to_broadcast([64, 64, 16])
            pi_ = Api[:, k, :].unsqueeze(2).to_broadcast([64, 64, 16])
            o_r = Wg[:, i, :, 0, :].rearrange("g c p -> g p c")
            o_i = Wg[:, i, :, 1, :].rearrange("g c p -> g p c")
            self.cmul(o_r, o_i, pr, pi_, Bbr, Bbi, T1a, T2a, rd, wr + [uWg])
            self.ada_tick()
        self.dma(brc, c_re_d, reads=rd, writes=wr)
        self.dma(bic, c_im_d, reads=rd, writes=wr)
        T1c = T1.rearrange("g (c p) -> g c p", c=16)
        T2c = T2.rearrange("g (c p) -> g c p", c=16)
        for k in range(5):
            pr = Apr[:, k, :].unsqueeze(1).to_broadcast([64, 16, 64])
            pi_ = Api[:, k, :].unsqueeze(1).to_broadcast([64, 16, 64])
            self.tt(T1c, cr, pr, M, rd, wr)
            self.tt(T2c, ci, pi_, M, rd, wr)
            self.tt(CAg[:, k, :, 0, :], T1c, T2c, S, rd, wr + [uCAg])
            self.tt(T1c, cr, pi_, M, rd, wr)
            self.tt(T2c, ci, pr, M, rd, wr)
            self.stt(CAg[:, k, :, 1, :], T1c, -1.0, T2c, M, S, rd, wr + [uCAg])
            self.ada_tick()
        for k in range(5):
            pb = 6 + (k % 2)
            psb = PS[pb][:].bitcast(BF16)
            for c in range(16):
                self.tr(psb[:, c * 64:(c + 1) * 64], CAg[:, k, c, :, :].rearrange("g r p -> g (r p)"),
                        ident_b[0:64, 0:64], [uCAg, uID], [uPS[pb]])
            self.cp(CAp[:, :, k, :], psb[:, 0:1024].rearrange("p (c g) -> p g c", c=16), [uPS[pb]], [uCAp],
                    eng=self.evac_eng())
            self.ada_tick()
        A2.reset()
        Wpp, uWpp = A2.alloc([128, 4, 64, 32], BF16, "Wpp")
        self.memset(Wpp[:].rearrange("p a b c -> p (a b c)"), 0.0, [uWpp], eng="gpsimd")
        for i in range(4):
            pb = 6 + (i % 2)
            psb = PS[pb][:].bitcast(BF16)
            for c in range(16):
                self.tr(psb[:, c * 64:(c + 1) * 64], Wg[:, i, c, :, :].rearrange("g r p -> g (r p)"),
                        ident_b[0:64, 0:64], [uWg, uID], [uPS[pb]])
            self.cp(Wpp[:, i, :, 0:16], psb[:, 0:1024].rearrange("p (c g) -> p g c", c=16), [uPS[pb]], [uWpp],
                    eng=self.evac_eng())
        A3.reset()
        BpK, uBpK = A3.alloc([128, 64, 128], BF16, "BpK")
        CAL, uCAL = A3.alloc([128, 64, 4, 32], BF16, "CAL")
        self.memset(BpK[:].rearrange("p a b -> p (a b)"), 0.0, [uBpK], eng="gpsimd")
        BpKv = BpK.rearrange("p (t g8) (blk c) -> p t g8 blk c", g8=8, c=16)
        Wp3 = Wpp[:, 3, :, :].rearrange("p (t g8) c -> p t g8 c", g8=8)
        for g8 in range(8):
            self.cp(BpKv[:, :, g8, g8, :], Wp3[:, :, g8, 0:16], [uWpp], [uBpK], eng=("vector" if g8 % 2 else "gpsimd"))
        for kk in range(8):
            pb = 6 + (kk % 2)
            psb = PS[pb][:].bitcast(BF16)
            j = 0
            for t in (2 * kk, 2 * kk + 1):
                for i in range(4):
                    self.tr(psb[:, j * 128:(j + 1) * 128], Wpp[:, i, 4 * t:4 * t + 4, :].rearrange("p a b -> p (a b)"),
                            ident_b[:, :], [uWpp, uID], [uPS[pb]])
                    j += 1
            self.cp(self.WT[:, 2 * kk:2 * kk + 2, :, :].rearrange("p a b c -> p (a b c)"), psb[:, 0:1024],
                    [uPS[pb]], [self.uWT], eng=self.evac_eng())
        for t8 in range(8):
            pb = self.bank()
            pv = PS[pb][:, :].rearrange("p (l c) -> p l c", l=4)
            for g8 in range(8):
                g = 8 * t8 + g8
                self.mm(pv[:, :, 16 * g8:16 * g8 + 16], BpK[:, g, :], CAp[:, g, 0:4, :], True, True,
                        [uBpK, uCAp], [uPS[pb]])
            self.cp(KL[:, t8, :, :].rearrange("p l c -> p (l c)"), PS[pb][:, :], [uPS[pb]], [uKL], eng=self.evac_eng())
        self.dma(self.KL_s, KL[:].rearrange("p a b c -> p (a b c)"), reads=[uKL])
        self.memset(CAL[:].rearrange("p a b c -> p (a b c)"), 0.0, [uCAL], eng="gpsimd")
        CALv = CAL.rearrange("p (gp two) j c -> p gp two j c", two=2)
        CApv = CAp.rearrange("p (gp two) k c -> p gp two k c", two=2)
        self.cp(CALv[:, :, 0, :, 0:16], CApv[:, :, 0, 1:5, :], [uCAp], [uCAL], eng="vector")
        self.cp(CALv[:, :, 1, :, 16:32], CApv[:, :, 1, 1:5, :], [uCAp], [uCAL], eng="gpsimd")
        self.dma(self.CAL_s, CAL[:].rearrange("p a b c -> p (a b c)"), reads=[uCAL])
        for idx, (src_r, src_i, dst) in enumerate(((Apr[:, 4, :], Api[:, 4, :], self.A4p), (A128r, A128i, self.A128p))):
            for ri, src in enumerate((src_r, src_i)):
                pb = self.bank()
                self.tr(PS[pb][0:64, 0:64], src, ident_f[0:64, 0:64], rd + [uID], [uPS[pb]])
                self.cp(dst[:, ri, :], PS[pb][0:64, 0:64], [uPS[pb]], [self.uAp], eng=self.evac_eng())
        self.dma(XS[1][0:8, 0:128], dvec_d, writes=[uXS[1]])
        pb = self.bank()
        self.tr(PS[pb][:, 0:8], XS[1][0:8, 0:128], ident_f[0:8, 0:8], [uXS[1], uID], [uPS[pb]])
        self.cp(self.dcol[:], PS[pb][:, 0:8], [uPS[pb]], [self.udcol])
        A1.reset(); A2.reset(); A3.reset()
        self.TBL_s = self.scratch("TBL_s", [64, 4, 64, 32])
        TB, uTB = A3.alloc([64, 4, 64, 32], F32, "TB")
        q1, _ = A2.alloc([64, 64, 16], F32, "q1"); q2, _ = A2.alloc([64, 64, 16], F32, "q2")
        pw, _ = A2.alloc([64, 4, 64], F32, "pw")
        s1, _ = A2.alloc([64, 64], F32, "s1"); s2, _ = A2.alloc([64, 64], F32, "s2"); s3, _ = A2.alloc([64, 64], F32, "s3")
        ut = U("tbl")
        A2.cur.append(ut)
        rdt, wrt = [ut, self.uAp, uTB], [ut]
        a4r, a4i = self.A4p[:, 0, :], self.A4p[:, 1, :]
        self.tt(s1, a4r, a4r, M, rdt, wrt); self.tt(s2, a4i, a4i, M, rdt, wrt); self.tt(s1, s1, s2, A, rdt, wrt)
        P.op("vector", lambda h: h.reciprocal(out=s2, in_=s1), rdt, wrt)
        self.cp(pw[:, 0, :], a4r, rdt, wrt); self.cp(pw[:, 1, :], a4i, rdt, wrt)
        self.tt(pw[:, 2, :], a4r, s2, M, rdt, wrt)
        self.stt(pw[:, 3, :], a4i, -1.0, s2, M, M, rdt, wrt)
        self.memset(TB[:, 0, :, 0:1], 1.0, [uTB]); self.memset(TB[:, 1, :, 0:1], 0.0, [uTB])
        self.cp(TB[:, 2, :, 0:1], pw[:, 2, :].unsqueeze(2), rdt, [uTB]); self.cp(TB[:, 3, :, 0:1], pw[:, 3, :].unsqueeze(2), rdt, [uTB])
        for k in (1, 2, 4, 8, 16):
            self.ada_tick()
            for base in (0, 2):
                br_ = pw[:, base, :].unsqueeze(2).to_broadcast([64, 64, k])
                bi_ = pw[:, base + 1, :].unsqueeze(2).to_broadcast([64, 64, k])
                self.cmul(TB[:, base, :, k:2 * k], TB[:, base + 1, :, k:2 * k], TB[:, base, :, 0:k], TB[:, base + 1, :, 0:k],
                          br_, bi_, q1[:, :, 0:k], q2[:, :, 0:k], rdt, wrt + [uTB])
            if k < 16:
                for base in (0, 2):
                    r_, i_ = pw[:, base, :], pw[:, base + 1, :]
                    self.tt(s1, r_, r_, M, rdt, wrt); self.tt(s2, i_, i_, M, rdt, wrt); self.tt(s3, r_, i_, M, rdt, wrt)
                    self.tt(r_, s1, s2, S, rdt, wrt); self.ts(i_, s3, 2.0, None, M, None, rdt, wrt)
        self.dma(self.TBL_s.rearrange("p a b c -> p (a b c)"), TB[:].rearrange("p a b c -> p (a b c)"), reads=[uTB])
        A2.reset(); A3.reset()
    def ln_block(self, src_rows, nrows, par, mk_sh, mk_sc, sample, dst, dst_u, dst_tok0):
        P = self.P
        PS, uPS = self.PS, self.uPS
        xs, ux = self.XS[par], self.uXS[par]
        xn, uxn = self.XN[par], self.uXN[par]
        stt, mv, ust = self.stt_t, self.mv, self.ust
        mod, umod = self.mod, self.umod
        self.dma(xs[0:nrows, :], src_rows, writes=[ux])
        self.ln_stats(xs, ux, nrows, par)
        self.ts(xn[0:nrows, :], xs[0:nrows, :], mv[0:nrows, par, 0:1], mv[0:nrows, par, 3:4], ALU.subtract, ALU.mult,
                [ux, ust[par]], [uxn])
        self.xn_to_fm(xn, uxn, nrows, mk_sh, mk_sc, sample, dst, dst_u, dst_tok0)

    def ln_stats(self, xs, ux, nrows, par):
        P = self.P
        stt, mv, ust = self.stt_t, self.mv, self.ust
        for c in range(4):
            P.op("vector", lambda h, c=c: h.bn_stats(out=stt[0:nrows, par, c, :], in_=xs[0:nrows, c * 512:(c + 1) * 512]),
                 [ux], [ust[par]])
        P.op("vector", lambda h: h.bn_aggr(out=mv[0:nrows, par, 0:2], in_=stt[0:nrows, par, :, :].rearrange("p a b -> p (a b)")),
             [ust[par]], [ust[par]])
        self.act(mv[0:nrows, par, 2:3], mv[0:nrows, par, 1:2], AF.Sqrt, [ust[par], self.ueps], [ust[par]],
                 bias=self.eps_t[0:nrows, :], scale=1.0)
        P.op("vector", lambda h: h.reciprocal(out=mv[0:nrows, par, 3:4], in_=mv[0:nrows, par, 2:3]), [ust[par]], [ust[par]])

    def xn_to_fm(self, xn, uxn, nrows, mk_sh, mk_sc, sample, dst, dst_u, dst_tok0):
        PS, uPS = self.PS, self.uPS
        mod, umod = self.mod, self.umod
        for half in range(2):
            pb = 6 + half
            psb = PS[pb][:].bitcast(BF16)
            for j in range(8):
                kt = half * 8 + j
                self.tr(psb[:, j * 128:j * 128 + nrows], xn[0:nrows, kt * 128:(kt + 1) * 128],
                        self.ident_b[0:nrows, 0:nrows], [uxn, self.uID], [uPS[pb]])
            for j in range(8):
                kt = half * 8 + j
                src = psb[:, j * 128:j * 128 + nrows]
                o = dst[:, kt, dst_tok0:dst_tok0 + nrows]
                if not sample:
                    if j % 2 == 0:
                        self.act(o, src, AF.Identity, [uPS[pb], umod], [dst_u],
                                 scale=mod[:, mk_sc, kt, 0:1], bias=mod[:, mk_sh, kt, 0:1])
                    else:
                        self.ts(o, src, mod[:, mk_sc, kt, 0:1], mod[:, mk_sh, kt, 0:1], ALU.mult, ALU.add,
                                [uPS[pb], umod], [dst_u])
                else:
                    self.tt(o, src, mod[:, mk_sc, kt, 1:65], ALU.mult, [uPS[pb], umod], [dst_u])
                    self.tt(o, o, mod[:, mk_sh, kt, 1:65], ALU.add, [umod, dst_u], [dst_u], eng="gpsimd")

    def phase1(self):
        sb, P = self.sb, self.P
        PS, uPS, XS, uXS = self.PS, self.uPS, self.XS, self.uXS
        ident_f, ident_b, uID = self.ident_f, self.ident_b, self.uID
        xa, w_in = self.xa, self.w_in
        M, A, S = ALU.mult, ALU.add, ALU.subtract
        A4 = self.AR4
        A4.reset(keep=16384)
        xn0, uxn0 = A4.alloc([128, D], BF16, "XN0"); xn1, uxn1 = A4.alloc([128, D], BF16, "XN1")
        self.XN = [xn0, xn1]; self.uXN = [uxn0, uxn1]
        self.stt_t = sb("stt", [128, 2, 4, 6], F32)
        self.mv = sb("mv", [128, 2, 4], F32)
        self.ust = [U("st0"), U("st1")]
        self.eps_t = sb("eps_t", [128, 1], F32)
        self.ueps = U("eps")
        self.memset(self.eps_t[:], LN_EPS, [self.ueps])
        HG, uHG = self.AR1.alloc([128, 16, NOWN], BF16, "HG")
        self.HG, self.uHG = HG, uHG
        ZB, uZB = self.AR2.alloc([64, 2, 8, 256], F32, "ZB")
        ZBs, _ = self.AR2.alloc([64, 2, 8, 16], F32, "ZBs")
        self.AR2.cur.append(uZB)
        UTb = [self.AR2.alloc([128, 1024], BF16, "UTb%d" % i) for i in range(2)]
        UO, uUO = self.AR3.alloc([128, 8, NOWN], BF16, "UO")
        self.UO, self.uUO = UO, uUO
        UPb = [self.AR3.alloc([128, 2, NOWN], BF16, "UPb%d" % i) for i in range(2)]
        SM, uSM = self.AR3.alloc([64, 2, 8, 272], BF16, "SM")
        SMBb, uSMBb = self.AR6.alloc([128, 8 * 272], BF16, "SMBb")
        KST = [XS[0][:, 0:1024], XS[1][:, 0:1024]]
        uKST = uXS
        KTB, uKTB = A4.alloc([128, 8, 128], BF16, "KTB")
        VB, uVB = A4.alloc([128, 1024], BF16, "VB")
        for i in range(2):
            self.memset(UPb[i][0][:].rearrange("p a b -> p (a b)"), 0.0, [UPb[i][1]], eng="gpsimd")
        ZBLK, uZBLK = A4.alloc([64, 2, 64, 8], F32, "ZBLKo")
        ZOWN, _ = A4.alloc([64, 2, 8, 8], F32, "ZOWN")
        A4.cur.append(uZBLK)
        uR = U("Rdummy")
        TBb, uTBb = self.AR3.alloc([64, 4, 8, 32], F32, "TBb")
        C31, uC31 = A4.alloc([64, 2, 8, 8], F32, "C31")
        BS, _ = A4.alloc([64, 2, 64], F32, "BS"); A4.cur.append(uC31)
        onec, _ = A4.alloc([64, 1], F32, "onec")
        self.memset(onec[:], 1.0, [uC31]); self.memset(BS[:].rearrange("p a b -> p (a b)"), 0.0, [uC31])
        uP2 = U("pass2"); A4.cur.append(uP2)
        F1, _ = A4.alloc([64, 2, 8, 8], F32, "F1"); F2, _ = A4.alloc([64, 2, 8, 8], F32, "F2")
        WW, _ = A4.alloc([64, 2, 8, 8], F32, "WW"); SALL, _ = A4.alloc([64, 2, 8, 9], F32, "SALL")
        A256, _ = A4.alloc([64, 2, 8], F32, "A256")
        TT, uTT = A4.alloc([64, 4, 8, 16], F32, "TT")
        SOWN, uSOWN = A4.alloc([64, 2, 8, 8], F32, "SOWN")
        SFIN, uSFIN = A4.alloc([64, 2, 64], F32, "SFIN")
        S0b, uS0 = A4.alloc([64, 2, 8, 16], F32, "S0b")
        SFSb, uSFS = A4.alloc([64, 2, 8, 16], F32, "SFSb")
        self.SMB_s = self.scratch("SMB_s", [8, 128, 8 * 272], BF16)
        A4p, A128p, uAp = self.A4p, self.A128p, self.uAp
        s0T = self.inp("s0T", [64, 2, 64, 16])
        sfin_o = self.outp("sfin_o", [64, 2, 64])
        sfs_o = self.outp("sfs_o", [64, 2, 64, 16])

        def cstep(ar, ai, sr, si, shape_n):
            g_, l_ = shape_n
            t = [TT[:, k, 0:g_, 0:l_] for k in range(4)]
            rdd = [uR, uAp, uZB, uZBLK, uSOWN, uSFIN, uSFS, uS0, uC31, uP2]
            self.tt(t[0], ar, sr, M, rdd, [uTT])
            self.tt(t[1], ai, si, M, rdd, [uTT])
            self.tt(t[2], ar, si, M, rdd, [uTT], eng="gpsimd")
            self.tt(t[3], ai, sr, M, rdd, [uTT], eng="gpsimd")
            self.tt(t[0], t[0], t[1], S, [uTT], [uTT])
            self.tt(t[2], t[2], t[3], A, [uTT], [uTT], eng="gpsimd")
            return t[0], t[2]

        def ssm_batch(gb, usrc, uusrc, ntok, is_A):
            nm = ntok // 4
            up, uup = UPb[gb % 2]
            gsl = slice(8 * gb, 8 * gb + 8)
            self.dma(TBb[:], self.TBL_s[:, :, gsl, :], writes=[uTBb])
            for g8 in range(8):
                self.dma(up[32 * (g8 % 4):32 * (g8 % 4) + 16, g8 // 4, 0:ntok], usrc[16 * g8:16 * g8 + 16, 0:ntok],
                         reads=[uusrc], writes=[uup])
            for tt_ in range(2):
                t = 2 * gb + tt_
                for ri in range(2):
                    banks = [(2 * ri + tt_ * 0 + q_) % 6 for q_ in range(4)]
                    banks = [self.bank() for _ in range(4)]
                    for i in range(4):
                        for q in range(4):
                            upv = up[32 * q:32 * q + 32, tt_, 0:ntok].rearrange("p (m i) -> p m i", i=4)
                            pb = banks[q]
                            self.mm(PS[pb][0:64, 0:nm], self.WT[32 * q:32 * q + 32, t, i, 64 * ri:64 * ri + 64], upv[:, :, i],
                                    i == 0, i == 3, [self.uWT, uup], [uPS[pb]], tile_position=(32 * q, 0))
                    for q in range(4):
                        g8 = 4 * tt_ + q
                        pb = banks[q]
                        self.cp(ZB[:, ri, g8, :], PS[pb][0:64, 0:256], [uPS[pb]], [uZB], eng=self.evac_eng())
                        if is_A:
                            self.cp(ZBs[:, ri, g8, :], PS[pb][0:64, 256:272], [uPS[pb]], [uZB], eng=self.evac_eng())
            tA = self.XN[0][0:64, :].bitcast(F32); utA = self.uXN[0]
            tB = self.XN[1][0:64, :].bitcast(F32); utB = self.uXN[1]
            for hf in range(2):
                g4 = slice(4 * hf, 4 * hf + 4)
                Zr = ZB[:, 0, g4, :].rearrange("p g (b m) -> p g b m", m=32)
                Zi = ZB[:, 1, g4, :].rearrange("p g (b m) -> p g b m", m=32)
                nr_ = TBb[:, 2, g4, :].unsqueeze(2).to_broadcast([64, 4, 8, 32])
                ni_ = TBb[:, 3, g4, :].unsqueeze(2).to_broadcast([64, 4, 8, 32])
                t1 = tA.rearrange("p (g b m) -> p g b m", g=4, b=8)
                t2 = tB.rearrange("p (g b m) -> p g b m", g=4, b=8)
                self.tt(t1, nr_, Zr, M, [uTBb, uZB], [utA])
                self.tt(t2, ni_, Zi, M, [uTBb, uZB], [utB], eng="gpsimd")
                self.tt(t1, t1, t2, S, [utA, utB], [utA])
                self.tt(t2, ni_, Zr, M, [uTBb, uZB, utA], [utB], eng="gpsimd")
                self.cp(Zr, t1, [utA, utB], [uZB])
                self.tt(t1, nr_, Zi, M, [uTBb, uZB], [utA])
                self.tt(Zi, t1, t2, A, [utA, utB], [uZB])
            a128r = A128p[:, 0, gsl].unsqueeze(2).to_broadcast([64, 8, 8])
            a128i = A128p[:, 1, gsl].unsqueeze(2).to_broadcast([64, 8, 8])
            if not is_A:
                for ri in range(2):
                    P.op("vector", lambda h, ri=ri: h.tensor_reduce(out=C31[:, ri, :, :], in_=ZB[:, ri, :, :].rearrange("p g (b m) -> p g b m", m=32),
                                                                     axis=AX.X, op=A), [uZB], [uC31])
                pr, pi_ = cstep(a128r, a128i, C31[:, 0, :, :], C31[:, 1, :, :], (8, 8))
                self.cp(ZBLK[:, 0, gsl, :], pr, [uTT], [uZBLK])
                self.cp(ZBLK[:, 1, gsl, :], pi_, [uTT], [uZBLK], eng="gpsimd")
                return
            for ri in range(2):
                flat = ZB[:, ri, :, :].rearrange("p g m -> p (g m)")
                P.op("vector", lambda h, flat=flat: h.tensor_tensor_scan(out=flat, data0=onec[:, 0:1].to_broadcast([64, 2048]), data1=flat,
                                                                          initial=0.0, op0=M, op1=A), [uZB, uC31], [uZB])
                seg = flat.rearrange("p (s m) -> p s m", m=32)
                self.cp(BS[:, ri, 1:64], seg[:, 0:63, 31], [uZB], [uC31], eng="gpsimd")
                self.tt(seg, seg, BS[:, ri, :].unsqueeze(2).to_broadcast([64, 64, 32]), S, [uZB, uC31], [uZB])
            Cv = [ZB[:, ri, :, :].rearrange("p g (b m) -> p g b m", m=32) for ri in range(2)]
            pr, pi_ = cstep(a128r, a128i, Cv[0][:, :, :, 31], Cv[1][:, :, :, 31], (8, 8))
            self.cp(ZOWN[:, 0, :, :], pr, [uTT], [uZBLK])
            self.cp(ZOWN[:, 1, :, :], pi_, [uTT], [uZBLK], eng="gpsimd")
            hf_ = self.hf[:, 0:1]
            for ri in range(2):
                X = ZOWN[:, ri, :, :]
                Y = ZBLK[:, ri, gsl, :]
                self.tt(F2[:, ri], Y, X, S, [uZBLK], [uP2])
                self.stt(F1[:, ri], F2[:, ri], hf_, X, M, A, [uP2, self.uhf, uZBLK], [uP2])
                self.tt(F2[:, ri], X, Y, A, [uZBLK, uP2], [uP2])
                self.tt(F2[:, ri], F2[:, ri], F1[:, ri], S, [uP2], [uP2])
            pr, pi_ = cstep(a128r, a128i, F1[:, 0], F1[:, 1], (8, 8))
            self.tt(WW[:, 0], pr, F2[:, 0], A, [uTT, uP2], [uP2])
            self.tt(WW[:, 1], pi_, F2[:, 1], A, [uTT, uP2], [uP2], eng="gpsimd")
            t = [TT[:, k, :, 0] for k in range(4)]
            b128r, b128i = A128p[:, 0, gsl], A128p[:, 1, gsl]
            self.tt(t[0], b128r, b128r, M, [uAp], [uTT]); self.tt(t[1], b128i, b128i, M, [uAp], [uTT])
            self.tt(A256[:, 0, :], t[0], t[1], S, [uTT], [uP2])
            self.tt(t[2], b128r, b128i, M, [uAp], [uTT])
            self.ts(A256[:, 1, :], t[2], 2.0, None, M, None, [uTT], [uP2])
            self.memset(SALL[:, :, :, 0:1], 0.0, [uP2])
            for i in range(8):
                sr, si = SALL[:, 0, :, i], SALL[:, 1, :, i]
                self.tt(t[0], A256[:, 0, :], sr, M, [uP2], [uTT]); self.tt(t[1], A256[:, 1, :], si, M, [uP2], [uTT])
                self.tt(t[2], A256[:, 0, :], si, M, [uP2], [uTT], eng="gpsimd"); self.tt(t[3], A256[:, 1, :], sr, M, [uP2], [uTT], eng="gpsimd")
                self.tt(t[0], t[0], t[1], S, [uTT], [uTT]); self.tt(t[2], t[2], t[3], A, [uTT], [uTT], eng="gpsimd")
                self.tt(SALL[:, 0, :, i + 1], t[0], WW[:, 0, :, i], A, [uTT, uP2], [uP2])
                self.tt(SALL[:, 1, :, i + 1], t[2], WW[:, 1, :, i], A, [uTT, uP2], [uP2], eng="gpsimd")
            pr, pi_ = cstep(a128r, a128i, SALL[:, 0, :, 0:8], SALL[:, 1, :, 0:8], (8, 8))
            for ri, pp in ((0, pr), (1, pi_)):
                self.tt(F2[:, ri], pp, F1[:, ri], A, [uTT, uP2], [uP2])
                self.tt(F2[:, ri], F2[:, ri], SALL[:, ri, :, 0:8], S, [uP2], [uP2])
                self.stt(SOWN[:, ri, :, :], F2[:, ri], hf_, SALL[:, ri, :, 0:8], M, A, [uP2, self.uhf], [uSOWN])
            self.cp(SFIN[:, 0, gsl], SALL[:, 0, :, 8], [uP2], [uSFIN])
            self.cp(SFIN[:, 1, gsl], SALL[:, 1, :, 8], [uP2], [uSFIN])
            for pc in range(4):
                g2 = slice(2 * pc, 2 * pc + 2)
                Dr = tA[:, 0:512].rearrange("p (g b m) -> p g b m", g=2, b=8)
                Di = tA[:, 512:1024].rearrange("p (g b m) -> p g b m", g=2, b=8)
                p1 = tB[:, 0:512].rearrange("p (g b m) -> p g b m", g=2, b=8)
                p2 = tB[:, 512:1024].rearrange("p (g b m) -> p g b m", g=2, b=8)
                for ri, Dx in ((0, Dr), (1, Di)):
                    e_ = "vector" if ri == 0 else "gpsimd"
                    self.tt(Dx[:, :, :, 1:32], Cv[ri][:, g2, :, 0:31], SOWN[:, ri, g2, :].unsqueeze(3).to_broadcast([64, 2, 8, 31]), A,
                            [uZB, uSOWN], [utA], eng=e_)
                    self.cp(Dx[:, :, :, 0:1], SOWN[:, ri, g2, :].unsqueeze(3), [uSOWN], [utA], eng=e_)
                tr_ = TBb[:, 0, g2, :].unsqueeze(2).to_broadcast([64, 2, 8, 32])
                ti_ = TBb[:, 1, g2, :].unsqueeze(2).to_broadcast([64, 2, 8, 32])
                smr = SM[:, 0, g2, 0:256].rearrange("p g (b m) -> p g b m", m=32)
                smi = SM[:, 1, g2, 0:256].rearrange("p g (b m) -> p g b m", m=32)
                self.tt(p1, tr_, Dr, M, [uTBb, utA], [utB]); self.tt(p2, ti_, Di, M, [uTBb, utA], [utB], eng="gpsimd")
                self.tt(smr, p1, p2, S, [utB], [uSM])
                self.tt(p1, tr_, Di, M, [uTBb, utA, uSM], [utB]); self.tt(p2, ti_, Dr, M, [uTBb, utA, uSM], [utB], eng="gpsimd")
                self.tt(smi, p1, p2, A, [utB], [uSM])
            self.dma(S0b[:], s0T[:, :, gsl, :], writes=[uS0])
            s0r, s0i = S0b[:, 0, :, :], S0b[:, 1, :, :]
            self.cp(SM[:, 0, :, 256:272], s0r, [uS0], [uSM], eng="scalar")
            self.cp(SM[:, 1, :, 256:272], s0i, [uS0], [uSM], eng="scalar")
            a4r16 = A4p[:, 0, gsl].unsqueeze(2).to_broadcast([64, 8, 16])
            a4i16 = A4p[:, 1, gsl].unsqueeze(2).to_broadcast([64, 8, 16])
            pr, pi_ = cstep(a4r16, a4i16, s0r, s0i, (8, 16))
            self.tt(SFSb[:, 0, :, :], pr, ZBs[:, 0, :, :], A, [uTT, uZB], [uSFS])
            self.tt(SFSb[:, 1, :, :], pi_, ZBs[:, 1, :, :], A, [uTT, uZB], [uSFS], eng="gpsimd")
            self.dma(sfs_o[:, :, gsl, :], SFSb[:], reads=[uSFS])
            SMf = SM[:].rearrange("p r g m -> p r (g m)")
            for c0 in range(0, 8 * 272, 512):
                n = min(512, 8 * 272 - c0)
                pb = self.bank()
                self.mm(PS[pb][:, 0:n], self.esel[:, 0, :], SMf[:, 0, c0:c0 + n], True, False, [self.uhf, uSM], [uPS[pb]])
                self.mm(PS[pb][:, 0:n], self.esel[:, 1, :], SMf[:, 1, c0:c0 + n], False, True, [self.uhf, uSM], [uPS[pb]])
                self.cp(SMBb[:, c0:c0 + n], PS[pb][:, 0:n], [uPS[pb]], [uSMBb], eng=self.evac_eng())
            self.dma(self.SMB_s[gb], SMBb[:, :], reads=[uSMBb])

        groupB = [(1024 + 128 * j, 128, False, False, None) for j in range(8)]
        groupA = [(128 * j, 128, False, True, 128 * j) for j in range(8)] + [(2048, 64, True, True, 1024)]
        for gi, grp in enumerate((groupB, groupA)):
            is_A = gi == 1
            toks = 0
            offs = []
            for bi, (r0, nr, is_s, is_own, orow) in enumerate(grp):
                self.ln_block(xa[r0:r0 + nr, :], nr, bi % 2, 0, 1, is_s, HG, uHG, toks)
                offs.append(toks)
                toks += nr
            ntok = toks
            chunks = [(0, 512), (512, 512)] + ([(1024, 64)] if is_A else [])
            for cb in range(2):
                def ev(ms, t0, n, ps, ups, cb=cb):
                    gb = cb * 4 + ms
                    if is_A:
                        self.cp(UO[:, gb, t0:t0 + n], ps, [ups], [uUO], eng=self.evac_eng())
                        if t0 + n == ntok:
                            ssm_batch(gb, UO[:, gb, :], uUO, ntok, True)
                    else:
                        ut, uut = UTb[gb % 2]
                        self.cp(ut[:, t0:t0 + n], ps, [ups], [uut], eng=self.evac_eng())
                        if t0 + n == ntok:
                            ssm_batch(gb, ut, uut, ntok, False)
                self.proj_fm(w_in, 0, 16, cb * 512, 512, lambda kt, t0, n: HG[:, kt, t0:t0 + n], [uHG], chunks, ev)
            for which, cbase in (("k", 2048), ("v", 3072)):
                wbs = [self.wload(w_in, 0, 16, cbase + cb * 512, 512) for cb in range(2)]
                for bi, (r0, nr, is_s, is_own, orow) in enumerate(grp):
                    st_i = bi % 2
                    for cb in range(2):
                        wb, uwb = wbs[cb]
                        pb = self.bank()
                        for kt in range(16):
                            self.mm(PS[pb][0:nr, :], HG[:, kt, offs[bi]:offs[bi] + nr], wb[:, kt, :], kt == 0, kt == 15,
                                    [uHG, uwb], [uPS[pb]])
                        self.cp(KST[st_i][0:nr, cb * 512:(cb + 1) * 512], PS[pb][0:nr, :], [uPS[pb]], [uKST[st_i]],
                                eng=self.evac_eng())
                    tok0 = (r0 if not is_s else 2048)
                    if is_own:
                        if is_s:
                            dst = (self.k_s if which == "k" else self.v_s)[0:64, :]
                        else:
                            dst = (self.k_own if which == "k" else self.v_own)[orow:orow + 128, :]
                        self.dma(dst, KST[st_i][0:nr, :], reads=[uKST[st_i]])
                    if which == "k":
                        for hh in range(8):
                            pq = 6 + (hh % 2)
                            self.tr(PS[pq][:, 0:nr], KST[st_i][0:nr, hh * 128:(hh + 1) * 128],
                                    ident_f[0:nr, 0:nr], [uKST[st_i], uID], [uPS[pq]])
                            self.cp(KTB[:, hh, 0:nr], PS[pq][:, 0:nr], [uPS[pq]], [uKTB], eng=self.evac_eng())
                        self.dma(self.KT_s[:, :, tok0:tok0 + nr].rearrange("h p t -> p h t"), KTB[:, :, 0:nr], reads=[uKTB])
                    else:
                        self.cp(VB[0:nr, :], KST[st_i][0:nr, :], [uKST[st_i]], [uVB], eng="gpsimd")
                        self.dma(self.V_s[:, tok0:tok0 + nr, :].rearrange("h t d -> t h d"),
                                 VB[0:nr, :].rearrange("t (h d) -> t h d", h=8), reads=[uVB])
        self.dma(sfin_o, SFIN[:], reads=[uSFIN])
        self.AR2.reset()

    def phase_ssm_y(self):
        sb, P = self.sb, self.P
        PS, uPS = self.PS, self.uPS
        M, A, S = ALU.mult, ALU.add, ALU.subtract
        UO, uUO, HG, uHG = self.UO, self.uUO, self.HG, self.uHG
        self.AR3.reset(keep=8 * NOWN * 2)
        self.AR4.reset()
        ZACT, uZACT = self.AR3.alloc([128, 8, NOWN], BF16, "ZACT")
        GLUO, uGLUO = self.AR4.alloc([128, 8, NOWN], BF16, "GLUO")
        KLt = [self.AR2.alloc([128, 4, 128], BF16, "KLt%d" % i) for i in range(2)]
        CALt = [self.AR2.alloc([128, 8, 4, 32], BF16, "CALt%d" % i) for i in range(2)]
        SMBt = [self.AR2.alloc([128, 8, 272], BF16, "SMBt%d" % i) for i in range(2)]
        Yt, uYt = self.AR2.alloc([128, 512], F32, "Yt")
        Tt, uTt = self.AR2.alloc([128, 512], F32, "Tt")
        chunks = [(0, 512, 0), (512, 512, 128), (1024, 64, 256)]
        for t8 in range(8):
            kl, ukl = KLt[t8 % 2]; cal, ucal = CALt[t8 % 2]; smb, usmb = SMBt[t8 % 2]
            self.dma(kl[:].rearrange("p a b -> p (a b)"), self.KL_s[:, t8 * 512:(t8 + 1) * 512], writes=[ukl])
            self.dma(cal[:].rearrange("p a b c -> p (a b c)"), self.CAL_s[:, t8 * 1024:(t8 + 1) * 1024], writes=[ucal])
            self.dma(smb[:].rearrange("p a b -> p (a b)"), self.SMB_s[t8], writes=[usmb])
            for (t0, n, m0) in chunks:
                nm = n // 4
                pb = self.bank()
                pv = PS[pb][:, 0:n].rearrange("p (m j) -> p m j", j=4)
                uv = UO[:, t8, t0:t0 + n].rearrange("p (m j) -> p m j", j=4)
                for l in range(4):
                    self.mm(pv[:, :, l:4], kl[:, l, :], uv[:, :, 0:4 - l], l == 0, False, [ukl, uUO], [uPS[pb]])
                for g8 in range(8):
                    pair = g8 // 2
                    for j in range(4):
                        self.mm(pv[32 * pair:32 * pair + 32, :, j], cal[:, g8, j, :], smb[:, g8, m0:m0 + nm], False,
                                (g8 == 7 and j == 3), [ucal, usmb], [uPS[pb]], tile_position=(0, 32 * pair))
                self.stt(Yt[:, 0:n], UO[:, t8, t0:t0 + n], self.dcol[:, t8:t8 + 1], PS[pb][:, 0:n], M, A,
                         [uUO, self.udcol, uPS[pb]], [uYt])
                self.tt(Tt[:, 0:n], Yt[:, 0:n], Yt[:, 0:n], M, [uYt], [uTt], eng="gpsimd")
                self.ts(Tt[:, 0:n], Tt[:, 0:n], 0.044715, 1.0, M, A, [uTt], [uTt], eng="gpsimd")
                self.tt(Tt[:, 0:n], Tt[:, 0:n], Yt[:, 0:n], M, [uTt, uYt], [uTt], eng="gpsimd")
                self.act(Tt[:, 0:n], Tt[:, 0:n], AF.Sigmoid, [uTt], [uTt], scale=1.5957691216057308)
                self.tt(ZACT[:, t8, t0:t0 + n], Yt[:, 0:n], Tt[:, 0:n], M, [uYt, uTt], [uZACT])
        pchunks = [(0, 512), (512, 512), (1024, 64)]
        for cb in range(2):
            def ev(ms, t0, n, ps, ups, cb=cb):
                self.act(Tt[:, 0:n], ps, AF.Sigmoid, [ups], [uTt])
                self.tt(GLUO[:, cb * 4 + ms, t0:t0 + n], ZACT[:, cb * 4 + ms, t0:t0 + n], Tt[:, 0:n], M, [uZACT, uTt], [uGLUO])
            self.proj_fm(self.w_glu, 0, 8, cb * 512, 512, lambda kt, t0, n: ZACT[:, kt, t0:t0 + n], [uZACT], pchunks, ev)
        self.AR3.reset()
        MIX, uMIX = self.AR3.alloc([128, 16, NOWN], BF16, "MIX")
        self.MIX, self.uMIX = MIX, uMIX
        self.gated_up(self.w_up_ssm, GLUO, uGLUO, 4096, first=True)
        self.AR2.reset(); self.AR4.reset()

    def gated_up(self, w_up, src, usrc, gate_c0, first):
        PS, uPS = self.PS, self.uPS
        HG, uHG, MIX, uMIX = self.HG, self.uHG, self.MIX, self.uMIX
        M, A = ALU.mult, ALU.add
        self.AR6.reset()
        SG = [self.AR6.alloc([128, 512], F32, "SG%d" % i) for i in range(2)]
        pchunks = [(0, 512), (512, 512), (1024, 64)]
        for cb in range(4):
            wA, uwA = self.wload(w_up, 0, 8, cb * 512, 512)
            wB, uwB = self.wload(self.w_in, 0, 16, gate_c0 + cb * 512, 512)
            for ms in range(4):
                mt = cb * 4 + ms
                for (t0, n) in pchunks:
                    pa = self.bank()
                    for kt in range(8):
                        self.mm(PS[pa][:, 0:n], wA[:, kt, ms * 128:(ms + 1) * 128], src[:, kt, t0:t0 + n], kt == 0, kt == 7,
                                [uwA, usrc], [uPS[pa]])
                    pg = self.bank()
                    for kt in range(16):
                        self.mm(PS[pg][:, 0:n], wB[:, kt, ms * 128:(ms + 1) * 128], HG[:, kt, t0:t0 + n], kt == 0, kt == 15,
                                [uwB, uHG], [uPS[pg]])
                    sg, usg = SG[(mt + (t0 // 512)) % 2]
                    self.act(sg[:, 0:n], PS[pg][:, 0:n], AF.Sigmoid, [uPS[pg]], [usg])
                    if first:
                        self.tt(MIX[:, mt, t0:t0 + n], sg[:, 0:n], PS[pa][:, 0:n], M, [usg, uPS[pa]], [uMIX])
                    else:
                        self.tt(sg[:, 0:n], sg[:, 0:n], PS[pa][:, 0:n], M, [usg, uPS[pa]], [usg])
                        self.tt(MIX[:, mt, t0:t0 + n], MIX[:, mt, t0:t0 + n], sg[:, 0:n], A, [usg, uMIX], [uMIX], eng="gpsimd")

    def phase_attn(self):
        sb, P, inp = self.sb, self.P, self.inp
        PS, uPS, XS, uXS = self.PS, self.uPS, self.XS, self.uXS
        ident_f, ident_b, uID = self.ident_f, self.ident_b, self.uID
        M, A, S = ALU.mult, ALU.add, ALU.subtract
        HG, uHG = self.HG, self.uHG
        SCALE = 0.125
        bt_d = inp("bias_tab", [128, 128]); tri_d = inp("tri", [128, 256], BF16)
        ka_d = inp("ka", [4, 17 * 128], BF16); qa_d = inp("qa", [4, 512], BF16)
        ms_d = inp("msk", [64, 16 * 64], BF16); lamv_d = inp("lamv", [1, 256]); subg_d = inp("subg", [128, 1])
        iota_d = inp("iota", [128, 1]); ptab_d = inp("ptab", [1, 256], I32)
        cache_k = inp("cache_k", [self.npool_rows, 1024]); cache_v = inp("cache_v", [self.npool_rows, 1024])
        A2, A4 = self.AR2, self.AR4
        self.AR5.reset(); self.AR6.reset()
        A3, A5, A6 = self.AR3, self.AR5, self.AR6
        QO, uQO = A2.alloc([128, 8, NOWN], BF16, "QO")
        AT, uAT = A4.alloc([128, 8, NOWN], BF16, "AT")
        self.AT, self.uAT = AT, uAT
        KTh = [A4.alloc([128, NTOK], BF16, "KTh%d" % i) for i in range(2)]
        Vh = [A4.alloc([128, 17, 129], BF16, "Vh%d" % i) for i in range(2)]
        cst = U("attn_const")
        BT = sb("BT", [128, 8, 2, 8], F32); TRI = sb("TRI", [128, 256], BF16)
        KA, _ = A5.alloc([4, 17, 128], BF16, "KA"); A5.cur.append(cst)
        QA = sb("QA", [4, 512], BF16); MS = sb("MS", [64, 16, 64], BF16)
        lamv = XS[1][0:1, 1024:1280].rearrange("p (a b) -> p a b", a=4); lsc = sb("lsc", [128, 8], F32); subg = sb("subg", [128, 1], F32)
        iota = sb("iota", [128, 1], F32); PTB = XS[1][:, 0:256].bitcast(I32); IDX = sb("IDX", [128, 256], I32)
        ones1 = sb("ones1", [1, 128], F32); CSEL = sb("CSEL", [36, 4], F32)
        for t_, d_ in ((BT[:].rearrange("p a b c -> p (a b c)"), bt_d), (TRI[:], tri_d), (KA[:].rearrange("p a b -> p (a b)"), ka_d),
                       (QA[:], qa_d), (MS[:].rearrange("p a b -> p (a b)"), ms_d), (lamv[:].rearrange("p a b -> p (a b)"), lamv_d),
                       (subg[:], subg_d), (iota[:], iota_d)):
            self.dma(t_, d_, writes=[cst])
        self.dma(PTB[:], ptab_d[0, :].partition_broadcast(128), writes=[cst])
        self.ts(IDX[:], PTB[:], 128.0, iota[:, 0:1], M, A, [cst], [cst])
        self.memset(ones1[:], 1.0, [cst])
        self.tt(lamv[:, 0, :], lamv[:, 0, :], lamv[:, 1, :], M, [cst], [cst])
        self.tt(lamv[:, 2, :], lamv[:, 2, :], lamv[:, 3, :], M, [cst], [cst])
        P.op("vector", lambda h: h.tensor_reduce(out=lamv[:, 1, 0:1], in_=lamv[:, 0, :], axis=AX.X, op=A), [cst], [cst])
        P.op("vector", lambda h: h.tensor_reduce(out=lamv[:, 1, 1:2], in_=lamv[:, 2, :], axis=AX.X, op=A), [cst], [cst])
        self.act(lamv[:, 1, 0:2], lamv[:, 1, 0:2], AF.Exp, [cst], [cst])
        self.tt(lamv[:, 1, 2:3], lamv[:, 1, 0:1], lamv[:, 1, 1:2], S, [cst], [cst])
        self.ts(lamv[:, 1, 2:3], lamv[:, 1, 2:3], 0.2, None, A, None, [cst], [cst])
        pb = self.bank()
        self.mm(PS[pb][:, 0:1], ones1[:, :], lamv[:, 1, 2:3], True, True, [cst], [uPS[pb]])
        self.cp(lsc[:, 0:1], PS[pb][:, 0:1], [uPS[pb]], [cst])
        self.ts(lsc[:, 1:2], lsc[:, 0:1], -1.0, None, M, None, [cst], [cst])
        self.ts(subg[:], subg[:], 0.8, None, M, None, [cst], [cst])
        self.memset(CSEL[:], 0.0, [cst])
        self.cp(CSEL[0:4, :], ident_f[0:4, 0:4], [cst, uID], [cst])
        self.ts(CSEL[32:36, :], ident_f[32:36, 32:36], lsc[32:36, 1:2], None, M, None, [cst, uID], [cst])
        pchunks = [(0, 512), (512, 512), (1024, 64)]
        for cb in range(2):
            def ev(ms, t0, n, ps, ups, cb=cb):
                self.cp(QO[:, cb * 4 + ms, t0:t0 + n], ps, [ups], [uQO], eng=self.evac_eng())
            self.proj_fm(self.w_in, 0, 16, 1024 + cb * 512, 512, lambda kt, t0, n: HG[:, kt, t0:t0 + n], [uHG], pchunks, ev)
        QP = [A2.alloc([128, 2, 128], BF16, "QP%d" % i) for i in range(2)]
        for qp, uqp in QP:
            self.memset(qp[:].rearrange("p a b -> p (a b)"), 0.0, [uqp])
        PT = [A2.alloc([128, 512], BF16, "PT%d" % i) for i in range(2)]
        ON, uON = A3.alloc([128, 4, 128], F32, "ON")
        ONb = XS[0][:, 1544:1672].bitcast(BF16); uONb = uXS[0]
        sc = sb("sc", [128, 16], F32); usc = U("sc")
        for v_, uv_ in Vh:
            self.memset(v_[:, :, 128:129], 1.0, [uv_])
        QSb = [A2.alloc([128, 8, 64], BF16, "QS%d" % i) for i in range(2)]
        for q_, uq_ in QSb:
            self.memset(q_[:].rearrange("p a b -> p (a b)"), 0.0, [uq_], eng="gpsimd")

        def finish_rows(nrows, o1, o2, z1, z2, dst_tok0, ntk, hsel):
            pass

        cnt = {"n": 0}

        def qk_item(it):
            hh, i, kb, kind, rel, first, last, kt_, ukt, vv, uvv, qp, uqp, bo = it
            n = cnt["n"]; cnt["n"] += 1
            ps_ = n % 2
            pt, upt = PT[n % 2]
            self.mm(PS[ps_][:, 0:128], kt_[:, kb * 128:(kb + 1) * 128], qp[:, 0, :], True, True, [ukt, uqp], [uPS[ps_]])
            self.mm(PS[ps_][:, 128:256], kt_[:, kb * 128:(kb + 1) * 128], qp[:, 1, :], True, True, [ukt, uqp], [uPS[ps_]])
            self.act(pt[:, 0:256], PS[ps_][:, 0:256], AF.Exp, [uPS[ps_], cst], [upt], scale=SCALE,
                     bias=BT[:, hh, kind, rel:rel + 1])
            if kind == 0 and rel == 0:
                self.tt(pt[:, 0:256], pt[:, 0:256], TRI[:, :], M, [upt, cst], [upt], eng="gpsimd")
            return (pt, upt)

        def pv_item(it, ptt):
            hh, i, kb, kind, rel, first, last, kt_, ukt, vv, uvv, qp, uqp, bo = it
            pt, upt = ptt
            self.mm(PS[bo][:, 0:129], pt[:, 0:128], vv[:, kb, :], first, last, [upt, uvv], [uPS[bo]])
            self.mm(PS[bo + 1][:, 0:129], pt[:, 128:256], vv[:, kb, :], first, last, [upt, uvv], [uPS[bo + 1]])
            if last:
                finalize(hh, i, bo)

        def finalize(hh, i, bo):
            c = 2 * (i % 2)
            P.op("vector", lambda h, bo=bo, c=c: h.reciprocal(out=sc[:, c:c + 1], in_=PS[bo][:, 128:129]), [uPS[bo]], [usc])
            P.op("vector", lambda h, bo=bo, c=c: h.reciprocal(out=sc[:, c + 1:c + 2], in_=PS[bo + 1][:, 128:129]), [uPS[bo + 1]], [usc])
            self.tt(sc[:, c + 1:c + 2], sc[:, c + 1:c + 2], lsc[:, 0:1], M, [usc, cst], [usc])
            o_ = ON[:, i % 2, 0:128]
            t_ = ON[:, 2 + i % 2, 0:128]
            self.ts(t_, PS[bo + 1][:, 0:128], sc[:, c + 1:c + 2], None, M, None, [uPS[bo + 1], usc], [uON])
            self.stt(o_, PS[bo][:, 0:128], sc[:, c:c + 1], t_, M, S, [uPS[bo], usc, uON], [uON])
            self.act(t_, o_, AF.Square, [uON], [uON, usc], accum_out=sc[:, 4 + c:5 + c])
            self.ts(sc[:, 5 + c:6 + c], sc[:, 4 + c:5 + c], 1.0 / 128, LN_EPS, M, A, [usc], [usc])
            self.act(sc[:, 5 + c:6 + c], sc[:, 5 + c:6 + c], AF.Sqrt, [usc], [usc])
            P.op("vector", lambda h, c=c: h.reciprocal(out=sc[:, 5 + c:6 + c], in_=sc[:, 5 + c:6 + c]), [usc], [usc])
            self.ts(ONb[:, (i % 2) * 128:(i % 2) * 128 + 128], o_, sc[:, 5 + c:6 + c], None, M, None, [uON, usc], [uONb])
            psb = PS[6 + i % 2][:].bitcast(BF16)
            self.tr(psb[:, 0:128], ONb[:, (i % 2) * 128:(i % 2) * 128 + 128], ident_b[:, :], [uONb, uID], [uPS[6 + i % 2]])
            self.ts(AT[:, hh, i * 128:(i + 1) * 128], psb[:, 0:128], subg[:, 0:1], None, M, None, [uPS[6 + i % 2], cst], [uAT])

        prev = None
        for hh in range(8):
            kt_, ukt = KTh[hh % 2]
            vv, uvv = Vh[hh % 2]
            self.dma(kt_[:, :], self.KT_s[hh], writes=[ukt])
            self.dma(vv[:, 0:16, 0:128], self.V_s[hh, 0:2048, :].rearrange("(j p) d -> p j d", p=128), writes=[uvv])
            self.dma(vv[0:64, 16, 0:128], self.V_s[hh, 2048:2112, :], writes=[uvv])
            for i in range(8):
                qp, uqp = QP[i % 2]
                self.cp(qp[0:64, 0, :], QO[0:64, hh, i * 128:(i + 1) * 128], [uQO], [uqp], eng="vector")
                self.cp(qp[64:128, 1, :], QO[64:128, hh, i * 128:(i + 1) * 128], [uQO], [uqp], eng="gpsimd")
                bo = 2 + 2 * (i % 2)
                blocks = []
                for j in range(i + 1):
                    blocks.append((j, 0, i - j))
                    blocks.append((8 + j, 1, i - j))
                for bi_, (kb, kind, rel) in enumerate(blocks):
                    it = (hh, i, kb, kind, rel, bi_ == 0, bi_ == len(blocks) - 1, kt_, ukt, vv, uvv, qp, uqp, bo)
                    ptt = qk_item(it)
                    if prev is not None:
                        pv_item(*prev)
                    prev = (it, ptt)
        pv_item(*prev)

        KP = [A5.alloc([128, 1024], BF16, "KP%d" % i) for i in range(2)]
        VP = [A6.alloc([128, 8, 129], BF16, "VP%d" % i) for i in range(2)]
        VPc = [(sb("VPc0", [128, 1024], BF16), U("VPc0")), A3.alloc([128, 1024], BF16, "VPc1")]
        for v_, uv_ in VP:
            self.memset(v_[:, :, 128:129], 1.0, [uv_])
        KTP = [A4.alloc([128, 8, 128], BF16, "KTP%d" % i) for i in range(2)]
        KTN, uKN = A4.alloc([128, 8, 64], BF16, "KTN"); VN, _ = A4.alloc([64, 8, 129], BF16, "VN")
        self.dma(KTN[:], self.KT_s[:, :, 2048:2112].rearrange("h p t -> p h t"), writes=[uKN])
        self.memset(VN[:, :, 128:129], 1.0, [uKN])
        self.dma(VN[:, :, 0:128], self.V_s[:, 2048:2112, :].rearrange("h t d -> t h d"), writes=[uKN])
        OACC = XS[0][0:36, 0:1032].rearrange("p (a b) -> p a b", a=8); uOA = uXS[0]
        OCBb = XS[0][0:4, 1032:1544].bitcast(BF16)
        OCB = XS[1][0:4, 0:1024].rearrange("p (a b) -> p a b", a=8); uOCB = uXS[1]
        OCT = XS[1][0:4, 1024:2048].rearrange("p (a b) -> p a b", a=8)
        hb = [(0, 3), (3, 6), (6, 8)]
        ck3 = cache_v.rearrange("r (h d) -> r h d", h=8)
        pages = []
        ng = 0
        for s_ in range(16):
            for slot in range(17):
                pages.append({"s": s_, "slot": slot, "g": (ng if slot < 16 else None), "n": len(pages)})
                if slot < 16:
                    ng += 1

        def stA0(pg):
            s_, slot = pg["s"], pg["slot"]
            if slot == 0:
                QS_, uQS = QSb[s_ % 2]
                srcq = QO[:, :, 1024 + 4 * s_:1024 + 4 * s_ + 4]
                self.cp(QS_[0:64, :, 0:4], srcq[0:64], [uQO], [uQS], eng="vector")
                self.cp(QS_[64:128, :, 32:36], srcq[64:128], [uQO], [uQS], eng="gpsimd")
            if slot == 16:
                return
            g = pg["g"]
            kp, ukp = KP[g % 2]
            col = s_ * 16 + slot
            P.op("gpsimd", lambda h, kp=kp, col=col: h.indirect_dma_start(
                out=kp[:, :], out_offset=None, in_=cache_k[:, :],
                in_offset=bass.IndirectOffsetOnAxis(ap=IDX[:, col:col + 1], axis=0)), [cst], [ukp], dma=True)

        def stA(pg):
            s_, slot = pg["s"], pg["slot"]
            if slot == 16:
                return
            g = pg["g"]
            kp, ukp = KP[g % 2]; ktp, uktp = KTP[g % 2]; vpc, uvpc = VPc[g % 2]
            col = s_ * 16 + slot
            P.op("gpsimd", lambda h, vpc=vpc, col=col: h.indirect_dma_start(
                out=vpc[:, :], out_offset=None, in_=cache_v[:, :],
                in_offset=bass.IndirectOffsetOnAxis(ap=IDX[:, col:col + 1], axis=0)), [cst], [uvpc], dma=True)
            pq = 6 + g % 2
            psb = PS[pq][:].bitcast(BF16)
            for hh in range(8):
                self.tr(psb[:, hh * 128:(hh + 1) * 128], kp[:, hh * 128:(hh + 1) * 128], ident_b[:, :], [ukp, uID], [uPS[pq]])
            self.cp(ktp[:].rearrange("p a b -> p (a b)"), psb[:, 0:1024], [uPS[pq]], [uktp], eng=self.evac_eng())

        def stB(pg):
            s_, slot, n = pg["s"], pg["slot"], pg["n"]
            QS_, uQS = QSb[s_ % 2]
            if slot < 16:
                g = pg["g"]
                ktp, uktp = KTP[g % 2]; vpc, uvpc = VPc[g % 2]; vp, uvp = VP[g % 2]
                self.cp(vp[:, :, 0:128], vpc[:, :].rearrange("p (h d) -> p h d", h=8), [uvpc], [uvp], eng=self.evac_eng())
                nk = 128
                kt_of = lambda hh: ktp[:, hh, :]
                rd_k = [uktp]
            else:
                nk = 64
                kt_of = lambda hh: KTN[:, hh, :]
                rd_k = [uKN]
            ps_ = n % 2
            pt, upt = PT[n % 2]
            self.mm(PS[ps_][0:nk, :], KA[:, slot, 0:nk], QA[:, :], True, False, [cst], [uPS[ps_]])
            for hh in range(8):
                self.mm(PS[ps_][0:nk, hh * 64:(hh + 1) * 64], kt_of(hh), QS_[:, hh, :], False, hh == 7,
                        rd_k + [uQS], [uPS[ps_]])
            self.act(pt[0:nk, :], PS[ps_][0:nk, :], AF.Exp, [uPS[ps_]], [upt], scale=SCALE)
            if slot == 16:
                self.tt(pt[0:64, :].rearrange("p (h c) -> p h c", h=8), pt[0:64, :].rearrange("p (h c) -> p h c", h=8),
                        MS[:, s_, :].unsqueeze(1).to_broadcast([64, 8, 64]), M, [upt, cst], [upt], eng="gpsimd")

        def stC(pg):
            s_, slot, n = pg["s"], pg["slot"], pg["n"]
            pt, upt = PT[n % 2]
            if slot < 16:
                vp, uvp = VP[pg["g"] % 2]
                nk, v_of, rd_v = 128, (lambda hh: vp[:, hh, :]), [uvp]
            else:
                nk, v_of, rd_v = 64, (lambda hh: VN[:, hh, :]), [uKN]
            for bi_, (h0, h1) in enumerate(hb):
                pb_ = 2 + bi_
                for hh in range(h0, h1):
                    self.mm(PS[pb_][0:36, (hh - h0) * 129:(hh - h0 + 1) * 129], pt[0:nk, hh * 64:hh * 64 + 36], v_of(hh),
                            True, True, [upt] + rd_v, [uPS[pb_]])
                nh = h1 - h0
                dst = OACC[:, h0:h1, :].rearrange("p a b -> p (a b)")
                if slot == 0:
                    self.cp(dst, PS[pb_][0:36, 0:nh * 129], [uPS[pb_]], [uOA])
                else:
                    self.tt(dst, dst, PS[pb_][0:36, 0:nh * 129], A, [uPS[pb_], uOA], [uOA])
            if slot == 16:
                finish_seq(s_)

        def finish_seq(s_):
            P.op("vector", lambda h: h.reciprocal(out=OACC[:, :, 128:129], in_=OACC[:, :, 128:129]), [uOA], [uOA])
            self.tt(OACC[:, :, 0:128], OACC[:, :, 0:128], OACC[:, :, 128:129].to_broadcast([36, 8, 128]), M, [uOA], [uOA])
            for half in range(2):
                pb_ = 5
                self.mm(PS[pb_][0:4, :], CSEL[:, :], OACC[:, 4 * half:4 * half + 4, 0:128], True, True, [cst, uOA], [uPS[pb_]])
                self.cp(OCB[:, 4 * half:4 * half + 4, :], PS[pb_][0:4, :].rearrange("p (a b) -> p a b", a=4), [uPS[pb_]], [uOCB])
            self.tt(OCT[:], OCB[:], OCB[:], M, [uOCB], [uOCB])
            P.op("vector", lambda h: h.tensor_reduce(out=sc[0:4, 8:16], in_=OCT[:], axis=AX.X, op=A), [uOCB], [usc])
            self.ts(sc[0:4, 8:16], sc[0:4, 8:16], 1.0 / 128, LN_EPS, M, A, [usc], [usc])
            self.act(sc[0:4, 8:16], sc[0:4, 8:16], AF.Sqrt, [usc], [usc])
            P.op("vector", lambda h: h.reciprocal(out=sc[0:4, 8:16], in_=sc[0:4, 8:16]), [usc], [usc])
            self.tt(OCBb[:].rearrange("p (a b) -> p a b", a=8), OCB[:], sc[0:4, 8:16].unsqueeze(2).to_broadcast([4, 8, 128]), M,
                    [uOCB, usc], [uOCB])
            psb = PS[6 + s_ % 2][:].bitcast(BF16)
            for hh in range(8):
                self.tr(psb[:, hh * 4:hh * 4 + 4], OCBb[:, hh * 128:(hh + 1) * 128], ident_b[0:4, 0:4], [uOCB, uID], [uPS[6 + s_ % 2]])
            self.ts(AT[:, :, 1024 + 4 * s_:1024 + 4 * s_ + 4], psb[:, 0:32].rearrange("p (a b) -> p a b", a=8), subg[:, 0:1], None,
                    M, None, [uPS[6 + s_ % 2], cst], [uAT])

        npg = len(pages)
        for it in range(npg + 3):
            if 0 <= it - 3 < npg:
                stC(pages[it - 3])
            if 0 <= it - 2 < npg:
                stB(pages[it - 2])
            if 0 <= it - 1 < npg:
                stA(pages[it - 1])
            if it < npg:
                stA0(pages[it])
        self.gated_up(self.w_up_att, AT, uAT, 6144, first=False)
        self.AR2.reset(); self.AR4.reset(); self.AR1.reset()

    def phase_rest(self):
        sb, P, inp = self.sb, self.P, self.inp
        PS, uPS, XS, uXS = self.PS, self.uPS, self.XS, self.uXS
        ident_f, ident_b, uID = self.ident_f, self.ident_b, self.uID
        M, A, S = ALU.mult, ALU.add, ALU.subtract
        A1, A2, A3, A4, A5, A6 = self.AR1, self.AR2, self.AR3, self.AR4, self.AR5, self.AR6
        w_o = inp("w_o", [D, D]); w_ff1 = inp("w_ff1", [D, 8192]); w_ff2 = inp("w_ff2", [8192, D])
        lnp = inp("lnp", [4, D])
        y_own = self.outp("y_own", [1024, D]); y_s = self.outp("y_s", [64, D])
        X1_s = self.scratch("X1_s", [NOWN, D])
        MIX, uMIX = self.MIX, self.uMIX
        pchunks = [(0, 512), (512, 512), (1024, 64)]
        blocks = [(128 * i, 128, 128 * i, False) for i in range(8)] + [(2048, 64, 1024, True)]
        MO, uMO = A4.alloc([128, 16, NOWN], BF16, "MO")
        for cb in range(4):
            def ev(ms, t0, n, ps, ups, cb=cb):
                self.cp(MO[:, cb * 4 + ms, t0:t0 + n], ps, [ups], [uMO], eng=self.evac_eng())
            self.proj_fm(w_o, 0, 16, cb * 512, 512, lambda kt, t0, n: MIX[:, kt, t0:t0 + n], [uMIX], pchunks, ev)
        A3.reset(); A5.reset(); A6.reset()
        mod2, umod2 = A3.alloc([128, 2, 16, 65], F32, "mod2")
        self.mod, self.umod = mod2, umod2
        self.dma(mod2[:].rearrange("p a b c -> p (a b c)"), self.MOD2_s, writes=[umod2])
        G1, uG1 = A3.alloc([128, D], F32, "G1"); LNG, uLNG = A3.alloc([128, D], F32, "LNG"); LNB, uLNB = A3.alloc([128, D], F32, "LNB")
        self.dma(LNG[:], lnp[0, :].partition_broadcast(128), writes=[uLNG])
        self.dma(LNB[:], lnp[1, :].partition_broadcast(128), writes=[uLNB])
        xn0, uxn0 = A2.alloc([128, D], BF16, "XN0"); xn1, uxn1 = A2.alloc([128, D], BF16, "XN1")
        self.XN = [xn0, xn1]; self.uXN = [uxn0, uxn1]
        H2, uH2 = A1.alloc([128, 16, NOWN], BF16, "H2")
        mv, ust = self.mv, self.ust

        def res_ln(bi, r0, nr, tok0, is_s, src_fm, usrc_fm, src_f32, xsrc, gidx, G, uG, lng, ulng, lnb, ulnb, out_fn):
            xs, ux = XS[0], uXS[0]
            rt, urt = XS[1], uXS[1]
            self.dma(xs[0:nr, :], xsrc, writes=[ux])
            if bi == 0 or is_s:
                self.dma(G[0:nr, :], self.GS[gidx, 128:192, :] if is_s else self.GS[gidx, 0:128, :], writes=[uG])
            if not src_f32:
                for half in range(2):
                    pb = 6 + half
                    psb = PS[pb][:].bitcast(BF16)
                    for j in range(8):
                        kt = half * 8 + j
                        self.tr(psb[0:nr, j * 128:(j + 1) * 128], src_fm(kt)[:, tok0:tok0 + nr], ident_b[:, :], usrc_fm + [uID], [uPS[pb]])
                    self.tt(rt[0:nr, half * 1024:(half + 1) * 1024], psb[0:nr, 0:1024], G[0:nr, half * 1024:(half + 1) * 1024], M,
                            [uPS[pb], uG], [urt])
            else:
                for q4 in range(4):
                    pb = 2 + q4
                    for j in range(4):
                        kt = q4 * 4 + j
                        self.tr(PS[pb][0:nr, j * 128:(j + 1) * 128], src_fm(kt)[:, tok0:tok0 + nr], ident_f[:, :], usrc_fm + [uID], [uPS[pb]])
                    self.tt(rt[0:nr, q4 * 512:(q4 + 1) * 512], PS[pb][0:nr, :], G[0:nr, q4 * 512:(q4 + 1) * 512], M,
                            [uPS[pb], uG], [urt])
            self.stt(xs[0:nr, :], xs[0:nr, :], ALPHA, rt[0:nr, :], M, A, [ux, urt], [ux])
            self.ln_stats(xs, ux, nr, 0)
            self.ts(xs[0:nr, :], xs[0:nr, :], mv[0:nr, 0, 0:1], mv[0:nr, 0, 3:4], S, M, [ux, ust[0]], [ux])
            self.tt(xs[0:nr, :], xs[0:nr, :], lng[0:nr, :], M, [ux, ulng], [ux], eng="gpsimd")
            self.tt(xs[0:nr, :], xs[0:nr, :], lnb[0:nr, :], A, [ux, ulnb], [ux])
            out_fn(xs, ux)

        for bi, (r0, nr, tok0, is_s) in enumerate(blocks):
            def after1(xs, ux, tok0=tok0, nr=nr, is_s=is_s, bi=bi):
                self.dma(X1_s[tok0:tok0 + nr, :], xs[0:nr, :], reads=[ux])
                self.ln_stats(xs, ux, nr, 1)
                xn, uxn = self.XN[bi % 2], self.uXN[bi % 2]
                self.ts(xn[0:nr, :], xs[0:nr, :], mv[0:nr, 1, 0:1], mv[0:nr, 1, 3:4], S, M, [ux, ust[1]], [uxn])
                self.xn_to_fm(xn, uxn, nr, 0, 1, is_s, H2, uH2, tok0)
            res_ln(bi, r0, nr, tok0, is_s, lambda kt: MO[:, kt, :], [uMO], False, self.xa[r0:r0 + nr, :], 0, G1, uG1,
                   LNG, uLNG, LNB, uLNB, after1)
        A2.reset(); A3.reset(); A4.reset()
        HID, uHID = A2.alloc([128, 8, NOWN], BF16, "HID")
        ACCl, uACCl = A4.alloc([128, 8, NOWN], F32, "ACCl")
        ACCh, uACCh = A3.alloc([128, 8, NOWN], F32, "ACCh")
        RT = [A6.alloc([128, 512], F32, "RT%d" % i) for i in range(2)]
        rk = {"n": 0}
        for hc in range(8):
            for cb in range(2):
                def ev1(ms, t0, n, ps, ups, cb=cb):
                    rt_, urt_ = RT[rk["n"] % 2]
                    rk["n"] += 1
                    self.act(rt_[:, 0:n], ps, AF.Relu, [ups], [urt_])
                    self.tt(HID[:, cb * 4 + ms, t0:t0 + n], rt_[:, 0:n], rt_[:, 0:n], M, [urt_], [uHID],
                            eng=("gpsimd" if rk["n"] % 2 else "vector"))
                self.proj_fm(w_ff1, 0, 16, hc * 1024 + cb * 512, 512, lambda kt, t0, n: H2[:, kt, t0:t0 + n], [uH2], pchunks, ev1)
            for cb in range(4):
                def ev2(ms, t0, n, ps, ups, cb=cb, hc=hc):
                    mt = cb * 4 + ms
                    acc, uacc = (ACCl, uACCl) if mt < 8 else (ACCh, uACCh)
                    o = acc[:, mt % 8, t0:t0 + n]
                    if hc == 0:
                        self.cp(o, ps, [ups], [uacc], eng=self.evac_eng())
                    else:
                        self.tt(o, o, ps, A, [ups, uacc], [uacc])
                self.proj_fm(w_ff2, hc * 1024, 8, cb * 512, 512, lambda kt, t0, n: HID[:, kt, t0:t0 + n], [uHID], pchunks, ev2)
        A2.reset(); A5.reset()
        G2, uG2 = A2.alloc([128, D], F32, "G2"); LNG2, uLNG2 = A2.alloc([128, D], F32, "LNG2")
        LNB2, uLNB2 = A5.alloc([128, D], F32, "LNB2")
        self.dma(LNG2[:], lnp[2, :].partition_broadcast(128), writes=[uLNG2])
        self.dma(LNB2[:], lnp[3, :].partition_broadcast(128), writes=[uLNB2])
        uACC = U("accboth")
        for bi, (r0, nr, tok0, is_s) in enumerate(blocks):
            def after2(xs, ux, tok0=tok0, nr=nr, is_s=is_s):
                dst = y_s[0:64, :] if is_s else y_own[tok0:tok0 + nr, :]
                self.dma(dst, xs[0:nr, :], reads=[ux])
            res_ln(bi, r0, nr, tok0, is_s, lambda kt: (ACCl if kt < 8 else ACCh)[:, kt % 8, :], [uACCl, uACCh], True,
                   X1_s[tok0:tok0 + nr, :], 1, G2, uG2, LNG2, uLNG2, LNB2, uLNB2, after2)

    def finish(self):
        self.P.emit(self.st)
        self.st.close()


SLOPES = np.array([2.0 ** (-(hh + 1)) for hh in range(8)], np.float64)


def _bias_tab(h):
    kk = np.arange(128, dtype=np.float64)[:, None, None, None]
    sl = SLOPES[None, :, None, None]
    rel = np.arange(8, dtype=np.float64)[None, None, None, :]
    own = sl * (kk - 256.0 * rel)
    oth = sl * (kk - 256.0 * rel + 128.0 * (1 - 2 * h))
    bt = np.concatenate([np.broadcast_to(own, (128, 8, 1, 8)), np.broadcast_to(oth, (128, 8, 1, 8))], axis=2).copy()
    if h == 0:
        bt[:, :, 1, 0] = -30000.0
    return bt.reshape(128, 128).astype(np.float32)


def _consts():
    tri = (np.arange(128)[None, :] >= np.arange(128)[:, None]).astype(np.float32)
    tri = np.concatenate([tri, tri], axis=1).astype(NPBF)
    ka = np.zeros((4, 17, 128), np.float32)
    for slot in range(16):
        kpos = 128 * slot + np.arange(128)
        ka[0, slot] = (kpos // 256) * 256
        ka[1, slot] = kpos % 256
    ka[0, 16] = 2048.0
    ka[1, 16] = np.arange(128) % 4
    ka[2] = 1.0
    ka[3] = 1.0
    qa = np.zeros((4, 8, 64), np.float32)
    for hh in range(8):
        s8 = 8.0 * SLOPES[hh]
        for c0 in (0, 32):
            for t in range(4):
                qa[:, hh, c0 + t] = [s8, s8, -s8 * 2048.0, -s8 * t]
    ms = np.zeros((64, 16, 64), np.float32)
    for kk in range(64):
        for c0 in (0, 32):
            for t in range(4):
                if kk % 4 <= t:
                    ms[kk, kk // 4, c0 + t] = 1.0
    return tri, ka.reshape(4, 17 * 128).astype(NPBF), qa.reshape(4, 512).astype(NPBF), ms.reshape(64, 1024).astype(NPBF)


TRI_C, KA_C, QA_C, MS_C = _consts()


def _core_maps(inputs, cores, small_pool):
    xp = np.asarray(inputs["x_prompt"])
    xsamp = np.asarray(inputs["x_sample"])
    maps = []
    ident = np.eye(128, dtype=np.float32)
    esel = np.zeros((64, 2, 128), np.float32)
    esel[np.arange(64), 0, np.arange(64)] = 1.0
    esel[np.arange(64), 1, 64 + np.arange(64)] = 1.0
    esel = esel.reshape(64, 256).astype(NPBF)
    for c in cores:
        b, h = c // 2, c % 2
        own = [2 * i + h for i in range(8)]
        oth = [2 * i + 1 - h for i in range(8)]
        xb = xp[b].reshape(16, 128, D)
        xa = np.concatenate([xb[own].reshape(1024, D), xb[oth].reshape(1024, D),
                             xsamp[16 * c:16 * c + 16].reshape(64, D)], axis=0)
        call = np.concatenate([np.asarray(inputs["c_prompt"])[b:b + 1],
                               np.repeat(np.asarray(inputs["c_sample"])[16 * c:16 * c + 16], 4, axis=0)], axis=0)
        m = {
            "xa": np.ascontiguousarray(xa),
            "call": np.ascontiguousarray(call),
            "w_ada": np.asarray(inputs["w_ada"])[0],
            "b_ada": np.asarray(inputs["b_ada"])[0].reshape(96, 128),
            "w_in": np.asarray(inputs["w_in"])[0],
            "w_glu": np.asarray(inputs["w_glu"])[0], "w_up_ssm": np.asarray(inputs["w_up_ssm"])[0],
            "w_up_att": np.asarray(inputs["w_up_att"])[0], "w_o": np.asarray(inputs["w_o"])[0],
            "w_ff1": np.asarray(inputs["w_ff1"])[0], "w_ff2": np.asarray(inputs["w_ff2"])[0],
            "ident_f": ident,
            "ident_b": ident.astype(NPBF),
            "a_re": np.asarray(inputs["ssm_a_re"])[0], "a_im": np.asarray(inputs["ssm_a_im"])[0],
            "log_dt": np.asarray(inputs["ssm_log_dt"])[0].reshape(64, 1),
            "b_re": np.asarray(inputs["ssm_b_re"])[0].reshape(64, 1024), "b_im": np.asarray(inputs["ssm_b_im"])[0].reshape(64, 1024),
            "c_re": np.asarray(inputs["ssm_c_re"])[0].reshape(64, 1024), "c_im": np.asarray(inputs["ssm_c_im"])[0].reshape(64, 1024),
            "dvec": np.asarray(inputs["ssm_d"])[0].reshape(8, 128),
            "hflag": np.full((64, 1), float(h), np.float32),
            "lnp": np.stack([np.asarray(inputs[k])[0] for k in ("ln1_g", "ln1_b", "ln2_g", "ln2_b")]).astype(np.float32),
            "bias_tab": _bias_tab(h), "tri": TRI_C, "ka": KA_C, "qa": QA_C, "msk": MS_C,
            "lamv": np.concatenate([np.asarray(inputs[k])[0] for k in ("lam_q1", "lam_k1", "lam_q2", "lam_k2")]).reshape(1, 256).astype(np.float32),
            "subg": np.asarray(inputs["subln_g"])[0].reshape(128, 1), "iota": np.arange(128, dtype=np.float32).reshape(128, 1),
            "esel": esel,
            "s0T": np.ascontiguousarray(np.stack([np.asarray(inputs["state_ssm_re"])[0, 16 * c:16 * c + 16],
                                                  np.asarray(inputs["state_ssm_im"])[0, 16 * c:16 * c + 16]], 0).transpose(3, 0, 2, 1)),
        }
        pt = np.asarray(inputs["page_table"])[16 * c:16 * c + 16].astype(np.int32)
        if small_pool:
            ck = np.asarray(inputs["cache_k"])[0][pt.reshape(-1)].reshape(256 * 128, 1024)
            cv = np.asarray(inputs["cache_v"])[0][pt.reshape(-1)].reshape(256 * 128, 1024)
            pt = np.arange(256, dtype=np.int32).reshape(16, 16)
        else:
            ck = np.asarray(inputs["cache_k"])[0].reshape(-1, 1024)
            cv = np.asarray(inputs["cache_v"])[0].reshape(-1, 1024)
        m["cache_k"] = ck; m["cache_v"] = cv; m["ptab"] = pt.reshape(1, 256)
        maps.append(m)
    return maps


def _assemble(results, cores, outs):
    for ci, c in enumerate(cores):
        b, h = c // 2, c % 2
        r = results[ci]
        own = [2 * i + h for i in range(8)]
        if "k_own" in r:
            ko = r["k_own"].reshape(8, 128, 8, 128)
            vo = r["v_own"].reshape(8, 128, 8, 128)
            for i, g in enumerate(own):
                outs[2][0, b, g * 128:(g + 1) * 128] = ko[i]
                outs[3][0, b, g * 128:(g + 1) * 128] = vo[i]
            outs[6][0, 16 * c:16 * c + 16] = r["k_s"].reshape(16, 4, 8, 128)
            outs[7][0, 16 * c:16 * c + 16] = r["v_s"].reshape(16, 4, 8, 128)
        if "y_own" in r:
            yo = r["y_own"].reshape(8, 128, D)
            for i, g in enumerate(own):
                outs[0][b, g * 128:(g + 1) * 128] = yo[i]
            outs[1][16 * c:16 * c + 16] = r["y_s"].reshape(16, 4, D)
        if "sfin_o" in r:
            if h == 0:
                outs[4][0, b] = r["sfin_o"][:, 0, :].T
                outs[5][0, b] = r["sfin_o"][:, 1, :].T
            outs[8][0, 16 * c:16 * c + 16] = r["sfs_o"][:, 0].transpose(2, 1, 0)
            outs[9][0, 16 * c:16 * c + 16] = r["sfs_o"][:, 1].transpose(2, 1, 0)
    return outs


def _empty_outs():
    return [np.zeros((4, 2048, 2048), np.float32), np.zeros((128, 4, 2048), np.float32),
            np.zeros((1, 4, 2048, 8, 128), np.float32), np.zeros((1, 4, 2048, 8, 128), np.float32),
            np.zeros((1, 4, 64, 64), np.float32), np.zeros((1, 4, 64, 64), np.float32),
            np.zeros((1, 128, 4, 8, 128), np.float32), np.zeros((1, 128, 4, 8, 128), np.float32),
            np.zeros((1, 128, 64, 64), np.float32), np.zeros((1, 128, 64, 64), np.float32)]


def run(inputs, cores=tuple(range(8)), small_pool=False, stage=99, trace=False):
    bld = Builder(256 * 128 if small_pool else 2560 * 128, stage)
    bld.build()
    maps = _core_maps(inputs, cores, small_pool)
    maps = [{k: v for k, v in m.items() if k in bld.din} for m in maps]
    res = run_bass_kernel_spmd(bld.nc, maps, core_ids=list(range(len(cores))), **({"trace": True} if trace else {}))
    if trace:
        print("EXEC_NS", res.exec_time_ns)
    return _assemble(res.results, cores, _empty_outs())


def kernel(**inputs):
    outs = run(inputs)
    return tuple(outs)
```
